# Optimizing a Trainium2 kernel written in Bass

```python
import math
import jax, jax.numpy as jnp
from jax import lax
import numpy as np

D_MODEL = 2048
BATCH = 4
SEQ = 4096
DEPTH = 2
DEC_BATCH = 2
DEC_SEQ = 4096
PAST_LEN = 128

S5_WIDTH = D_MODEL // 2
S5_GROUP = 16
S5_GROUPS = S5_WIDTH // S5_GROUP
S5_STATE = 64
HEAD_DIM = 128
N_Q_HEADS = (D_MODEL // 2) // HEAD_DIM
N_KV_HEADS = 2
Q_PER_KV = N_Q_HEADS // N_KV_HEADS
ATTN_WIDTH = N_Q_HEADS * HEAD_DIM
KV_WIDTH = N_KV_HEADS * HEAD_DIM
WINDOW = 128
BLOCK = 128
ROPE_DIM = HEAD_DIM // 4
ROPE_THETA = 500000.0
EVEN_IN = S5_WIDTH + ATTN_WIDTH + 2 * KV_WIDTH
EVEN_MIX = S5_WIDTH + ATTN_WIDTH
LRU_WIDTH = D_MODEL
LRU_BLOCKS = 8
LRU_BS = LRU_WIDTH // LRU_BLOCKS
CONV_WIDTH = 4
CONV_LEFT = CONV_WIDTH // 2
LRU_C = 8.0
D_FF = ((8 * D_MODEL + 3 * 256 - 1) // (3 * 256)) * 256
N_EVEN = (DEPTH + 1) // 2
N_ODD = DEPTH // 2
EPS = 1e-6
NEG_INF = -1e30

kernel_name = "hybrid_s5_swa_rglru_encoder"


def rmsnorm(x, g):
    xf = x.astype(jnp.float32)
    y = xf * lax.rsqrt(jnp.mean(xf * xf, axis=-1, keepdims=True) + EPS)
    return (y * g.astype(jnp.float32)).astype(x.dtype)


def rope_partial(x, pos):
    half = ROPE_DIM // 2
    inv = ROPE_THETA ** (-jnp.arange(half, dtype=jnp.float32) / half)
    ang = pos.astype(jnp.float32)[:, None] * inv[None, :]
    cos = jnp.cos(ang)[None, :, None, :]
    sin = jnp.sin(ang)[None, :, None, :]
    xr = x[..., :ROPE_DIM].astype(jnp.float32)
    x1, x2 = xr[..., :half], xr[..., half:]
    rot = jnp.concatenate([x1 * cos - x2 * sin, x2 * cos + x1 * sin], axis=-1)
    return jnp.concatenate([rot.astype(x.dtype), x[..., ROPE_DIM:]], axis=-1)


def window_attention(q, k, v, sink):
    Bsz, L = q.shape[0], q.shape[1]
    nb = L // BLOCK
    qb = q.reshape(Bsz, nb, BLOCK, N_KV_HEADS, Q_PER_KV, HEAD_DIM)
    pad = ((0, 0), (BLOCK, BLOCK), (0, 0), (0, 0))
    kp = jnp.pad(k, pad).reshape(Bsz, nb + 2, BLOCK, N_KV_HEADS, HEAD_DIM)
    vp = jnp.pad(v, pad).reshape(Bsz, nb + 2, BLOCK, N_KV_HEADS, HEAD_DIM)
    kb = jnp.concatenate([kp[:, :-2], kp[:, 1:-1], kp[:, 2:]], axis=2)
    vb = jnp.concatenate([vp[:, :-2], vp[:, 1:-1], vp[:, 2:]], axis=2)
    s = jnp.einsum('bnqhgd,bnkhd->bnhgqk', qb, kb,
                   preferred_element_type=jnp.float32) * (HEAD_DIM ** -0.5)
    qi = jnp.arange(BLOCK)[:, None]
    kj = jnp.arange(3 * BLOCK)[None, :]
    off = kj - BLOCK - qi
    kpos = (jnp.arange(nb)[:, None, None] - 1) * BLOCK + kj[None]
    valid = (jnp.abs(off) <= WINDOW)[None] & (kpos >= 0) & (kpos < L)
    s = jnp.where(valid[None, :, None, None], s, NEG_INF)
    sk = sink.astype(jnp.float32).reshape(1, 1, N_KV_HEADS, Q_PER_KV, 1, 1)
    m = jnp.maximum(jnp.max(s, axis=-1, keepdims=True), sk)
    p = jnp.exp(s - m)
    denom = jnp.sum(p, axis=-1, keepdims=True) + jnp.exp(sk - m)
    o = jnp.einsum('bnhgqk,bnkhd->bnqhgd', (p / denom).astype(v.dtype), vb)
    return o.reshape(Bsz, L, ATTN_WIDTH)


def _complex_combine(e1, e2):
    a1r, a1i, b1r, b1i = e1
    a2r, a2i, b2r, b2i = e2
    return (a1r * a2r - a1i * a2i,
            a1r * a2i + a1i * a2r,
            a2r * b1r - a2i * b1i + b2r,
            a2r * b1i + a2i * b1r + b2i)


def _real_combine(e1, e2):
    a1, b1 = e1
    a2, b2 = e2
    return (a1 * a2, a2 * b1 + b2)


def s5_mixer(u, lam_re, lam_im, log_dt, b_re, b_im, c_re, c_im, d_skip, w_glu, b_glu):
    Bsz, L = u.shape[0], u.shape[1]
    uf = u.astype(jnp.float32).reshape(Bsz, L, S5_GROUPS, S5_GROUP)
    lr = jnp.minimum(lam_re.astype(jnp.float32), -1e-4)
    li = lam_im.astype(jnp.float32)
    dt = jnp.exp(log_dt.astype(jnp.float32))[..., None]
    mag = jnp.exp(lr * dt)
    ab_re = mag * jnp.cos(li * dt)
    ab_im = mag * jnp.sin(li * dt)
    nr, ni = ab_re - 1.0, ab_im
    den = lr * lr + li * li
    f_re = (nr * lr + ni * li) / den
    f_im = (ni * lr - nr * li) / den
    br, bi = b_re.astype(jnp.float32), b_im.astype(jnp.float32)
    bb_re = f_re[..., None] * br - f_im[..., None] * bi
    bb_im = f_re[..., None] * bi + f_im[..., None] * br
    y = d_skip.astype(jnp.float32).reshape(S5_GROUPS, S5_GROUP) * uf
    for z, rev in ((0, False), (1, True)):
        bu_re = jnp.einsum('blgn,gpn->blgp', uf, bb_re[z])
        bu_im = jnp.einsum('blgn,gpn->blgp', uf, bb_im[z])
        a_re = jnp.broadcast_to(ab_re[z], bu_re.shape)
        a_im = jnp.broadcast_to(ab_im[z], bu_re.shape)
        _, _, h_re, h_im = lax.associative_scan(
            _complex_combine, (a_re, a_im, bu_re, bu_im), reverse=rev, axis=1)
        y = y + (jnp.einsum('blgp,gnp->blgn', h_re, c_re[z].astype(jnp.float32))
                 - jnp.einsum('blgp,gnp->blgn', h_im, c_im[z].astype(jnp.float32)))
    g = jax.nn.gelu(y.reshape(Bsz, L, S5_WIDTH))
    out = g * jax.nn.sigmoid(g @ w_glu.astype(jnp.float32) + b_glu.astype(jnp.float32))
    return out.astype(u.dtype)


def rglru_mixer(xb, conv_w, conv_b, wa, ba, wx, bx, lam):
    Bsz, L = xb.shape[0], xb.shape[1]
    xf = xb.astype(jnp.float32)
    xp = jnp.pad(xf, ((0, 0), (CONV_LEFT, CONV_WIDTH - 1 - CONV_LEFT), (0, 0)))
    cw = conv_w.astype(jnp.float32)
    conv = conv_b.astype(jnp.float32) + sum(xp[:, t:t + L] * cw[t] for t in range(CONV_WIDTH))
    cblk = conv.reshape(Bsz, L, LRU_BLOCKS, LRU_BS)
    h_sum = jnp.zeros_like(conv)
    for z, rev in ((0, False), (1, True)):
        r = jax.nn.sigmoid(jnp.einsum('blni,nij->blnj', cblk, wa[z].astype(jnp.float32))
                           .reshape(Bsz, L, LRU_WIDTH) + ba[z].astype(jnp.float32))
        i = jax.nn.sigmoid(jnp.einsum('blni,nij->blnj', cblk, wx[z].astype(jnp.float32))
                           .reshape(Bsz, L, LRU_WIDTH) + bx[z].astype(jnp.float32))
        log_a = -LRU_C * r * jax.nn.softplus(-lam[z].astype(jnp.float32))
        a = jnp.exp(log_a)
        b = jnp.sqrt(-jnp.expm1(2.0 * log_a)) * (i * conv)
        _, h = lax.associative_scan(_real_combine, (a, b), reverse=rev, axis=1)
        h_sum = h_sum + h
    return h_sum.astype(xb.dtype)


def setup_inputs(seed: int = 0) -> dict:
    key = jax.random.key(seed)
    ks = iter(jax.random.split(key, 48))
    f32 = jnp.float32

    def nrm(shape, scale):
        return jax.random.normal(next(ks), shape, f32) * scale

    def gain(shape):
        return 1.0 + nrm(shape, 0.02)

    lam_a0 = jax.random.uniform(next(ks), (N_ODD, 2, LRU_WIDTH), f32, minval=0.9, maxval=0.999)
    lam_s = lam_a0 ** (1.0 / LRU_C)
    return {
        "x_prompt": nrm((BATCH, SEQ, D_MODEL), 1.0),
        "x_sample": nrm((DEC_BATCH, DEC_SEQ, D_MODEL), 1.0),
        "norm_mix": gain((DEPTH, D_MODEL)),
        "norm_ffn": gain((DEPTH, D_MODEL)),
        "ev_w_in": nrm((N_EVEN, D_MODEL, EVEN_IN), D_MODEL ** -0.5),
        "ev_w_out": nrm((N_EVEN, EVEN_MIX, D_MODEL), EVEN_MIX ** -0.5),
        "s5_lam_re": -0.5 * jnp.exp(nrm((N_EVEN, 2, S5_GROUPS, S5_STATE), 0.05)),
        "s5_lam_im": jnp.pi * jnp.arange(S5_STATE, dtype=f32) + nrm((N_EVEN, 2, S5_GROUPS, S5_STATE), 0.01),
        "s5_log_dt": jax.random.uniform(next(ks), (N_EVEN, 2, S5_GROUPS), f32,
                                        minval=math.log(1e-3), maxval=math.log(1e-1)),
        "s5_b_re": nrm((N_EVEN, 2, S5_GROUPS, S5_STATE, S5_GROUP), (2 * S5_GROUP) ** -0.5),
        "s5_b_im": nrm((N_EVEN, 2, S5_GROUPS, S5_STATE, S5_GROUP), (2 * S5_GROUP) ** -0.5),
        "s5_c_re": nrm((N_EVEN, 2, S5_GROUPS, S5_GROUP, S5_STATE), S5_STATE ** -0.5),
        "s5_c_im": nrm((N_EVEN, 2, S5_GROUPS, S5_GROUP, S5_STATE), S5_STATE ** -0.5),
        "s5_d": nrm((N_EVEN, S5_WIDTH), 1.0),
        "s5_w_glu": nrm((N_EVEN, S5_WIDTH, S5_WIDTH), S5_WIDTH ** -0.5),
        "s5_b_glu": nrm((N_EVEN, S5_WIDTH), 0.01),
        "attn_q_norm": gain((N_EVEN, HEAD_DIM)),
        "attn_k_norm": gain((N_EVEN, HEAD_DIM)),
        "attn_sink": nrm((N_EVEN, N_Q_HEADS), 0.5),
        "od_w_in": nrm((N_ODD, D_MODEL, 2 * LRU_WIDTH), D_MODEL ** -0.5),
        "od_w_out": nrm((N_ODD, LRU_WIDTH, D_MODEL), LRU_WIDTH ** -0.5),
        "lru_conv_w": nrm((N_ODD, CONV_WIDTH, LRU_WIDTH), CONV_WIDTH ** -0.5),
        "lru_conv_b": nrm((N_ODD, LRU_WIDTH), 0.01),
        "lru_wa": nrm((N_ODD, 2, LRU_BLOCKS, LRU_BS, LRU_BS), LRU_BS ** -0.5),
        "lru_ba": nrm((N_ODD, 2, LRU_WIDTH), 0.01),
        "lru_wx": nrm((N_ODD, 2, LRU_BLOCKS, LRU_BS, LRU_BS), LRU_BS ** -0.5),
        "lru_bx": nrm((N_ODD, 2, LRU_WIDTH), 0.01),
        "lru_lam": jnp.log(lam_s) - jnp.log1p(-lam_s),
        "ffn_w1": nrm((DEPTH, D_MODEL, D_FF), D_MODEL ** -0.5),
        "ffn_w3": nrm((DEPTH, D_MODEL, D_FF), D_MODEL ** -0.5),
        "ffn_w2": nrm((DEPTH, D_FF, D_MODEL), D_FF ** -0.5),
    }


def reference(x_prompt, x_sample, norm_mix, norm_ffn, ev_w_in, ev_w_out,
              s5_lam_re, s5_lam_im, s5_log_dt, s5_b_re, s5_b_im, s5_c_re, s5_c_im,
              s5_d, s5_w_glu, s5_b_glu, attn_q_norm, attn_k_norm, attn_sink,
              od_w_in, od_w_out, lru_conv_w, lru_conv_b, lru_wa, lru_ba, lru_wx, lru_bx, lru_lam,
              ffn_w1, ffn_w3, ffn_w2):
    q0 = S5_WIDTH
    k0 = q0 + ATTN_WIDTH
    v0 = k0 + KV_WIDTH

    def run(x):
        Bsz, L = x.shape[0], x.shape[1]
        pos = jnp.arange(L)
        for layer in range(DEPTH):
            h = rmsnorm(x, norm_mix[layer])
            if layer % 2 == 0:
                e = layer // 2
                proj = h @ ev_w_in[e]
                u = proj[..., :q0]
                q = proj[..., q0:k0].reshape(Bsz, L, N_Q_HEADS, HEAD_DIM)
                k = proj[..., k0:v0].reshape(Bsz, L, N_KV_HEADS, HEAD_DIM)
                v = proj[..., v0:].reshape(Bsz, L, N_KV_HEADS, HEAD_DIM)
                q = rope_partial(rmsnorm(q, attn_q_norm[e]), pos)
                k = rope_partial(rmsnorm(k, attn_k_norm[e]), pos)
                y_attn = window_attention(q, k, v, attn_sink[e])
                y_s5 = s5_mixer(u, s5_lam_re[e], s5_lam_im[e], s5_log_dt[e], s5_b_re[e], s5_b_im[e],
                                s5_c_re[e], s5_c_im[e], s5_d[e], s5_w_glu[e], s5_b_glu[e])
                mix = jnp.concatenate([y_s5, y_attn], axis=-1)
                x = x + mix @ ev_w_out[e]
            else:
                o = layer // 2
                proj = h @ od_w_in[o]
                gate = proj[..., :LRU_WIDTH]
                xb = proj[..., LRU_WIDTH:]
                y_rec = rglru_mixer(xb, lru_conv_w[o], lru_conv_b[o], lru_wa[o], lru_ba[o],
                                    lru_wx[o], lru_bx[o], lru_lam[o])
                x = x + (y_rec * jax.nn.gelu(gate)) @ od_w_out[o]
            h = rmsnorm(x, norm_ffn[layer])
            x = x + (jax.nn.silu(h @ ffn_w1[layer]) * (h @ ffn_w3[layer])) @ ffn_w2[layer]
        return x

    y_prompt = run(x_prompt)
    y_sample = run(x_sample)
    return (y_prompt, y_sample)
```

```python
import math
from contextlib import ExitStack

import numpy as np
import concourse.bass as bass
import concourse.mybir as mybir
from concourse.bass_utils import run_bass_kernel_spmd

F32 = mybir.dt.float32
BF16 = mybir.dt.bfloat16
AF = mybir.ActivationFunctionType
ALU = mybir.AluOpType

D = 2048
KC = D // 128
S5W = 1024
NG = 64
NQ = 8
NKV = 2
HD = 128
EVEN_IN = 2560
DFF = 5632
FC = DFF // 128
TT = 512
EPS = 1e-6
ROPE_THETA = 500000.0
NSLOT = 40


class Tl:
    __slots__ = ("name", "w", "r")

    def __init__(self, name):
        self.name = name
        self.w = None
        self.r = {}


class Sched:
    ENGS = ("pe", "act", "dve", "pool", "sp")

    def __init__(self, nc, es):
        self.nc = nc
        self.sem = {}
        self.cnt = {}
        for e in ("pe", "act", "dve", "pool"):
            self.sem[e] = es.enter_context(nc.semaphore("s_" + e))
            self.cnt[e] = 0
        self.slots = {}
        self.slot_i = {}
        for q in ("sp", "pool"):
            self.slots[q] = []
            for i in range(NSLOT):
                k = "d_%s%d" % (q, i)
                self.sem[k] = es.enter_context(nc.semaphore(k))
                self.cnt[k] = 0
                self.slots[q].append(k)
            self.slot_i[q] = 0
        self.known = {e: {} for e in self.ENGS}
        self.stream = {e: [] for e in self.ENGS}
        self.pending = {"sp": [], "pool": []}

    def _flush_pending(self, q):
        for ent in self.pending[q]:
            self.stream[q].append(ent[0])
        self.pending[q] = []

    def _emit(self, eng, fn, reads, writes, dma, defer=False):
        waits = {}

        def need(dep):
            if dep is None:
                return
            k, v = dep
            if waits.get(k, 0) < v:
                waits[k] = v

        for t in reads:
            need(t.w)
        for t in writes:
            need(t.w)
            for k, v in t.r.items():
                need((k, v))
        if dma:
            q = eng
            k = self.slots[q][self.slot_i[q] % NSLOT]
            self.slot_i[q] += 1
            if self.cnt[k] > 0:
                need((k, self.cnt[k]))
        for q2 in self.pending:
            if self.pending[q2]:
                for (_, tk, tv) in self.pending[q2]:
                    if waits.get(tk, 0) >= tv:
                        self._flush_pending(q2)
                        break
        kn = self.known[eng]
        wl = []
        for k_, v in waits.items():
            if kn.get(k_, 0) < v:
                if not defer:
                    kn[k_] = v
                wl.append((k_, v))
        if dma:
            self.cnt[k] += 16
            tok = (k, self.cnt[k])
            inc = 16
        else:
            self.cnt[eng] += 1
            tok = (eng, self.cnt[eng])
            inc = 1
        ent = (wl, fn, tok[0], inc)
        if defer:
            self.pending[eng].append((ent, tok[0], tok[1]))
            if len(self.pending[eng]) >= 32:
                self._flush_pending(eng)
        else:
            self.stream[eng].append(ent)
        for t in reads:
            if t.r.get(tok[0], 0) < tok[1]:
                t.r[tok[0]] = tok[1]
        for t in writes:
            t.w = tok
            t.r = {}
        return tok

    def op(self, eng, fn, reads=(), writes=()):
        return self._emit(eng, fn, reads, writes, False)

    def dma(self, q, fn, reads=(), writes=(), defer=False):
        return self._emit(q, fn, reads, writes, True, defer)

    def barrier(self):
        for q in self.pending:
            self._flush_pending(q)
        allv = dict(self.cnt)
        for eng in self.ENGS:
            kn = self.known[eng]
            wl = []
            for k, v in allv.items():
                if v > 0 and kn.get(k, 0) < v:
                    kn[k] = v
                    wl.append((k, v))
            if wl:
                self.stream[eng].append((wl, None, None, 0))

    def flush(self):
        nc = self.nc
        sem = self.sem
        streams = self.stream
        self.stream = {e: [] for e in self.ENGS}

        def replay(e, items):
            for wl, fn, tk, inc in items:
                for k, v in wl:
                    e.wait_ge(sem[k], v)
                if fn is not None:
                    ins = fn(e)
                    ins.then_inc(sem[tk], inc)

        with nc.Block() as block:
            @block.tensor
            def _(e):
                replay(e, streams["pe"])

            @block.scalar
            def _(e):
                replay(e, streams["act"])

            @block.vector
            def _(e):
                replay(e, streams["dve"])

            @block.gpsimd
            def _(e):
                replay(e, streams["pool"])

            @block.sync
            def _(e):
                replay(e, streams["sp"])


def kxm(W, m=128):
    K, M = W.shape
    return np.ascontiguousarray(
        W.reshape(K // 128, 128, M // m, m).transpose(2, 1, 0, 3))


def pvec(v):
    return np.ascontiguousarray(v.reshape(-1, 128).T)


class Prog:
    def __init__(self, L, debug=()):
        self.L = L
        self.NT = L // TT
        self.debug = set(debug)
        self.nc = bass.Bass("TRN2", target_bir_lowering=False)
        self.es = ExitStack()
        self.S = Sched(self.nc, self.es)
        self.dram = {}
        self.wcast = {}
        self._uid = 0

    def din(self, name, shape, dt=F32):
        t = self.nc.dram_tensor(name, list(shape), dt, kind="ExternalInput").ap()
        self.dram[name] = t
        return t

    def dout(self, name, shape, dt=F32):
        t = self.nc.dram_tensor(name, list(shape), dt, kind="ExternalOutput").ap()
        self.dram[name] = t
        return t

    def dscr(self, name, shape, dt):
        kind = "ExternalOutput" if name in self.debug else "Internal"
        t = self.nc.dram_tensor(name, list(shape), dt, kind=kind).ap()
        self.dram[name] = t
        return t

    def sb(self, es, name, shape, dt=F32):
        self._uid += 1
        return es.enter_context(self.nc.sbuf_tensor("%s_%d" % (name, self._uid), list(shape), dt))

    def ps(self, es, name, shape=(128, 512), dt=F32):
        self._uid += 1
        return es.enter_context(self.nc.psum_tensor("%s_%d" % (name, self._uid), list(shape), dt))


def _load(P, q, dst_ap, src_ap, tl, reads=()):
    return P.S.dma(q, lambda e: e.dma_start(out=dst_ap, in_=src_ap), reads=reads, writes=(tl,))


def _store(P, q, dst_ap, src_ap, tl, dtl):
    return P.S.dma(q, lambda e: e.dma_start(out=dst_ap, in_=src_ap), reads=(tl,), writes=(dtl,))


def emit_rmsnorm(P, C, xt, xt_tl, gain_ap, hT, hT_tl, sq, sq_tl, ps_stat, ps_stat_tl, rs, rs_tl):
    S = P.S
    S.op("act", lambda e: e.activation(out=sq[:, :, :], in_=xt[:, :, :], func=AF.Square),
         reads=(xt_tl,), writes=(sq_tl,))

    def mm(e):
        ins = None
        for kc in range(KC):
            ins = e.matmul(ps_stat[:, :], C["ones"][:, :], sq[:, kc, :], start=(kc == 0), stop=(kc == KC - 1))
        return ins
    S.op("pe", mm, reads=(sq_tl, C["ones_tl"]), writes=(ps_stat_tl,))
    S.op("act", lambda e: e.activation(out=rs[:, :], in_=ps_stat[:, :], func=AF.Ln, scale=1.0 / D,
                                       bias=C["eps"][:, 0:1]),
         reads=(ps_stat_tl, C["ones_tl"]), writes=(rs_tl,))
    S.op("act", lambda e: e.activation(out=rs[:, :], in_=rs[:, :], func=AF.Exp, scale=-0.5),
         reads=(rs_tl,), writes=(rs_tl,))
    for kc in range(KC):
        S.op("dve", lambda e, kc=kc: e.scalar_tensor_tensor(
            out=hT[:, kc, :], in0=xt[:, kc, :], scalar=gain_ap[:, kc:kc + 1], in1=rs[:, :],
            op0=ALU.mult, op1=ALU.mult),
            reads=(xt_tl, rs_tl, C["ones_tl"]), writes=(hT_tl[kc],))


def precast(P, specs):
    S = P.S
    for name, shape in specs:
        src = P.din(name, shape)
        dst = P.dscr(name + "_b", shape, BF16)
        tls = []
        if len(shape) == 4:
            for oc in range(shape[0]):
                tl = Tl(name + "_b")
                S.dma("pool", lambda e, oc=oc, src=src, dst=dst: e.dma_start(out=dst[oc], in_=src[oc]), writes=(tl,))
                tls.append(tl)
        else:
            h = shape[1] // 2
            for (k0, k1) in ((0, h), (h, shape[1])):
                tl = Tl(name + "_b")
                S.dma("pool", lambda e, k0=k0, k1=k1, src=src, dst=dst: e.dma_start(out=dst[:, k0:k1, :], in_=src[:, k0:k1, :]),
                      writes=(tl,))
                tls.append(tl)
        P.wcast[name] = (dst, tls)


def load_consts(P):
    es = P.es
    S = P.S
    C = {}
    tl = Tl("consts")
    C["ones_tl"] = tl

    def ld(name, shape, dt=F32):
        src = P.din(name, shape, dt)
        t = P.sb(es, "c_" + name, shape, dt)
        nd = len(shape)
        if nd == 2:
            S.dma("sp", lambda e: e.dma_start(out=t[:, :], in_=src), writes=(tl,))
        elif nd == 3:
            S.dma("sp", lambda e: e.dma_start(out=t[:, :, :], in_=src), writes=(tl,))
        else:
            S.dma("sp", lambda e: e.dma_start(out=t[:, :, :, :], in_=src), writes=(tl,))
        C[name] = t
        return t

    ld("ones", [128, 128])
    ld("eps", [128, 1])
    ld("gmix", [128, 2, KC])
    ld("gffn", [128, 2, KC])
    ld("ident", [128, 128])
    onesb = P.sb(es, "c_onesb", [128, 128], BF16)
    S.op("act", lambda e: e.activation(out=onesb[:, :], in_=C["ones"][:, :], func=AF.Copy),
         reads=(tl,), writes=(tl,))
    C["onesb"] = onesb
    identb = P.sb(es, "c_identb", [128, 128], BF16)
    S.op("act", lambda e: e.activation(out=identb[:, :], in_=C["ident"][:, :], func=AF.Copy),
         reads=(tl,), writes=(tl,))
    C["identb"] = identb
    return C


def phase_l0a(P, C, x_src):
    L, NT, S = P.L, P.NT, P.S
    NCH = L // 8
    precast(P, [("w_in0", [18, 128, KC, 128]), ("w_v0", [128, KC, 256])])
    w_in, w_in_tl = P.wcast["w_in0"]
    w_v, w_v_tl = P.wcast["w_v0"]
    ropeC = P.din("ropeC", [128, L])
    ropeS = P.din("ropeS", [128, L])
    ropeP = P.din("ropeP", [128, 128])
    gqk = P.din("gqk", [128, 2])
    U_d = P.dscr("U_d", [8, S5W, NCH], BF16)
    q_d = P.dscr("q_d", [NQ + NKV, 128, L], BF16)
    v_d = P.dscr("v_d", [L, 256], BF16)
    junk = Tl("dram")
    with ExitStack() as es:
        xt = P.sb(es, "xt", [128, KC, TT]); xt_tl = Tl("xt")
        sq = P.sb(es, "sq", [128, KC, TT]); sq_tl = Tl("sq")
        hT2 = [P.sb(es, "hT%d" % i, [128, KC, TT], BF16) for i in range(2)]
        hT2_tl = [[Tl("hT%d" % k) for k in range(KC)] for i in range(2)]
        rs = P.sb(es, "rs", [128, TT]); rs_tl = Tl("rs")
        tC = P.sb(es, "ropeC", [128, L]); tS = P.sb(es, "ropeS", [128, L]); tP = P.sb(es, "ropeP", [128, 128])
        tg = P.sb(es, "gqk", [128, 2]); cst_tl = Tl("l0a_consts")
        wv = P.sb(es, "wv", [128, KC, 256], BF16); wv_tl = Tl("wv")
        NWB = 4
        NPO = 4
        NQS = 3
        wb = [P.sb(es, "wb%d" % i, [128, KC, 128], BF16) for i in range(NWB)]
        wb_tl = [Tl("wb%d" % i) for i in range(NWB)]
        ude = [P.sb(es, "ude%d" % i, [128, 8, TT // 8], BF16) for i in range(2)]
        ude_tl = [Tl("ude%d" % i) for i in range(2)]
        sqq_ = [P.sb(es, "sqq%d" % i, [128, TT]) for i in range(NQS)]; sqq_tl_ = [Tl("sqq") for i in range(NQS)]
        rq_ = [P.sb(es, "rq%d" % i, [128, TT]) for i in range(NQS)]; rq_tl_ = [Tl("rq") for i in range(NQS)]
        qn_ = [P.sb(es, "qn%d" % i, [128, TT]) for i in range(NQS)]; qn_tl_ = [Tl("qn") for i in range(NQS)]
        t1_ = [P.sb(es, "t1%d" % i, [128, TT]) for i in range(NQS)]; t1_tl_ = [Tl("t1") for i in range(NQS)]
        t2_ = [P.sb(es, "t2%d" % i, [128, TT]) for i in range(NQS)]; t2_tl_ = [Tl("t2") for i in range(NQS)]
        qo = [P.sb(es, "qo%d" % i, [128, TT], BF16) for i in range(2)]
        qo_tl = [Tl("qo%d" % i) for i in range(2)]
        vo = [P.sb(es, "vo%d" % i, [128, 256], BF16) for i in range(2)]
        vo_tl = [Tl("vo%d" % i) for i in range(2)]
        ps_stat = P.ps(es, "ps_stat"); ps_stat_tl = Tl("ps_stat")
        ps_o = [P.ps(es, "ps_o%d" % i) for i in range(NPO)]; ps_o_tl = [Tl("ps_o%d" % i) for i in range(NPO)]
        ps_s = P.ps(es, "ps_s"); ps_s_tl = Tl("ps_s")
        ps_r = P.ps(es, "ps_r"); ps_r_tl = Tl("ps_r")
        ps_v = P.ps(es, "ps_v"); ps_v_tl = Tl("ps_v")

        for (dst, src) in ((tC, ropeC), (tS, ropeS), (tP, ropeP), (tg, gqk)):
            S.dma("sp", lambda e, dst=dst, src=src: e.dma_start(out=dst[:, :], in_=src), writes=(cst_tl,))
        S.dma("pool", lambda e: e.dma_start(out=wv[:, :, :], in_=w_v), reads=w_v_tl, writes=(wv_tl,))

        xview = x_src.rearrange("(kc p) t -> p kc t", p=128)
        wi = 0

        def norm(tt_):
            t0_ = tt_ * TT
            S.dma("sp", lambda e, t0_=t0_: e.dma_start(out=xt[:, :, :], in_=xview[:, :, t0_:t0_ + TT]),
                  writes=(xt_tl,))
            emit_rmsnorm(P, C, xt, xt_tl, C["gmix"][:, 0, :], hT2[tt_ % 2], hT2_tl[tt_ % 2], sq, sq_tl,
                         ps_stat, ps_stat_tl, rs, rs_tl)

        norm(0)
        pendB = []
        pendC = []
        for tt in range(NT):
            t0 = tt * TT
            hT, hT_tl = hT2[tt % 2], hT2_tl[tt % 2]
            for oc in range(18):
                if oc == 6 and tt + 1 < NT:
                    norm(tt + 1)
                wbi = wi % NWB
                b = wi % NPO
                wi += 1
                S.dma("pool", lambda e, wbi=wbi, oc=oc: e.dma_start(out=wb[wbi][:, :, :], in_=w_in[oc]),
                      reads=(w_in_tl[oc],), writes=(wb_tl[wbi],))

                def mm(e, b=b, wbi=wbi, hT=hT):
                    ins = None
                    for kc in range(KC):
                        ins = e.matmul(ps_o[b][:, :], wb[wbi][:, kc, :], hT[:, kc, :],
                                       start=(kc == 0), stop=(kc == KC - 1))
                    return ins
                S.op("pe", mm, reads=[wb_tl[wbi]] + hT_tl, writes=(ps_o_tl[b],))
                if oc < 8:
                    ub = oc % 2
                    S.op("act", lambda e, b=b, ub=ub: e.activation(
                        out=ude[ub][:, :, :], in_=ps_o[b][:, :].rearrange("p (c j) -> p j c", j=8),
                        func=AF.Copy), reads=(ps_o_tl[b],), writes=(ude_tl[ub],))
                    c0 = tt * (TT // 8)
                    for j0_ in (0, 4):
                        dst = U_d[j0_:j0_ + 4, oc * 128:(oc + 1) * 128, c0:c0 + TT // 8].rearrange("j p c -> p j c")
                        S.dma("sp", lambda e, dst=dst, ub=ub, j0_=j0_: e.dma_start(out=dst, in_=ude[ub][:, j0_:j0_ + 4, :]),
                              reads=(ude_tl[ub],), writes=(junk,), defer=True)
                else:
                    hq = oc - 8
                    gi = 0 if hq < 8 else 1
                    ob = hq % 2
                    qs = hq % NQS
                    sqq, sqq_tl = sqq_[qs], sqq_tl_[qs]
                    rq, rq_tl = rq_[qs], rq_tl_[qs]
                    qn, qn_tl = qn_[qs], qn_tl_[qs]
                    t1, t1_tl = t1_[qs], t1_tl_[qs]
                    t2, t2_tl = t2_[qs], t2_tl_[qs]
                    S.op("act", lambda e, b=b, sqq=sqq: e.activation(out=sqq[:, :], in_=ps_o[b][:, :], func=AF.Square),
                         reads=(ps_o_tl[b],), writes=(sqq_tl,))

                    def stageB(b=b, gi=gi, sqq=sqq, sqq_tl=sqq_tl, rq=rq, rq_tl=rq_tl, qn=qn, qn_tl=qn_tl):
                        S.op("pe", lambda e: e.matmul(ps_s[:, :], C["ones"][:, :], sqq[:, :], start=True, stop=True),
                             reads=(sqq_tl, C["ones_tl"]), writes=(ps_s_tl,))
                        S.op("act", lambda e: e.activation(out=rq[:, :], in_=ps_s[:, :], func=AF.Ln,
                                                           scale=1.0 / HD, bias=C["eps"][:, 0:1]),
                             reads=(ps_s_tl, C["ones_tl"]), writes=(rq_tl,))
                        S.op("act", lambda e: e.activation(out=rq[:, :], in_=rq[:, :], func=AF.Exp, scale=-0.5),
                             reads=(rq_tl,), writes=(rq_tl,))
                        S.op("dve", lambda e: e.scalar_tensor_tensor(
                            out=qn[:, :], in0=ps_o[b][:, :], scalar=tg[:, gi:gi + 1], in1=rq[:, :],
                            op0=ALU.mult, op1=ALU.mult),
                            reads=(ps_o_tl[b], rq_tl, cst_tl), writes=(qn_tl,))

                    def stageC(hq=hq, ob=ob, t0=t0, qn=qn, qn_tl=qn_tl, t1=t1, t1_tl=t1_tl, t2=t2, t2_tl=t2_tl):
                        S.op("pe", lambda e: e.matmul(ps_r[:, :], tP[:, :], qn[:, :], start=True, stop=True),
                             reads=(qn_tl, cst_tl), writes=(ps_r_tl,))
                        S.op("dve", lambda e: e.tensor_tensor(out=t1[:, :], in0=qn[:, :], in1=tC[:, t0:t0 + TT], op=ALU.mult),
                             reads=(qn_tl, cst_tl), writes=(t1_tl,))
                        S.op("dve", lambda e: e.tensor_tensor(out=t2[:, :], in0=ps_r[:, :], in1=tS[:, t0:t0 + TT], op=ALU.mult),
                             reads=(ps_r_tl, cst_tl), writes=(t2_tl,))
                        S.op("dve", lambda e: e.tensor_tensor(out=qo[ob][:, :], in0=t1[:, :], in1=t2[:, :], op=ALU.add),
                             reads=(t1_tl, t2_tl), writes=(qo_tl[ob],))
                        S.dma("sp", lambda e: e.dma_start(out=q_d[hq][:, t0:t0 + TT], in_=qo[ob][:, :]),
                              reads=(qo_tl[ob],), writes=(junk,), defer=True)
                    if len(pendC) > 0:
                        pendC.pop(0)()
                    if len(pendB) > 0:
                        fB, fC = pendB.pop(0)
                        fB()
                        pendC.append(fC)
                    pendB.append((stageB, stageC))
            while pendB or pendC:
                if pendC:
                    pendC.pop(0)()
                if pendB:
                    fB, fC = pendB.pop(0)
                    fB()
                    pendC.append(fC)
            for tb in range(TT // 128):
                vb = tb % 2

                def mmv(e, tb=tb, hT=hT):
                    ins = None
                    for kc in range(KC):
                        ins = e.matmul(ps_v[:, 0:256], hT[:, kc, tb * 128:(tb + 1) * 128], wv[:, kc, :],
                                       start=(kc == 0), stop=(kc == KC - 1))
                    return ins
                S.op("pe", mmv, reads=[wv_tl] + hT_tl, writes=(ps_v_tl,))
                S.op("act", lambda e, vb=vb: e.activation(out=vo[vb][:, :], in_=ps_v[:, 0:256], func=AF.Copy),
                     reads=(ps_v_tl,), writes=(vo_tl[vb],))
                r0 = t0 + tb * 128
                S.dma("sp", lambda e, vb=vb, r0=r0: e.dma_start(out=v_d[r0:r0 + 128, :], in_=vo[vb][:, :]),
                      reads=(vo_tl[vb],), writes=(junk,), defer=True)
        S.barrier()
        S.flush()
    return U_d, q_d, v_d


def host_consts(L):
    c = {}
    c["ones"] = np.ones((128, 128), np.float32)
    c["eps"] = np.full((128, 1), EPS, np.float32)
    c["ident"] = np.eye(128, dtype=np.float32)
    half = 16
    inv = (np.float32(ROPE_THETA) ** (-np.arange(half, dtype=np.float32) / np.float32(half))).astype(np.float32)
    ang = (np.arange(L, dtype=np.float32)[:, None] * inv[None, :]).astype(np.float32)
    cs = np.cos(ang).astype(np.float32).T
    sn = np.sin(ang).astype(np.float32).T
    rc = np.ones((128, L), np.float32)
    rsn = np.zeros((128, L), np.float32)
    rc[0:16] = cs; rc[16:32] = cs
    rsn[0:16] = sn; rsn[16:32] = sn
    c["ropeC"] = rc
    c["ropeS"] = rsn
    pt = np.zeros((128, 128), np.float32)
    for m in range(16):
        pt[m + 16, m] = -1.0
        pt[m, m + 16] = 1.0
    c["ropeP"] = pt
    return c


def host_weights(inp):
    w = {}
    w["gmix"] = np.ascontiguousarray(inp["norm_mix"].reshape(2, KC, 128).transpose(2, 0, 1))
    w["gffn"] = np.ascontiguousarray(inp["norm_ffn"].reshape(2, KC, 128).transpose(2, 0, 1))
    Win = inp["ev_w_in"][0]
    w["w_in0"] = kxm(Win[:, :2304])
    w["w_v0"] = np.ascontiguousarray(Win[:, 2304:2560].reshape(KC, 128, 256).transpose(1, 0, 2))
    w["gqk"] = np.ascontiguousarray(np.stack([inp["attn_q_norm"][0], inp["attn_k_norm"][0]], axis=1))
    return w


EXPS = [float(e) for e in range(-8, 9)] + [float(16 * 2 ** m) for m in range(8)]
TWO_PI = 2.0 * math.pi
CW1 = 6.28125
CW2 = TWO_PI - CW1
PI_F = 3.1415927410125732


def s5_host(inp):
    w = {}
    lre = inp["s5_lam_re"][0]; lim = inp["s5_lam_im"][0]; ldt = inp["s5_log_dt"][0]

    def st(a):
        return np.ascontiguousarray(a.reshape(2, 32, 2, 64).transpose(2, 3, 0, 1).reshape(128, 2, 32))
    w["s5_lre"] = st(lre)
    w["s5_lim"] = st(lim)
    w["s5_ldt"] = st(np.broadcast_to(ldt[:, :, None], (2, 64, 64)))

    def sb(a):
        return np.ascontiguousarray(a.reshape(2, 32, 2, 64, 16).transpose(2, 3, 0, 1, 4).reshape(128, 2, 32, 16))

    def sc(a):
        return np.ascontiguousarray(a.reshape(2, 32, 2, 16, 64).transpose(2, 4, 0, 1, 3).reshape(128, 2, 32, 16))
    w["s5_bre"] = sb(inp["s5_b_re"][0]); w["s5_bim"] = sb(inp["s5_b_im"][0])
    w["s5_cre"] = sc(inp["s5_c_re"][0]); w["s5_cim"] = sc(inp["s5_c_im"][0])
    d = inp["s5_d"][0].reshape(64, 16)
    w["s5_drep"] = np.ascontiguousarray(np.broadcast_to(d.T[None, :, :], (8, 16, 64)).reshape(128, 64))
    w["s5_etab"] = np.ascontiguousarray(np.broadcast_to(np.array(EXPS, np.float32)[None, :], (128, 25)))
    jj = np.repeat(np.arange(8), 16)
    m2 = np.zeros((128, 2, 128), np.float32)
    m2[:, 0, :] = (jj[None, :] >= jj[:, None])
    m2[:, 1, :] = (jj[:, None] >= jj[None, :])
    w["s5_m2"] = m2
    return w


def phase_s5(P, C, U_d):
    L, S = P.L, P.S
    NCH = L // 8
    NST = int(math.log2(NCH))
    junk = Tl("dram")
    Y_d = P.dscr("Y_d", [8, S5W, NCH], F32)
    src = {}
    for nm, shp in (("s5_lre", [128, 2, 32]), ("s5_lim", [128, 2, 32]), ("s5_ldt", [128, 2, 32]),
                    ("s5_bre", [128, 2, 32, 16]), ("s5_bim", [128, 2, 32, 16]),
                    ("s5_cre", [128, 2, 32, 16]), ("s5_cim", [128, 2, 32, 16]),
                    ("s5_drep", [128, 64]), ("s5_etab", [128, 25]), ("s5_m2", [128, 2, 128])):
        src[nm] = P.din(nm, shp)
    precast(P, [("w_glu", [8, 128, 8, 128]), ("w_out0", [KC, 128, KC, 128]),
                ("w_in1", [32, 128, KC, 128]), ("w_out1", [KC, 128, KC, 128])])
    with ExitStack() as es:
        MB = P.sb(es, "MB", [128, 2, 32, 2, 128], BF16); MB_tl = Tl("MB")
        MCb = P.sb(es, "MCb", [128, 2, 32, 2, 128], BF16); MCb_tl = Tl("MCb")
        Toep = P.sb(es, "Toep", [128, NG, 128], BF16); Toep_tl = Tl("Toep")
        kar = P.sb(es, "kar", [128, 9, 64]); kai = P.sb(es, "kai", [128, 9, 64]); kan = P.sb(es, "kan", [128, 9, 64])
        ks_tl = Tl("kstab")
        with ExitStack() as es2:
            pt = Tl("prep")
            t = {}
            for nm, shp in (("s5_lre", [128, 64]), ("s5_lim", [128, 64]), ("s5_ldt", [128, 64]),
                            ("s5_drep", [128, 64]), ("s5_etab", [128, 25])):
                t[nm] = P.sb(es2, nm, shp)
                sa = src[nm] if len(src[nm].shape) == 2 else src[nm].rearrange("p z q -> p (z q)")
                S.dma("sp", lambda e, d=t[nm], s=sa: e.dma_start(out=d[:, :], in_=s), writes=(pt,))
            for nm in ("s5_bre", "s5_bim", "s5_cre", "s5_cim"):
                t[nm] = P.sb(es2, nm, [128, 2, 32, 16])
                S.dma("sp", lambda e, d=t[nm], s=src[nm]: e.dma_start(out=d[:, :, :, :], in_=s), writes=(pt,))
            m2 = P.sb(es2, "m2", [128, 2, 128])
            S.dma("sp", lambda e: e.dma_start(out=m2[:, :, :], in_=src["s5_m2"]), writes=(pt,))
            NE = 25

            def T2(nm, n=64):
                return P.sb(es2, nm, [128, n])

            def T3(nm, dt=F32):
                return P.sb(es2, nm, [128, NE, 64], dt)
            dt_ = T2("dt"); lr = T2("lr"); xm = T2("xm"); an = T2("an")
            XE = T3("XE"); AE = T3("AE"); mag = XE; kf = T3("kf"); ki = T3("ki", mybir.dt.int32)
            rr = T3("rr"); rc = T3("rc"); msk = kf; wre = rc; wim = rr
            D1 = lambda f, r=(pt,), w=(pt,): S.op("dve", f, reads=r, writes=w)
            A1 = lambda f, r=(pt,), w=(pt,): S.op("act", f, reads=r, writes=w)
            A1(lambda e: e.activation(out=dt_[:, :], in_=t["s5_ldt"][:, :], func=AF.Exp))
            D1(lambda e: e.tensor_scalar(out=lr[:, :], in0=t["s5_lre"][:, :], scalar1=-1e-4, scalar2=None, op0=ALU.min))
            D1(lambda e: e.tensor_tensor(out=xm[:, :], in0=lr[:, :], in1=dt_[:, :], op=ALU.mult))
            D1(lambda e: e.tensor_tensor(out=an[:, :], in0=t["s5_lim"][:, :], in1=dt_[:, :], op=ALU.mult))
            eb = t["s5_etab"][:, :].unsqueeze(2).broadcast_to([128, NE, 64])
            D1(lambda e: e.tensor_tensor(out=XE[:, :, :], in0=xm[:, :].unsqueeze(1).broadcast_to([128, NE, 64]),
                                         in1=eb, op=ALU.mult))
            D1(lambda e: e.tensor_tensor(out=AE[:, :, :], in0=an[:, :].unsqueeze(1).broadcast_to([128, NE, 64]),
                                         in1=eb, op=ALU.mult))
            A1(lambda e: e.activation(out=mag[:, :, :], in_=XE[:, :, :], func=AF.Exp))
            D1(lambda e: e.tensor_scalar(out=kf[:, :, :], in0=AE[:, :, :], scalar1=1.0 / TWO_PI, scalar2=None, op0=ALU.mult))
            D1(lambda e: e.tensor_copy(out=ki[:, :, :], in_=kf[:, :, :]))
            D1(lambda e: e.tensor_copy(out=kf[:, :, :], in_=ki[:, :, :]))
            D1(lambda e: e.scalar_tensor_tensor(out=rr[:, :, :], in0=kf[:, :, :], scalar=-CW1, in1=AE[:, :, :],
                                                op0=ALU.mult, op1=ALU.add))
            D1(lambda e: e.scalar_tensor_tensor(out=rr[:, :, :], in0=kf[:, :, :], scalar=-CW2, in1=rr[:, :, :],
                                                op0=ALU.mult, op1=ALU.add))

            def wrap(x):
                D1(lambda e: e.tensor_scalar(out=msk[:, :, :], in0=x[:, :, :], scalar1=PI_F, scalar2=None, op0=ALU.is_gt))
                D1(lambda e: e.scalar_tensor_tensor(out=x[:, :, :], in0=msk[:, :, :], scalar=-TWO_PI, in1=x[:, :, :],
                                                    op0=ALU.mult, op1=ALU.add))
                D1(lambda e: e.tensor_scalar(out=msk[:, :, :], in0=x[:, :, :], scalar1=-PI_F, scalar2=None, op0=ALU.is_lt))
                D1(lambda e: e.scalar_tensor_tensor(out=x[:, :, :], in0=msk[:, :, :], scalar=TWO_PI, in1=x[:, :, :],
                                                    op0=ALU.mult, op1=ALU.add))
            wrap(rr)
            D1(lambda e: e.tensor_scalar(out=rc[:, :, :], in0=rr[:, :, :], scalar1=math.pi / 2, scalar2=None, op0=ALU.add))
            wrap(rc)
            A1(lambda e: e.activation(out=rr[:, :, :], in_=rr[:, :, :], func=AF.Sin))
            A1(lambda e: e.activation(out=rc[:, :, :], in_=rc[:, :, :], func=AF.Sin))
            D1(lambda e: e.tensor_tensor(out=wre[:, :, :], in0=mag[:, :, :], in1=rc[:, :, :], op=ALU.mult))
            D1(lambda e: e.tensor_tensor(out=wim[:, :, :], in0=mag[:, :, :], in1=rr[:, :, :], op=ALU.mult))
            D1(lambda e: e.tensor_copy(out=kar[:, :, :], in_=wre[:, 16:25, :]), w=(pt, ks_tl))
            D1(lambda e: e.tensor_copy(out=kai[:, :, :], in_=wim[:, 16:25, :]), w=(pt, ks_tl))
            D1(lambda e: e.tensor_scalar(out=kan[:, :, :], in0=wim[:, 16:25, :], scalar1=-1.0, scalar2=None, op0=ALU.mult),
               w=(pt, ks_tl))
            nr = T2("nr"); den = T2("den"); fa = T2("fa"); fb = T2("fb"); fre = T2("fre"); fim = T2("fim")
            li = t["s5_lim"]
            D1(lambda e: e.tensor_scalar(out=nr[:, :], in0=wre[:, 9, :], scalar1=-1.0, scalar2=None, op0=ALU.add))
            D1(lambda e: e.tensor_tensor(out=den[:, :], in0=lr[:, :], in1=lr[:, :], op=ALU.mult))
            D1(lambda e: e.tensor_tensor(out=fa[:, :], in0=li[:, :], in1=li[:, :], op=ALU.mult))
            D1(lambda e: e.tensor_tensor(out=den[:, :], in0=den[:, :], in1=fa[:, :], op=ALU.add))
            D1(lambda e: e.reciprocal(out=den[:, :], in_=den[:, :]))
            D1(lambda e: e.tensor_tensor(out=fa[:, :], in0=nr[:, :], in1=lr[:, :], op=ALU.mult))
            D1(lambda e: e.tensor_tensor(out=fb[:, :], in0=wim[:, 9, :], in1=li[:, :], op=ALU.mult))
            D1(lambda e: e.tensor_tensor(out=fa[:, :], in0=fa[:, :], in1=fb[:, :], op=ALU.add))
            D1(lambda e: e.tensor_tensor(out=fre[:, :], in0=fa[:, :], in1=den[:, :], op=ALU.mult))
            D1(lambda e: e.tensor_tensor(out=fa[:, :], in0=wim[:, 9, :], in1=lr[:, :], op=ALU.mult))
            D1(lambda e: e.tensor_tensor(out=fb[:, :], in0=nr[:, :], in1=li[:, :], op=ALU.mult))
            D1(lambda e: e.tensor_tensor(out=fa[:, :], in0=fa[:, :], in1=fb[:, :], op=ALU.subtract))
            D1(lambda e: e.tensor_tensor(out=fim[:, :], in0=fa[:, :], in1=den[:, :], op=ALU.mult))
            wfre = P.sb(es2, "wfre", [128, 16, 64]); wfim = P.sb(es2, "wfim", [128, 16, 64])
            tA = AE[:, 0:16, :]; tB = kf[:, 0:16, :]
            fre_b = fre[:, :].unsqueeze(1).broadcast_to([128, 16, 64])
            fim_b = fim[:, :].unsqueeze(1).broadcast_to([128, 16, 64])
            D1(lambda e: e.tensor_tensor(out=tA[:, :, :], in0=wre[:, 0:16, :], in1=fre_b, op=ALU.mult))
            D1(lambda e: e.tensor_tensor(out=tB[:, :, :], in0=wim[:, 0:16, :], in1=fim_b, op=ALU.mult))
            D1(lambda e: e.tensor_tensor(out=wfre[:, :, :], in0=tA[:, :, :], in1=tB[:, :, :], op=ALU.subtract))
            D1(lambda e: e.tensor_tensor(out=tA[:, :, :], in0=wre[:, 0:16, :], in1=fim_b, op=ALU.mult))
            D1(lambda e: e.tensor_tensor(out=tB[:, :, :], in0=wim[:, 0:16, :], in1=fre_b, op=ALU.mult))
            D1(lambda e: e.tensor_tensor(out=wfim[:, :, :], in0=tA[:, :, :], in1=tB[:, :, :], op=ALU.add))

            PCH = 4
            m1 = P.sb(es2, "m1", [128, PCH, 8, 16]); m2t = P.sb(es2, "m2t", [128, PCH, 8, 16])
            m_tl = [Tl("m1"), Tl("m2t")]
            MBt = P.sb(es2, "MBt", [128, 2, PCH, 2, 128], BF16); MBt_tl = Tl("MBt")
            MBn = P.sb(es2, "MBn", [128, 2, PCH, 2, 128]); MBn_tl = Tl("MBn")
            MC = P.sb(es2, "MC", [128, 2, PCH, 2, 128]); MC_tl = Tl("MC")
            tt2 = P.sb(es2, "tt2", [128, 2, 128]); tt2_tl = Tl("tt2")
            tt3 = P.sb(es2, "tt3", [128, 128]); tt3_tl = Tl("tt3")
            ps_tr = P.ps(es2, "ps_tr", (128, 1024), BF16); ps_tr_tl = Tl("ps_tr")
            ps_T = P.ps(es2, "ps_T"); ps_T_tl = Tl("ps_T")

            def wview(tab, z, pc, start, step):
                if step > 0:
                    v = tab[:, start:start + 8, z * 32 + pc * PCH: z * 32 + (pc + 1) * PCH]
                else:
                    stop = start - 8
                    v = tab[:, start:(stop if stop >= 0 else None):-1, z * 32 + pc * PCH: z * 32 + (pc + 1) * PCH]
                return v.rearrange("p j q -> p q j").unsqueeze(3).broadcast_to([128, PCH, 8, 16])

            def bview(tab, z, pc):
                return tab[:, z, pc * PCH:(pc + 1) * PCH, :].unsqueeze(2).broadcast_to([128, PCH, 8, 16])

            def oview(tab, z, comp):
                return tab[:, z, :, comp, :].rearrange("p q (j n) -> p q j n", n=16)

            def cprod(wr_v, wi_v, xr_v, xi_v, out_re, out_im, out_tl, neg_im):
                S.op("dve", lambda e: e.tensor_tensor(out=m1[:, :, :, :], in0=wr_v, in1=xr_v, op=ALU.mult),
                     reads=(pt,), writes=(m_tl[0],))
                S.op("pool", lambda e: e.tensor_tensor(out=m2t[:, :, :, :], in0=wi_v, in1=xi_v, op=ALU.mult),
                     reads=(pt,), writes=(m_tl[1],))
                S.op("dve", lambda e: e.tensor_tensor(out=out_re, in0=m1[:, :, :, :], in1=m2t[:, :, :, :], op=ALU.subtract),
                     reads=m_tl, writes=(out_tl,))
                S.op("dve", lambda e: e.tensor_tensor(out=m1[:, :, :, :], in0=wr_v, in1=xi_v, op=ALU.mult),
                     reads=(pt,), writes=(m_tl[0],))
                S.op("pool", lambda e: e.tensor_tensor(out=m2t[:, :, :, :], in0=wi_v, in1=xr_v, op=ALU.mult),
                     reads=(pt,), writes=(m_tl[1],))
                if neg_im:
                    S.op("dve", lambda e: e.scalar_tensor_tensor(out=out_im, in0=m1[:, :, :, :], scalar=-1.0,
                                                                 in1=m2t[:, :, :, :], op0=ALU.mult, op1=ALU.subtract),
                         reads=m_tl, writes=(out_tl,))
                else:
                    S.op("dve", lambda e: e.tensor_tensor(out=out_im, in0=m1[:, :, :, :], in1=m2t[:, :, :, :], op=ALU.add),
                         reads=m_tl, writes=(out_tl,))

            for pc in range(32 // PCH):
                for z in range(2):
                    if z == 0:
                        sB, stB = 15, -1
                        sN, stN = 7, -1
                        sC, stC = 9, 1
                    else:
                        sB, stB = 8, 1
                        sN, stN = 0, 1
                        sC, stC = 16, -1
                    bre_v = bview(t["s5_bre"], z, pc); bim_v = bview(t["s5_bim"], z, pc)
                    cre_v = bview(t["s5_cre"], z, pc); cim_v = bview(t["s5_cim"], z, pc)
                    cprod(wview(wfre, z, pc, sB, stB), wview(wfim, z, pc, sB, stB), bre_v, bim_v,
                          oview(MBt, z, 0), oview(MBt, z, 1), MBt_tl, False)
                    cprod(wview(wfre, z, pc, sN, stN), wview(wfim, z, pc, sN, stN), bre_v, bim_v,
                          oview(MBn, z, 0), oview(MBn, z, 1), MBn_tl, False)
                    cprod(wview(wre, z, pc, sC, stC), wview(wim, z, pc, sC, stC), cre_v, cim_v,
                          oview(MC, z, 0), oview(MC, z, 1), MC_tl, True)
                    S.op("act", lambda e, z=z, pc=pc: e.activation(
                        out=MCb[:, z, pc * PCH:(pc + 1) * PCH, :, :], in_=MC[:, z, :, :, :], func=AF.Copy),
                        reads=(MC_tl,), writes=(MCb_tl,))

                    def trs(e, z=z):
                        ins = None
                        for q in range(PCH):
                            for comp in range(2):
                                k = q * 2 + comp
                                ins = e.transpose(out=ps_tr[:, k * 128:(k + 1) * 128], in_=MBt[:, z, q, comp, :],
                                                  identity=C["identb"][:, :])
                        return ins
                    S.op("pe", trs, reads=(MBt_tl, C["ones_tl"]), writes=(ps_tr_tl,))
                    S.op("act", lambda e, z=z, pc=pc: e.activation(
                        out=MB[:, z, pc * PCH:(pc + 1) * PCH, :, :].rearrange("p q c m -> p (q c m)"),
                        in_=ps_tr[:, 0:PCH * 256], func=AF.Copy),
                        reads=(ps_tr_tl,), writes=(MB_tl,))
                for q in range(PCH):
                    for gpar in range(2):
                        g = 2 * (pc * PCH + q) + gpar
                        rows = slice(gpar * 64, (gpar + 1) * 64)

                        def mmT(e, q=q, rows=rows):
                            ins = None
                            for z in range(2):
                                for comp in range(2):
                                    ins = e.matmul(ps_T[:, z * 128:(z + 1) * 128], MBn[rows, z, q, comp, :],
                                                   MC[rows, z, q, comp, :], start=(comp == 0), stop=(comp == 1))
                            return ins
                        S.op("pe", mmT, reads=(MBn_tl, MC_tl), writes=(ps_T_tl,))
                        S.op("dve", lambda e: e.tensor_tensor(out=tt2[:, :, :].rearrange("p z m -> p (z m)"),
                                                              in0=ps_T[:, 0:256],
                                                              in1=m2[:, :, :].rearrange("p z m -> p (z m)"), op=ALU.mult),
                             reads=(ps_T_tl, pt), writes=(tt2_tl,))
                        S.op("pool", lambda e: e.tensor_tensor(out=tt3[:, :], in0=tt2[:, 0, :], in1=tt2[:, 1, :], op=ALU.add),
                             reads=(tt2_tl,), writes=(tt3_tl,))
                        S.op("dve", lambda e, g=g: e.scalar_tensor_tensor(
                            out=Toep[:, g, :], in0=C["ident"][:, :], scalar=t["s5_drep"][:, g:g + 1], in1=tt3[:, :],
                            op0=ALU.mult, op1=ALU.add),
                            reads=(tt3_tl, pt, C["ones_tl"]), writes=(Toep_tl,))
            S.barrier()
            S.flush()
        with ExitStack() as es3:
            ub = [[P.sb(es3, "ub%d%d" % (i, gp), [128, NCH], BF16) for gp in range(2)] for i in range(3)]
            ub_tl = [[[Tl("ub") for j in range(8)] for gp in range(2)] for i in range(3)]
            Hs = [[[P.sb(es3, "H%d%d%d" % (z, c, k), [128, NCH]) for k in range(4)] for c in range(2)] for z in range(2)]
            Hs_tl = [[[Tl("H") for k in range(4)] for c in range(2)] for z in range(2)]
            Hb = [[P.sb(es3, "Hb%d%d" % (z, c), [128, NCH], BF16) for c in range(2)] for z in range(2)]
            Hb_tl = [[Tl("Hb") for c in range(2)] for z in range(2)]
            yo = [P.sb(es3, "yo%d" % i, [128, NCH]) for i in range(2)]
            yo_tl = [Tl("yo") for i in range(2)]
            ps_S = [[P.ps(es3, "ps_S%d%d" % (z, c)) for c in range(2)] for z in range(2)]
            ps_S_tl = [[Tl("ps_S") for c in range(2)] for z in range(2)]
            ps_y = [P.ps(es3, "ps_y%d" % i) for i in range(2)]
            ps_y_tl = [Tl("ps_y") for i in range(2)]

            def front(pair):
                bi = pair % 3
                base = 2 * (pair % 2)
                for gp in range(2):
                    g_ = 2 * pair + gp
                    for j in range(8):
                        S.dma("pool", lambda e, bi=bi, gp=gp, g_=g_, j=j: e.dma_start(
                            out=ub[bi][gp][j * 16:(j + 1) * 16, :], in_=U_d[j, g_ * 16:(g_ + 1) * 16, :]),
                            writes=(ub_tl[bi][gp][j],))
                for z in range(2):
                    for comp in range(2):
                        def mmS(e, z=z, comp=comp, bi=bi, pair=pair):
                            ins = None
                            for gp in range(2):
                                ins = e.matmul(ps_S[z][comp][gp * 64:(gp + 1) * 64, 0:NCH],
                                               MB[:, z, pair, comp, gp * 64:(gp + 1) * 64], ub[bi][gp][:, :],
                                               start=True, stop=True)
                            return ins
                        S.op("pe", mmS, reads=[MB_tl] + ub_tl[bi][0] + ub_tl[bi][1], writes=(ps_S_tl[z][comp],))
                        S.op("act", lambda e, z=z, comp=comp, base=base: e.activation(out=Hs[z][comp][base][:, :],
                                                                                      in_=ps_S[z][comp][:, 0:NCH], func=AF.Copy),
                             reads=(ps_S_tl[z][comp],), writes=(Hs_tl[z][comp][base],))

            def ks(pair):
                base = 2 * (pair % 2)
                cur = 0
                for m in range(NST):
                    s = 2 ** m
                    nxt = 1 - cur
                    plan = []
                    for z in range(2):
                        col = z * 32 + pair
                        a_r = kar[:, m, col:col + 1]; a_i = kai[:, m, col:col + 1]; a_n = kan[:, m, col:col + 1]
                        o_re, o_im = Hs[z][0][base + cur], Hs[z][1][base + cur]
                        n_re, n_im = Hs[z][0][base + nxt], Hs[z][1][base + nxt]
                        o_tl = (Hs_tl[z][0][base + cur], Hs_tl[z][1][base + cur])
                        n_tl = (Hs_tl[z][0][base + nxt], Hs_tl[z][1][base + nxt])
                        if z == 0:
                            dst = slice(s, NCH); sh = slice(0, NCH - s); keep = slice(0, s)
                        else:
                            dst = slice(0, NCH - s); sh = slice(s, NCH); keep = slice(NCH - s, NCH)
                        S.op("act", lambda e, a=n_re, b=o_re, keep=keep: e.activation(out=a[:, keep], in_=b[:, keep], func=AF.Copy),
                             reads=(o_tl[0],), writes=(n_tl[0],))
                        S.op("act", lambda e, a=n_im, b=o_im, keep=keep: e.activation(out=a[:, keep], in_=b[:, keep], func=AF.Copy),
                             reads=(o_tl[1],), writes=(n_tl[1],))
                        plan.append((a_r, a_i, a_n, o_re, o_im, n_re, n_im, o_tl, n_tl, dst, sh))
                    for (a_r, a_i, a_n, o_re, o_im, n_re, n_im, o_tl, n_tl, dst, sh) in plan:
                        S.op("dve", lambda e, a=n_re, b=o_re, dst=dst, sh=sh, sc=a_r: e.scalar_tensor_tensor(
                            out=a[:, dst], in0=b[:, sh], scalar=sc, in1=b[:, dst], op0=ALU.mult, op1=ALU.add),
                            reads=(o_tl[0], ks_tl), writes=(n_tl[0],))
                    for (a_r, a_i, a_n, o_re, o_im, n_re, n_im, o_tl, n_tl, dst, sh) in plan:
                        S.op("dve", lambda e, a=n_im, b=o_im, dst=dst, sh=sh, sc=a_r: e.scalar_tensor_tensor(
                            out=a[:, dst], in0=b[:, sh], scalar=sc, in1=b[:, dst], op0=ALU.mult, op1=ALU.add),
                            reads=(o_tl[1], ks_tl), writes=(n_tl[1],))
                    for (a_r, a_i, a_n, o_re, o_im, n_re, n_im, o_tl, n_tl, dst, sh) in plan:
                        S.op("dve", lambda e, a=n_re, b=o_im, dst=dst, sh=sh, sc=a_n: e.scalar_tensor_tensor(
                            out=a[:, dst], in0=b[:, sh], scalar=sc, in1=a[:, dst], op0=ALU.mult, op1=ALU.add),
                            reads=(o_tl[1], n_tl[0], ks_tl), writes=(n_tl[0],))
                    for (a_r, a_i, a_n, o_re, o_im, n_re, n_im, o_tl, n_tl, dst, sh) in plan:
                        S.op("dve", lambda e, a=n_im, b=o_re, dst=dst, sh=sh, sc=a_i: e.scalar_tensor_tensor(
                            out=a[:, dst], in0=b[:, sh], scalar=sc, in1=a[:, dst], op0=ALU.mult, op1=ALU.add),
                            reads=(o_tl[0], n_tl[1], ks_tl), writes=(n_tl[1],))
                    cur = nxt
                return base + cur

            def back(pair, fin):
                bi = pair % 3
                for z in range(2):
                    for comp in range(2):
                        S.op("act", lambda e, z=z, comp=comp, fin=fin: e.activation(out=Hb[z][comp][:, :], in_=Hs[z][comp][fin][:, :],
                                                                                    func=AF.Copy),
                             reads=(Hs_tl[z][comp][fin],), writes=(Hb_tl[z][comp],))
                for gp in range(2):
                    g = 2 * pair + gp
                    rows = slice(gp * 64, (gp + 1) * 64)

                    def mmY(e, gp=gp, g=g, rows=rows, bi=bi, pair=pair):
                        e.matmul(ps_y[gp][:, 0:NCH], Toep[:, g, :], ub[bi][gp][:, :], start=True, stop=False)
                        e.matmul(ps_y[gp][:, 1:NCH], MCb[rows, 0, pair, 0, :], Hb[0][0][rows, 0:NCH - 1], start=False, stop=False)
                        e.matmul(ps_y[gp][:, 1:NCH], MCb[rows, 0, pair, 1, :], Hb[0][1][rows, 0:NCH - 1], start=False, stop=False)
                        e.matmul(ps_y[gp][:, 0:NCH - 1], MCb[rows, 1, pair, 0, :], Hb[1][0][rows, 1:NCH], start=False, stop=False)
                        return e.matmul(ps_y[gp][:, 0:NCH - 1], MCb[rows, 1, pair, 1, :], Hb[1][1][rows, 1:NCH],
                                        start=False, stop=True)
                    S.op("pe", mmY, reads=[Toep_tl, MCb_tl, Hb_tl[0][0], Hb_tl[0][1], Hb_tl[1][0], Hb_tl[1][1]] + ub_tl[bi][gp],
                         writes=(ps_y_tl[gp],))
                    S.op("act", lambda e, gp=gp: e.activation(out=yo[gp][:, :], in_=ps_y[gp][:, 0:NCH], func=AF.Copy),
                         reads=(ps_y_tl[gp],), writes=(yo_tl[gp],))
                    for i in range(8):
                        S.dma("sp", lambda e, gp=gp, g=g, i=i: e.dma_start(out=Y_d[i, g * 16:(g + 1) * 16, :],
                                                                         in_=yo[gp][i * 16:(i + 1) * 16, :]),
                              reads=(yo_tl[gp],), writes=(junk,), defer=True)

            front(0)
            for pair in range(32):
                if pair + 1 < 32:
                    front(pair + 1)
                fin = ks(pair)
                back(pair, fin)
            S.barrier()
            S.flush()
    return Y_d


GELU_K = 2.0 * math.sqrt(2.0 / math.pi)


def emit_gelu(S, src, src_tl, tmp, tmp_tl, out, out_tl):
    S.op("act", lambda e: e.activation(out=tmp, in_=src, func=AF.Square), reads=(src_tl,), writes=(tmp_tl,))
    S.op("dve", lambda e: e.tensor_scalar(out=tmp, in0=tmp, scalar1=0.044715, scalar2=1.0, op0=ALU.mult, op1=ALU.add),
         reads=(tmp_tl,), writes=(tmp_tl,))
    S.op("dve", lambda e: e.tensor_tensor(out=tmp, in0=tmp, in1=src, op=ALU.mult), reads=(tmp_tl, src_tl), writes=(tmp_tl,))
    S.op("act", lambda e: e.activation(out=tmp, in_=tmp, func=AF.Sigmoid, scale=GELU_K), reads=(tmp_tl,), writes=(tmp_tl,))
    S.op("dve", lambda e: e.tensor_tensor(out=out, in0=tmp, in1=src, op=ALU.mult), reads=(tmp_tl, src_tl), writes=(out_tl,))


def attn_host(inp):
    w = {}
    w["a_sink"] = np.ascontiguousarray(np.broadcast_to(inp["attn_sink"][0][None, :], (128, NQ)))
    jj = np.arange(128)
    m = np.zeros((128, 2, 128), np.float32)
    m[:, 0, :] = (jj[:, None] >= jj[None, :])
    m[:, 1, :] = (jj[:, None] <= jj[None, :])
    w["a_mask"] = m
    return w


def phase_attn(P, C, q_d, v_d):
    L, S = P.L, P.S
    NB = L // 128
    junk = Tl("dram")
    a_sink = P.din("a_sink", [128, NQ])
    a_mask = P.din("a_mask", [128, 2, 128])
    at_d = P.dscr("at_d", [NQ * HD, L], BF16)
    scale = HD ** -0.5
    with ExitStack() as es:
        esk = P.sb(es, "esk", [128, NQ]); msk = P.sb(es, "amask", [128, 2, 128]); cst = Tl("acst")
        S.dma("sp", lambda e: e.dma_start(out=esk[:, :], in_=a_sink), writes=(cst,))
        S.dma("sp", lambda e: e.dma_start(out=msk[:, :, :], in_=a_mask), writes=(cst,))
        S.op("act", lambda e: e.activation(out=esk[:, :], in_=esk[:, :], func=AF.Exp), reads=(cst,), writes=(cst,))
        kt = P.sb(es, "kt", [128, L], BF16); kt_tl = Tl("kt")
        q4 = P.sb(es, "q4", [128, 4, L], BF16); q4_tl = [Tl("q4") for i in range(4)]
        vt = P.sb(es, "vt", [128, NB, 128], BF16); vt_tls = [Tl("vt") for i in range(NB // 4)]
        ao = P.sb(es, "ao", [128, 4, L], BF16); ao_tl = Tl("ao")
        pb = [P.sb(es, "pb%d" % i, [128, 4, 128], BF16) for i in range(6)]
        pb_tl = [Tl("pb") for i in range(6)]
        den = P.sb(es, "den", [128, 4, 128]); den_tl = Tl("den")
        ps_s = [P.ps(es, "ps_s%d" % i) for i in range(3)]; ps_s_tl = [Tl("ps_s") for i in range(3)]
        ps_o = [P.ps(es, "ps_o%d" % i) for i in range(2)]; ps_o_tl = [Tl("ps_o") for i in range(2)]
        ps_d = [P.ps(es, "ps_d%d" % i) for i in range(2)]; ps_d_tl = [Tl("ps_d") for i in range(2)]
        vview = v_d.rearrange("(b p) c -> p b c", p=128)
        si = 0
        for h in range(NKV):
            S.dma("sp", lambda e, h=h: e.dma_start(out=kt[:, :], in_=q_d[NQ + h]), writes=(kt_tl,))
            for hh in range(4):
                S.dma("sp", lambda e, h=h, hh=hh: e.dma_start(out=q4[:, hh, :], in_=q_d[4 * h + hh]), writes=(q4_tl[hh],))
            for b0 in range(0, NB, 4):
                S.dma("sp", lambda e, h=h, b0=b0: e.dma_start(out=vt[:, b0:b0 + 4, :], in_=vview[:, b0:b0 + 4, h * 128:(h + 1) * 128]),
                      writes=(vt_tls[b0 // 4],))
            for n in range(NB):
                kbs = [kb for kb in (n - 1, n, n + 1) if 0 <= kb < NB]
                pbs = []
                for kb in kbs:
                    s_i = si % 3
                    p_i = si % 6
                    si += 1
                    S.op("pe", lambda e, s_i=s_i, kb=kb, n=n: e.matmul(
                        ps_s[s_i][:, :], kt[:, kb * 128:(kb + 1) * 128], q4[:, :, n * 128:(n + 1) * 128],
                        start=True, stop=True), reads=[kt_tl] + q4_tl, writes=(ps_s_tl[s_i],))
                    S.op("act", lambda e, s_i=s_i, p_i=p_i: e.activation(
                        out=pb[p_i][:, :, :].rearrange("p a b -> p (a b)"), in_=ps_s[s_i][:, :], func=AF.Exp, scale=scale),
                        reads=(ps_s_tl[s_i],), writes=(pb_tl[p_i],))
                    if kb != n:
                        mi = 0 if kb < n else 1
                        S.op("pool", lambda e, p_i=p_i, mi=mi: e.tensor_tensor(
                            out=pb[p_i][:, :, :], in0=pb[p_i][:, :, :],
                            in1=msk[:, mi, :].unsqueeze(1).broadcast_to([128, 4, 128]), op=ALU.mult),
                            reads=(pb_tl[p_i], cst), writes=(pb_tl[p_i],))
                    pbs.append(p_i)
                ob = n % 2

                def mmo(e, pbs=pbs, kbs=kbs, ob=ob):
                    ins = None
                    for i, (p_i, kb) in enumerate(zip(pbs, kbs)):
                        ins = e.matmul(ps_o[ob][:, :], vt[:, kb, :], pb[p_i][:, :, :].rearrange("p a b -> p (a b)"),
                                       start=(i == 0), stop=(i == len(kbs) - 1))
                    for i, (p_i, kb) in enumerate(zip(pbs, kbs)):
                        ins = e.matmul(ps_d[ob][:, :], C["onesb"][:, :], pb[p_i][:, :, :].rearrange("p a b -> p (a b)"),
                                       start=(i == 0), stop=(i == len(kbs) - 1))
                    return ins
                S.op("pe", mmo, reads=[C["ones_tl"]] + [vt_tls[kb // 4] for kb in kbs] + [pb_tl[p] for p in pbs],
                     writes=(ps_o_tl[ob], ps_d_tl[ob]))
                S.op("dve", lambda e, ob=ob, h=h: e.tensor_tensor(
                    out=den[:, :, :], in0=ps_d[ob][:, :].rearrange("p (a b) -> p a b", a=4),
                    in1=esk[:, 4 * h:4 * h + 4].unsqueeze(2).broadcast_to([128, 4, 128]), op=ALU.add),
                    reads=(ps_d_tl[ob], cst), writes=(den_tl,))
                S.op("dve", lambda e: e.reciprocal(out=den[:, :, :], in_=den[:, :, :]), reads=(den_tl,), writes=(den_tl,))
                S.op("dve", lambda e, ob=ob, n=n: e.tensor_tensor(
                    out=ao[:, :, n * 128:(n + 1) * 128], in0=ps_o[ob][:, :].rearrange("p (a b) -> p a b", a=4),
                    in1=den[:, :, :], op=ALU.mult),
                    reads=(ps_o_tl[ob], den_tl), writes=(ao_tl,))
            dst = at_d[h * 512:(h + 1) * 512, :].rearrange("(a d) t -> d a t", d=128)
            S.dma("sp", lambda e, dst=dst: e.dma_start(out=dst, in_=ao[:, :, :]), reads=(ao_tl,), writes=(junk,), defer=True)
        S.barrier()
        S.flush()
    return at_d


def l0e_host(inp):
    w = {}
    w["w_glu"] = kxm(inp["s5_w_glu"][0])
    w["b_glu"] = pvec(inp["s5_b_glu"][0])
    w["w_out0"] = kxm(inp["ev_w_out"][0])
    return w


def phase_l0e(P, C, Y_d, at_d, x_src, xres):
    L, NT, S = P.L, P.NT, P.S
    junk = Tl("dram")
    w_glu, w_glu_tl = P.wcast["w_glu"]
    b_glu = P.din("b_glu", [128, 8])
    w_out, w_out_tl = P.wcast["w_out0"]
    with ExitStack() as es:
        bg = P.sb(es, "bg", [128, 8]); cst = Tl("cst")
        S.dma("sp", lambda e: e.dma_start(out=bg[:, :], in_=b_glu), writes=(cst,))
        xt = P.sb(es, "xt", [128, KC, TT]); xt_tl = [Tl("xt") for k in range(KC)]
        yt = [P.sb(es, "yt%d" % i, [128, 8, TT // 8]) for i in range(2)]; yt_tl2 = [[Tl("yt") for j in range(2)] for i in range(2)]
        yn_ = [P.sb(es, "yn%d" % i, [128, TT]) for i in range(2)]; yn_tl_ = [Tl("yn") for i in range(2)]
        tmp_ = [P.sb(es, "tmp%d" % i, [128, TT]) for i in range(2)]; tmp_tl_ = [Tl("tmp") for i in range(2)]
        g32_ = [P.sb(es, "g32%d" % i, [128, 8, TT]) for i in range(2)]; g32_tl_ = [[Tl("g32") for k in range(8)] for i in range(2)]
        gb_ = [P.sb(es, "gb%d" % i, [128, 8, TT], BF16) for i in range(2)]; gb_tl_ = [[Tl("gb") for k in range(8)] for i in range(2)]
        mix_ = [P.sb(es, "mix%d" % i, [128, KC, TT], BF16) for i in range(2)]
        mix_tl_ = [[Tl("mix") for k in range(KC)] for i in range(2)]
        sg_ = [P.sb(es, "sg%d" % i, [128, TT]) for i in range(2)]; sg_tl_ = [Tl("sg") for i in range(2)]
        NWG = 3
        NWO = 4
        NPS = 6
        wg = [P.sb(es, "wg%d" % i, [128, 8, 128], BF16) for i in range(NWG)]; wg_tl = [Tl("wg") for i in range(NWG)]
        wo = [P.sb(es, "wo%d" % i, [128, KC, 128], BF16) for i in range(NWO)]; wo_tl = [Tl("wo") for i in range(NWO)]
        xo = [P.sb(es, "xo%d" % i, [128, TT]) for i in range(2)]; xo_tl = [Tl("xo") for i in range(2)]
        ps = [P.ps(es, "ps%d" % i) for i in range(NPS)]; ps_tl = [Tl("ps") for i in range(NPS)]
        xview = x_src.rearrange("(kc p) t -> p kc t", p=128)
        cnt = {"pi": 0}

        def gelu_stage(tt):
            t0 = tt * TT
            c0 = tt * (TT // 8)
            k = tt % 2
            g32, g32_tl, gb, gb_tl, mix, mix_tl = g32_[k], g32_tl_[k], gb_[k], gb_tl_[k], mix_[k], mix_tl_[k]
            S.dma("sp", lambda e: e.dma_start(
                out=mix[:, 8:16, :], in_=at_d.rearrange("(c p) t -> p c t", p=128)[:, :, t0:t0 + TT]),
                writes=mix_tl[8:16])
            for ct in range(8):
                yb = ct % 2
                for i0_ in (0, 4):
                    S.dma("sp", lambda e, yb=yb, ct=ct, i0_=i0_: e.dma_start(
                        out=yt[yb][:, i0_:i0_ + 4, :],
                        in_=Y_d[i0_:i0_ + 4, ct * 128:(ct + 1) * 128, c0:c0 + TT // 8].rearrange("i p c -> p i c")),
                        writes=(yt_tl2[yb][i0_ // 4],))
                yn, yn_tl, tmp, tmp_tl = yn_[yb], yn_tl_[yb], tmp_[yb], tmp_tl_[yb]
                S.op("act", lambda e, yb=yb, yn=yn: e.activation(out=yn[:, :].rearrange("p (c i) -> p i c", i=8), in_=yt[yb][:, :, :],
                                                                 func=AF.Copy),
                     reads=yt_tl2[yb], writes=(yn_tl,))
                emit_gelu(S, yn[:, :], yn_tl, tmp[:, :], tmp_tl, g32[:, ct, :], g32_tl[ct])
                S.op("act", lambda e, ct=ct: e.activation(out=gb[:, ct, :], in_=g32[:, ct, :], func=AF.Copy),
                     reads=(g32_tl[ct],), writes=(gb_tl[ct],))

        gelu_stage(0)
        for tt in range(NT):
            t0 = tt * TT
            k = tt % 2
            g32, g32_tl, gb, gb_tl, mix, mix_tl = g32_[k], g32_tl_[k], gb_[k], gb_tl_[k], mix_[k], mix_tl_[k]
            for kc in range(KC):
                S.dma("sp", lambda e, kc=kc, t0=t0: e.dma_start(out=xt[:, kc, :], in_=xview[:, kc, t0:t0 + TT]),
                      writes=(xt_tl[kc],))
            for oc in range(8):
                b = oc % NWG
                sg, sg_tl = sg_[oc % 2], sg_tl_[oc % 2]
                S.dma("pool", lambda e, b=b, oc=oc: e.dma_start(out=wg[b][:, :, :], in_=w_glu[oc]), reads=(w_glu_tl[oc],), writes=(wg_tl[b],))
                p_i = cnt["pi"] % NPS
                cnt["pi"] += 1

                def mm(e, b=b, p_i=p_i, gb=gb):
                    ins = None
                    for kc in range(8):
                        ins = e.matmul(ps[p_i][:, :], wg[b][:, kc, :], gb[:, kc, :], start=(kc == 0), stop=(kc == 7))
                    return ins
                S.op("pe", mm, reads=[wg_tl[b]] + gb_tl, writes=(ps_tl[p_i],))
                S.op("act", lambda e, p_i=p_i, oc=oc, sg=sg: e.activation(out=sg[:, :], in_=ps[p_i][:, :], func=AF.Sigmoid,
                                                                          bias=bg[:, oc:oc + 1]),
                     reads=(ps_tl[p_i], cst), writes=(sg_tl,))
                S.op("dve", lambda e, oc=oc, sg=sg, mix=mix, g32=g32: e.tensor_tensor(out=mix[:, oc, :], in0=g32[:, oc, :], in1=sg[:, :],
                                                                                      op=ALU.mult),
                     reads=(g32_tl[oc], sg_tl), writes=(mix_tl[oc],))
            for oc in range(KC):
                if oc == 2 and tt + 1 < NT:
                    gelu_stage(tt + 1)
                b = oc % NWO
                xb_ = oc % 2
                S.dma("pool", lambda e, b=b, oc=oc: e.dma_start(out=wo[b][:, :, :], in_=w_out[oc]), reads=(w_out_tl[oc],), writes=(wo_tl[b],))
                p_i = cnt["pi"] % NPS
                cnt["pi"] += 1

                def mm2(e, b=b, p_i=p_i, mix=mix):
                    ins = None
                    for kc in range(KC):
                        ins = e.matmul(ps[p_i][:, :], wo[b][:, kc, :], mix[:, kc, :], start=(kc == 0), stop=(kc == KC - 1))
                    return ins
                S.op("pe", mm2, reads=[wo_tl[b]] + mix_tl, writes=(ps_tl[p_i],))
                S.op("dve", lambda e, xb_=xb_, p_i=p_i, oc=oc: e.tensor_tensor(out=xo[xb_][:, :], in0=ps[p_i][:, :], in1=xt[:, oc, :],
                                                                               op=ALU.add),
                     reads=(ps_tl[p_i], xt_tl[oc]), writes=(xo_tl[xb_],))
                S.dma("sp", lambda e, xb_=xb_, oc=oc, t0=t0: e.dma_start(out=xres[oc * 128:(oc + 1) * 128, t0:t0 + TT], in_=xo[xb_][:, :]),
                      reads=(xo_tl[xb_],), writes=(junk,), defer=True)
        S.barrier()
        S.flush()


def ffn_host(inp):
    w = {}
    for l in range(2):
        w["w1_%d" % l] = kxm(inp["ffn_w1"][l])
        w["w3_%d" % l] = kxm(inp["ffn_w3"][l])
        w["w2_%d" % l] = kxm(inp["ffn_w2"][l])
    return w


def phase_ffn(P, C, layer, xres, TF):
    L, S = P.L, P.S
    NTF = L // TF
    NH = TF // TT
    junk = Tl("dram")
    w1 = P.din("w1_%d" % layer, [FC, 128, KC, 128])
    w3 = P.din("w3_%d" % layer, [FC, 128, KC, 128])
    w2 = P.din("w2_%d" % layer, [KC, 128, FC, 128])
    gain = C["gffn"][:, layer, :]
    with ExitStack() as es:
        hT = P.sb(es, "hT", [128, KC, TF], BF16); hT_tl = [Tl("hT") for k in range(KC)]
        act = P.sb(es, "act", [128, FC, TF], BF16); act_tl = [Tl("act") for k in range(FC)]
        xc = [P.sb(es, "xc%d" % i, [128, TF]) for i in range(3)]; xc_tl = [Tl("xc") for i in range(3)]
        sq = [P.sb(es, "sq%d" % i, [128, TF]) for i in range(2)]; sq_tl = [Tl("sq") for i in range(2)]
        rs = P.sb(es, "rs", [128, TF]); rs_tl = Tl("rs")
        s1 = [P.sb(es, "s1%d" % i, [128, TT]) for i in range(2)]; s1_tl = [Tl("s1") for i in range(2)]
        wa = [P.sb(es, "wa%d" % i, [128, KC, 128], BF16) for i in range(2)]; wa_tl = [Tl("wa") for i in range(2)]
        wb3 = [P.sb(es, "wb%d" % i, [128, KC, 128], BF16) for i in range(2)]; wb3_tl = [Tl("wb") for i in range(2)]
        wc = [P.sb(es, "wc%d" % i, [128, FC, 128], BF16) for i in range(2)]; wc_tl = [[Tl("wc") for k in range(3)] for i in range(2)]
        xo = [P.sb(es, "xo%d" % i, [128, TT]) for i in range(2)]; xo_tl = [Tl("xo") for i in range(2)]
        ps_st = [P.ps(es, "ps_st%d" % i) for i in range(NH)]; ps_st_tl = [Tl("ps_st") for i in range(NH)]
        ps_a = [P.ps(es, "ps_a%d" % i) for i in range(2)]; ps_a_tl = [Tl("ps_a") for i in range(2)]
        ps_b = [P.ps(es, "ps_b%d" % i) for i in range(2)]; ps_b_tl = [Tl("ps_b") for i in range(2)]
        ps_o = [P.ps(es, "ps_o%d" % i) for i in range(2)]; ps_o_tl = [Tl("ps_o") for i in range(2)]
        xview = xres.rearrange("(kc p) t -> p kc t", p=128)
        rs2 = [rs, P.sb(es, "rsb", [128, TF])]; rs2_tl = [rs_tl, Tl("rsb")]
        cnt = {"xi": 0, "wi": 0}

        def stats(tf):
            t0 = tf * TF
            rsx, rsx_tl = rs2[tf % 2], rs2_tl[tf % 2]
            for kc in range(KC):
                b = cnt["xi"] % 3; cnt["xi"] += 1
                sb_ = kc % 2
                S.dma("sp", lambda e, b=b, kc=kc, t0=t0: e.dma_start(out=xc[b][:, :], in_=xview[:, kc, t0:t0 + TF]),
                      writes=(xc_tl[b],))
                S.op("act", lambda e, b=b, sb_=sb_: e.activation(out=sq[sb_][:, :], in_=xc[b][:, :], func=AF.Square),
                     reads=(xc_tl[b],), writes=(sq_tl[sb_],))

                def mm(e, kc=kc, sb_=sb_):
                    ins = None
                    for hh in range(NH):
                        ins = e.matmul(ps_st[hh][:, :], C["ones"][:, :], sq[sb_][:, hh * TT:(hh + 1) * TT],
                                       start=(kc == 0), stop=(kc == KC - 1))
                    return ins
                S.op("pe", mm, reads=(sq_tl[sb_], C["ones_tl"]), writes=ps_st_tl)
            for hh in range(NH):
                S.op("act", lambda e, hh=hh, rsx=rsx: e.activation(out=rsx[:, hh * TT:(hh + 1) * TT], in_=ps_st[hh][:, :], func=AF.Ln,
                                                                   scale=1.0 / D, bias=C["eps"][:, 0:1]),
                     reads=(ps_st_tl[hh], C["ones_tl"]), writes=(rsx_tl,))
            S.op("act", lambda e, rsx=rsx: e.activation(out=rsx[:, :], in_=rsx[:, :], func=AF.Exp, scale=-0.5),
                 reads=(rsx_tl,), writes=(rsx_tl,))

        stats(0)
        for tf in range(NTF):
            t0 = tf * TF
            rsx, rsx_tl = rs2[tf % 2], rs2_tl[tf % 2]
            for kc in range(KC):
                b = cnt["xi"] % 3; cnt["xi"] += 1
                S.dma("sp", lambda e, b=b, kc=kc, t0=t0: e.dma_start(out=xc[b][:, :], in_=xview[:, kc, t0:t0 + TF]),
                      writes=(xc_tl[b],))
                S.op("dve", lambda e, b=b, kc=kc, rsx=rsx: e.scalar_tensor_tensor(
                    out=hT[:, kc, :], in0=xc[b][:, :], scalar=gain[:, kc:kc + 1], in1=rsx[:, :], op0=ALU.mult, op1=ALU.mult),
                    reads=(xc_tl[b], rsx_tl, C["ones_tl"]), writes=(hT_tl[kc],))
            for fc in range(FC):
                b = cnt["wi"] % 2; cnt["wi"] += 1
                S.dma("pool", lambda e, b=b, fc=fc: e.dma_start(out=wa[b][:, :, :], in_=w1[fc]), writes=(wa_tl[b],))
                S.dma("pool", lambda e, b=b, fc=fc: e.dma_start(out=wb3[b][:, :, :], in_=w3[fc]), writes=(wb3_tl[b],))
                for hh in range(NH):
                    pb_ = hh % 2
                    cs = slice(hh * TT, (hh + 1) * TT)

                    def mm1(e, b=b, pb_=pb_, cs=cs):
                        ins = None
                        for kc in range(KC):
                            ins = e.matmul(ps_a[pb_][:, :], wa[b][:, kc, :], hT[:, kc, cs], start=(kc == 0), stop=(kc == KC - 1))
                        return ins

                    def mm3(e, b=b, pb_=pb_, cs=cs):
                        ins = None
                        for kc in range(KC):
                            ins = e.matmul(ps_b[pb_][:, :], wb3[b][:, kc, :], hT[:, kc, cs], start=(kc == 0), stop=(kc == KC - 1))
                        return ins
                    S.op("pe", mm1, reads=[wa_tl[b]] + hT_tl, writes=(ps_a_tl[pb_],))
                    S.op("pe", mm3, reads=[wb3_tl[b]] + hT_tl, writes=(ps_b_tl[pb_],))
                    S.op("act", lambda e, pb_=pb_: e.activation(out=s1[pb_][:, :], in_=ps_a[pb_][:, :], func=AF.Silu),
                         reads=(ps_a_tl[pb_],), writes=(s1_tl[pb_],))
                    S.op("dve", lambda e, pb_=pb_, fc=fc, cs=cs: e.tensor_tensor(out=act[:, fc, cs], in0=ps_b[pb_][:, :],
                                                                                in1=s1[pb_][:, :], op=ALU.mult),
                         reads=(ps_b_tl[pb_], s1_tl[pb_]), writes=(act_tl[fc],))
            for oc in range(KC):
                b = oc % 2
                for pi_, (k0, k1) in enumerate(((0, 16), (16, 32), (32, FC))):
                    S.dma("pool", lambda e, b=b, oc=oc, k0=k0, k1=k1: e.dma_start(out=wc[b][:, k0:k1, :], in_=w2[oc][:, k0:k1, :]),
                          writes=(wc_tl[b][pi_],))
                xb_ = cnt["xi"] % 3; cnt["xi"] += 1
                S.dma("sp", lambda e, xb_=xb_, oc=oc, t0=t0: e.dma_start(out=xc[xb_][:, :], in_=xview[:, oc, t0:t0 + TF]),
                      writes=(xc_tl[xb_],))
                for hh in range(NH):
                    pb_ = hh % 2
                    cs = slice(hh * TT, (hh + 1) * TT)

                    def mm2(e, b=b, pb_=pb_, cs=cs):
                        ins = None
                        for fc in range(FC):
                            ins = e.matmul(ps_o[pb_][:, :], wc[b][:, fc, :], act[:, fc, cs], start=(fc == 0), stop=(fc == FC - 1))
                        return ins
                    S.op("pe", mm2, reads=wc_tl[b] + act_tl, writes=(ps_o_tl[pb_],))
                    S.op("dve", lambda e, pb_=pb_, xb_=xb_, cs=cs: e.tensor_tensor(out=xo[pb_][:, :], in0=ps_o[pb_][:, :],
                                                                                  in1=xc[xb_][:, cs], op=ALU.add),
                         reads=(ps_o_tl[pb_], xc_tl[xb_]), writes=(xo_tl[pb_],))
                    S.dma("sp", lambda e, pb_=pb_, oc=oc, t0=t0, hh=hh: e.dma_start(
                        out=xres[oc * 128:(oc + 1) * 128, t0 + hh * TT:t0 + (hh + 1) * TT], in_=xo[pb_][:, :]),
                        reads=(xo_tl[pb_],), writes=(junk,), defer=True)
                if oc == 3 and tf + 1 < NTF:
                    stats(tf + 1)
        S.barrier()
        S.flush()


def l1_host(inp):
    w = {}
    w["w_in1"] = kxm(inp["od_w_in"][0])
    w["w_out1"] = kxm(inp["od_w_out"][0])
    w["l_cw"] = np.ascontiguousarray(inp["lru_conv_w"][0].reshape(4, KC, 128).transpose(2, 1, 0))
    w["l_cb"] = pvec(inp["lru_conv_b"][0])

    def v2(a):
        return np.ascontiguousarray(a.reshape(2, KC, 128).transpose(2, 0, 1))
    w["l_ba"] = v2(inp["lru_ba"][0]); w["l_bx"] = v2(inp["lru_bx"][0]); w["l_lam"] = v2(inp["lru_lam"][0])
    w["l_wa"] = np.ascontiguousarray(inp["lru_wa"][0])
    w["l_wx"] = np.ascontiguousarray(inp["lru_wx"][0])
    return w


def phase_l1a(P, C, xres):
    L, NT, S = P.L, P.NT, P.S
    junk = Tl("dram")
    w_in, w_in_tl = P.wcast["w_in1"]
    gate_d = P.dscr("gate_d", [D, L], BF16)
    xb_d = P.dscr("xb_d", [D, L], F32)
    with ExitStack() as es:
        xt = P.sb(es, "xt", [128, KC, TT]); xt_tl = Tl("xt")
        sq = P.sb(es, "sq", [128, KC, TT]); sq_tl = Tl("sq")
        hT2 = [P.sb(es, "hT%d" % i, [128, KC, TT], BF16) for i in range(2)]
        hT2_tl = [[Tl("hT") for k in range(KC)] for i in range(2)]
        rs = P.sb(es, "rs", [128, TT]); rs_tl = Tl("rs")
        NWB = 4
        NPO = 5
        wb = [P.sb(es, "wb%d" % i, [128, KC, 128], BF16) for i in range(NWB)]; wb_tl = [Tl("wb") for i in range(NWB)]
        tmp_ = [P.sb(es, "tmp%d" % i, [128, TT]) for i in range(2)]; tmp_tl_ = [Tl("tmp") for i in range(2)]
        go = [P.sb(es, "go%d" % i, [128, TT], BF16) for i in range(2)]; go_tl = [Tl("go") for i in range(2)]
        xo = [P.sb(es, "xo%d" % i, [128, TT]) for i in range(2)]; xo_tl = [Tl("xo") for i in range(2)]
        ps_stat = P.ps(es, "ps_stat"); ps_stat_tl = Tl("ps_stat")
        ps_o = [P.ps(es, "ps_o%d" % i) for i in range(NPO)]; ps_o_tl = [Tl("ps_o") for i in range(NPO)]
        xview = xres.rearrange("(kc p) t -> p kc t", p=128)
        wi = 0
        def norm(tt_):
            t0_ = tt_ * TT
            S.dma("sp", lambda e, t0_=t0_: e.dma_start(out=xt[:, :, :], in_=xview[:, :, t0_:t0_ + TT]), writes=(xt_tl,))
            emit_rmsnorm(P, C, xt, xt_tl, C["gmix"][:, 1, :], hT2[tt_ % 2], hT2_tl[tt_ % 2], sq, sq_tl,
                         ps_stat, ps_stat_tl, rs, rs_tl)

        norm(0)
        for tt in range(NT):
            t0 = tt * TT
            hT, hT_tl = hT2[tt % 2], hT2_tl[tt % 2]
            for oc in range(32):
                if oc == 10 and tt + 1 < NT:
                    norm(tt + 1)
                wbi = wi % NWB
                b = wi % 2
                pb_ = wi % NPO
                wi += 1
                S.dma("pool", lambda e, wbi=wbi, oc=oc: e.dma_start(out=wb[wbi][:, :, :], in_=w_in[oc]), reads=(w_in_tl[oc],), writes=(wb_tl[wbi],))

                def mm(e, wbi=wbi, pb_=pb_, hT=hT):
                    ins = None
                    for kc in range(KC):
                        ins = e.matmul(ps_o[pb_][:, :], wb[wbi][:, kc, :], hT[:, kc, :], start=(kc == 0), stop=(kc == KC - 1))
                    return ins
                S.op("pe", mm, reads=[wb_tl[wbi]] + hT_tl, writes=(ps_o_tl[pb_],))
                if oc < 16:
                    emit_gelu(S, ps_o[pb_][:, :], ps_o_tl[pb_], tmp_[b][:, :], tmp_tl_[b], go[b][:, :], go_tl[b])
                    S.dma("sp", lambda e, b=b, oc=oc, t0=t0: e.dma_start(out=gate_d[oc * 128:(oc + 1) * 128, t0:t0 + TT], in_=go[b][:, :]),
                          reads=(go_tl[b],), writes=(junk,), defer=True)
                else:
                    S.op("act", lambda e, b=b, pb_=pb_: e.activation(out=xo[b][:, :], in_=ps_o[pb_][:, :], func=AF.Copy),
                         reads=(ps_o_tl[pb_],), writes=(xo_tl[b],))
                    S.dma("sp", lambda e, b=b, oc=oc, t0=t0: e.dma_start(
                        out=xb_d[(oc - 16) * 128:(oc - 15) * 128, t0:t0 + TT], in_=xo[b][:, :]),
                        reads=(xo_tl[b],), writes=(junk,), defer=True)
        S.barrier()
        S.flush()
    return gate_d, xb_d


def phase_l1b(P, C, gate_d, xb_d):
    L, S = P.L, P.S
    NT = L // TT
    junk = Tl("dram")
    srcs = {nm: P.din(nm, shp) for nm, shp in (("l_cw", [128, KC, 4]), ("l_cb", [128, KC]), ("l_ba", [128, 2, KC]),
                                               ("l_bx", [128, 2, KC]), ("l_lam", [128, 2, KC]))}
    l_wa = P.din("l_wa", [2, 8, 256, 256])
    l_wx = P.din("l_wx", [2, 8, 256, 256])
    yrec_d = P.dscr("yrec_d", [D, L], BF16)
    with ExitStack() as es:
        cst = Tl("cst")
        cw = P.sb(es, "cw", [128, KC, 4]); cb = P.sb(es, "cb", [128, KC])
        ba = P.sb(es, "ba", [128, 2, KC]); bx = P.sb(es, "bx", [128, 2, KC]); lam = P.sb(es, "lam", [128, 2 * KC])
        S.dma("sp", lambda e: e.dma_start(out=cw[:, :, :], in_=srcs["l_cw"]), writes=(cst,))
        S.dma("sp", lambda e: e.dma_start(out=cb[:, :], in_=srcs["l_cb"]), writes=(cst,))
        S.dma("sp", lambda e: e.dma_start(out=ba[:, :, :], in_=srcs["l_ba"]), writes=(cst,))
        S.dma("sp", lambda e: e.dma_start(out=bx[:, :, :], in_=srcs["l_bx"]), writes=(cst,))
        S.dma("sp", lambda e: e.dma_start(out=lam[:, :], in_=srcs["l_lam"].rearrange("p z c -> p (z c)")), writes=(cst,))
        tt_ = P.sb(es, "sp_t", [128, 2 * KC]); ser = P.sb(es, "sp_ser", [128, 2 * KC]); lnv = P.sb(es, "sp_ln", [128, 2 * KC])
        mk = P.sb(es, "sp_mk", [128, 2 * KC]); nsp = P.sb(es, "nsp", [128, 2 * KC])
        O = lambda eng, f: S.op(eng, f, reads=(cst,), writes=(cst,))
        O("act", lambda e: e.activation(out=tt_[:, :], in_=lam[:, :], func=AF.Exp, scale=-1.0))
        O("act", lambda e: e.activation(out=lnv[:, :], in_=tt_[:, :], func=AF.Ln, bias=C["ones"][:, 0:1]))
        O("dve", lambda e: e.tensor_scalar(out=ser[:, :], in0=tt_[:, :], scalar1=-0.25, scalar2=1.0 / 3.0, op0=ALU.mult, op1=ALU.add))
        O("dve", lambda e: e.tensor_tensor(out=ser[:, :], in0=ser[:, :], in1=tt_[:, :], op=ALU.mult))
        O("dve", lambda e: e.tensor_scalar(out=ser[:, :], in0=ser[:, :], scalar1=-0.5, scalar2=None, op0=ALU.add))
        O("dve", lambda e: e.tensor_tensor(out=ser[:, :], in0=ser[:, :], in1=tt_[:, :], op=ALU.mult))
        O("dve", lambda e: e.tensor_scalar(out=ser[:, :], in0=ser[:, :], scalar1=1.0, scalar2=None, op0=ALU.add))
        O("dve", lambda e: e.tensor_tensor(out=ser[:, :], in0=ser[:, :], in1=tt_[:, :], op=ALU.mult))
        O("dve", lambda e: e.tensor_scalar(out=mk[:, :], in0=tt_[:, :], scalar1=0.05, scalar2=None, op0=ALU.is_lt))
        O("dve", lambda e: e.tensor_tensor(out=ser[:, :], in0=ser[:, :], in1=lnv[:, :], op=ALU.subtract))
        O("dve", lambda e: e.tensor_tensor(out=ser[:, :], in0=ser[:, :], in1=mk[:, :], op=ALU.mult))
        O("dve", lambda e: e.tensor_tensor(out=ser[:, :], in0=ser[:, :], in1=lnv[:, :], op=ALU.add))
        O("dve", lambda e: e.tensor_scalar(out=nsp[:, :], in0=ser[:, :], scalar1=-8.0, scalar2=None, op0=ALU.mult))

        xb = P.sb(es, "xb", [128, 2, L]); xb_tl = [Tl("xb") for i in range(2)]
        cv = P.sb(es, "cv", [128, 2, L]); cv_tl = [Tl("cv") for i in range(2)]
        cvb = P.sb(es, "cvb", [128, 2, L], BF16); cvb_tl = [Tl("cvb") for i in range(2)]
        hs = xb; hs_tl = xb_tl
        rr_ = [P.sb(es, "rr%d" % i, [128, L]) for i in range(2)]; rr_tl_ = [Tl("rr") for i in range(2)]
        ii_ = [P.sb(es, "ii%d" % i, [128, L]) for i in range(2)]; ii_tl_ = [Tl("ii") for i in range(2)]
        tmp_ = [P.sb(es, "tmp%d" % i, [128, L]) for i in range(2)]; tmp_tl_ = [Tl("tmp") for i in range(2)]
        gt = P.sb(es, "gt", [128, 2, L], BF16); gt_tl = [Tl("gt") for i in range(2)]
        wsb = [[P.sb(es, "w%d%d" % (k, z), [128, 2, 256], BF16) for z in range(2)] for k in range(2)]
        wsb_tl = [[Tl("w") for z in range(2)] for k in range(2)]
        ps_r = [P.ps(es, "ps_r%d" % i) for i in range(3)]; ps_r_tl = [Tl("ps_r") for i in range(3)]
        ps_i = [P.ps(es, "ps_i%d" % i) for i in range(3)]; ps_i_tl = [Tl("ps_i") for i in range(3)]
        pi_box = [0]
        ui = 0

        def stage1(nb, z, co, k):
            ct = nb * 2 + co
            rr, rr_tl, ii, ii_tl = rr_[k], rr_tl_[k], ii_[k], ii_tl_[k]
            for tk in range(NT):
                cs = slice(tk * TT, (tk + 1) * TT)
                pb_ = pi_box[0] % 3
                pi_box[0] += 1

                def mmr(e, z=z, co=co, cs=cs, pb_=pb_):
                    ins = None
                    for ci in range(2):
                        ins = e.matmul(ps_r[pb_][:, :], wsb[0][z][:, ci, co * 128:(co + 1) * 128], cvb[:, ci, cs],
                                       start=(ci == 0), stop=(ci == 1))
                    return ins

                def mmi(e, z=z, co=co, cs=cs, pb_=pb_):
                    ins = None
                    for ci in range(2):
                        ins = e.matmul(ps_i[pb_][:, :], wsb[1][z][:, ci, co * 128:(co + 1) * 128], cvb[:, ci, cs],
                                       start=(ci == 0), stop=(ci == 1))
                    return ins
                S.op("pe", mmr, reads=(wsb_tl[0][z], cvb_tl[0], cvb_tl[1]), writes=(ps_r_tl[pb_],))
                S.op("pe", mmi, reads=(wsb_tl[1][z], cvb_tl[0], cvb_tl[1]), writes=(ps_i_tl[pb_],))
                S.op("act", lambda e, z=z, ct=ct, cs=cs, pb_=pb_, rr=rr: e.activation(
                    out=rr[:, cs], in_=ps_r[pb_][:, :], func=AF.Sigmoid, bias=ba[:, z, ct:ct + 1]),
                    reads=(ps_r_tl[pb_], cst), writes=(rr_tl,))
                S.op("act", lambda e, z=z, ct=ct, cs=cs, pb_=pb_, ii=ii: e.activation(
                    out=ii[:, cs], in_=ps_i[pb_][:, :], func=AF.Sigmoid, bias=bx[:, z, ct:ct + 1]),
                    reads=(ps_i_tl[pb_], cst), writes=(ii_tl,))

        def stage2(nb, z, co, k):
            ct = nb * 2 + co
            rr, rr_tl, ii, ii_tl, tmp, tmp_tl = rr_[k], rr_tl_[k], ii_[k], ii_tl_[k], tmp_[k], tmp_tl_[k]
            col = z * KC + ct
            S.op("act", lambda e: e.activation(out=rr[:, :], in_=rr[:, :], func=AF.Exp, scale=nsp[:, col:col + 1]),
                 reads=(rr_tl, cst), writes=(rr_tl,))
            S.op("dve", lambda e: e.tensor_tensor(out=ii[:, :], in0=ii[:, :], in1=cv[:, co, :], op=ALU.mult),
                 reads=(ii_tl, cv_tl[co]), writes=(ii_tl,))
            S.op("act", lambda e: e.activation(out=tmp[:, :], in_=rr[:, :], func=AF.Square),
                 reads=(rr_tl,), writes=(tmp_tl,))
            S.op("act", lambda e: e.activation(out=tmp[:, :], in_=tmp[:, :], func=AF.Sqrt, scale=-1.0, bias=C["ones"][:, 0:1]),
                 reads=(tmp_tl, C["ones_tl"]), writes=(tmp_tl,))
            S.op("dve", lambda e: e.tensor_tensor(out=ii[:, :], in0=ii[:, :], in1=tmp[:, :], op=ALU.mult),
                 reads=(ii_tl, tmp_tl), writes=(ii_tl,))
            if z == 0:
                S.op("dve", lambda e: e.tensor_tensor_scan(out=hs[:, co, :], data0=rr[:, :], data1=ii[:, :], initial=0.0,
                                                           op0=ALU.mult, op1=ALU.add),
                     reads=(rr_tl, ii_tl), writes=(hs_tl[co],))
            else:
                S.op("dve", lambda e: e.tensor_tensor_scan(out=tmp[:, ::-1], data0=rr[:, ::-1], data1=ii[:, ::-1], initial=0.0,
                                                           op0=ALU.mult, op1=ALU.add),
                     reads=(rr_tl, ii_tl, tmp_tl), writes=(tmp_tl,))
                S.op("dve", lambda e: e.tensor_tensor(out=hs[:, co, :], in0=hs[:, co, :], in1=tmp[:, :], op=ALU.add),
                     reads=(hs_tl[co], tmp_tl), writes=(hs_tl[co],))

        for nb in range(8):
            for ci in range(2):
                S.dma("sp", lambda e, ci=ci, nb=nb: e.dma_start(out=xb[:, ci, :], in_=xb_d[nb * 256 + ci * 128: nb * 256 + (ci + 1) * 128, :]),
                      writes=(xb_tl[ci],))
                S.dma("sp", lambda e, ci=ci, nb=nb: e.dma_start(out=gt[:, ci, :], in_=gate_d[nb * 256 + ci * 128: nb * 256 + (ci + 1) * 128, :]),
                      writes=(gt_tl[ci],))
            for z in range(2):
                S.dma("pool", lambda e, z=z, nb=nb: e.dma_start(out=wsb[0][z][:, :, :], in_=l_wa[z, nb].rearrange("(c p) j -> p c j", p=128)),
                      writes=(wsb_tl[0][z],))
                S.dma("pool", lambda e, z=z, nb=nb: e.dma_start(out=wsb[1][z][:, :, :], in_=l_wx[z, nb].rearrange("(c p) j -> p c j", p=128)),
                      writes=(wsb_tl[1][z],))
            for ci in range(2):
                ct = nb * 2 + ci
                S.op("act", lambda e, ci=ci, ct=ct: e.activation(out=cv[:, ci, :], in_=xb[:, ci, :], func=AF.Identity,
                                                                 scale=cw[:, ct, 2:3], bias=cb[:, ct:ct + 1]),
                     reads=(xb_tl[ci], cst), writes=(cv_tl[ci],))
                for (tap, dsl, ssl) in ((0, slice(2, L), slice(0, L - 2)), (1, slice(1, L), slice(0, L - 1)),
                                        (3, slice(0, L - 1), slice(1, L))):
                    S.op("dve", lambda e, ci=ci, ct=ct, tap=tap, dsl=dsl, ssl=ssl: e.scalar_tensor_tensor(
                        out=cv[:, ci, dsl], in0=xb[:, ci, ssl], scalar=cw[:, ct, tap:tap + 1], in1=cv[:, ci, dsl],
                        op0=ALU.mult, op1=ALU.add),
                        reads=(xb_tl[ci], cv_tl[ci], cst), writes=(cv_tl[ci],))
                S.op("act", lambda e, ci=ci: e.activation(out=cvb[:, ci, :], in_=cv[:, ci, :], func=AF.Copy),
                     reads=(cv_tl[ci],), writes=(cvb_tl[ci],))
            units = [(0, 0), (0, 1), (1, 0), (1, 1)]
            ks = [(ui + j) % 2 for j in range(4)]
            ui += 4
            for j in (0, 2):
                stage1(nb, units[j][0], units[j][1], ks[j])
                stage1(nb, units[j + 1][0], units[j + 1][1], ks[j + 1])
                stage2(nb, units[j][0], units[j][1], ks[j])
                stage2(nb, units[j + 1][0], units[j + 1][1], ks[j + 1])
            for co in range(2):
                S.op("dve", lambda e, co=co: e.tensor_tensor(out=gt[:, co, :], in0=hs[:, co, :], in1=gt[:, co, :], op=ALU.mult),
                     reads=(hs_tl[co], gt_tl[co]), writes=(gt_tl[co],))
                S.dma("sp", lambda e, co=co, nb=nb: e.dma_start(out=yrec_d[nb * 256 + co * 128: nb * 256 + (co + 1) * 128, :], in_=gt[:, co, :]),
                      reads=(gt_tl[co],), writes=(junk,), defer=True)
        S.barrier()
        S.flush()
    return yrec_d


def phase_l1c(P, C, yrec_d, xres):
    L, NT, S = P.L, P.NT, P.S
    junk = Tl("dram")
    w_out, w_out_tl = P.wcast["w_out1"]
    with ExitStack() as es:
        xt = P.sb(es, "xt", [128, KC, TT]); xt_tl = [Tl("xt") for k in range(KC)]
        mix2 = [P.sb(es, "mix%d" % i, [128, KC, TT], BF16) for i in range(2)]; mix2_tl = [Tl("mix") for i in range(2)]
        NWO = 4
        wo = [P.sb(es, "wo%d" % i, [128, KC, 128], BF16) for i in range(NWO)]; wo_tl = [Tl("wo") for i in range(NWO)]
        xo = [P.sb(es, "xo%d" % i, [128, TT]) for i in range(2)]; xo_tl = [Tl("xo") for i in range(2)]
        ps = [P.ps(es, "ps%d" % i) for i in range(4)]; ps_tl = [Tl("ps") for i in range(4)]
        xview = xres.rearrange("(kc p) t -> p kc t", p=128)
        yview = yrec_d.rearrange("(kc p) t -> p kc t", p=128)
        pi = 0
        for tt in range(NT):
            t0 = tt * TT
            for kc in range(KC):
                S.dma("sp", lambda e, kc=kc, t0=t0: e.dma_start(out=xt[:, kc, :], in_=xview[:, kc, t0:t0 + TT]), writes=(xt_tl[kc],))
            mix, mix_tl = mix2[tt % 2], mix2_tl[tt % 2]
            if tt == 0:
                S.dma("sp", lambda e, t0=t0, mix=mix: e.dma_start(out=mix[:, :, :], in_=yview[:, :, t0:t0 + TT]), writes=(mix_tl,))
            if tt + 1 < NT:
                S.dma("sp", lambda e, t1=t0 + TT, m2=mix2[(tt + 1) % 2]: e.dma_start(out=m2[:, :, :], in_=yview[:, :, t1:t1 + TT]),
                      writes=(mix2_tl[(tt + 1) % 2],))
            for oc in range(KC):
                b = oc % NWO
                xb_ = oc % 2
                S.dma("pool", lambda e, b=b, oc=oc: e.dma_start(out=wo[b][:, :, :], in_=w_out[oc]), reads=(w_out_tl[oc],), writes=(wo_tl[b],))
                p_i = pi % 4
                pi += 1

                def mm2(e, b=b, p_i=p_i, mix=mix):
                    ins = None
                    for kc in range(KC):
                        ins = e.matmul(ps[p_i][:, :], wo[b][:, kc, :], mix[:, kc, :], start=(kc == 0), stop=(kc == KC - 1))
                    return ins
                S.op("pe", mm2, reads=(wo_tl[b], mix_tl), writes=(ps_tl[p_i],))
                S.op("dve", lambda e, xb_=xb_, p_i=p_i, oc=oc: e.tensor_tensor(out=xo[xb_][:, :], in0=ps[p_i][:, :], in1=xt[:, oc, :], op=ALU.add),
                     reads=(ps_tl[p_i], xt_tl[oc]), writes=(xo_tl[xb_],))
                S.dma("sp", lambda e, xb_=xb_, oc=oc, t0=t0: e.dma_start(out=xres[oc * 128:(oc + 1) * 128, t0:t0 + TT], in_=xo[xb_][:, :]),
                      reads=(xo_tl[xb_],), writes=(junk,), defer=True)
        S.barrier()
        S.flush()


def all_host_weights(inp):
    w = host_weights(inp)
    w.update(s5_host(inp))
    w.update(attn_host(inp))
    w.update(l0e_host(inp))
    w.update(ffn_host(inp))
    w.update(l1_host(inp))
    return w


def build(L, upto="all", debug=(), TF=512):
    P = Prog(L, debug)
    x_in = P.din("xT", [D, L])
    xres = P.dout("yT", [D, L])
    C = load_consts(P)
    stages = ["l0a", "s5", "attn", "l0e", "ffn0", "all"]
    U_d, q_d, v_d = phase_l0a(P, C, x_in)
    if upto != "l0a":
        Y_d = phase_s5(P, C, U_d)
    if upto not in ("l0a", "s5"):
        at_d = phase_attn(P, C, q_d, v_d)
    if upto not in ("l0a", "s5", "attn"):
        phase_l0e(P, C, Y_d, at_d, x_in, xres)
    if upto not in ("l0a", "s5", "attn", "l0e"):
        phase_ffn(P, C, 0, xres, TF)
    if upto not in ("l0a", "s5", "attn", "l0e", "ffn0"):
        gate_d, xb_d = phase_l1a(P, C, xres)
    if upto not in ("l0a", "s5", "attn", "l0e", "ffn0", "l1a"):
        yrec_d = phase_l1b(P, C, gate_d, xb_d)
    if upto not in ("l0a", "s5", "attn", "l0e", "ffn0", "l1a", "l1b"):
        phase_l1c(P, C, yrec_d, xres)
    if upto not in ("l0a", "s5", "attn", "l0e", "ffn0", "l1a", "l1b", "l1c"):
        phase_ffn(P, C, 1, xres, TF)
    P.es.close()
    return P


SEQ_LEN = 4096
N_CORES = 8
TF_FFN = 1024


def kernel(**inputs):
    inp = {k: np.asarray(v) for k, v in inputs.items()}
    xp = inp["x_prompt"]
    xs = inp["x_sample"]
    seqs = [xp[i] for i in range(xp.shape[0])] + [xs[i] for i in range(xs.shape[0])]
    nseq = len(seqs)
    L = seqs[0].shape[0]
    P = build(L, "all", TF=TF_FFN)
    shared = {}
    shared.update(host_consts(L))
    shared.update(all_host_weights(inp))
    names = set(P.dram.keys())
    shared = {k: v for k, v in shared.items() if k in names}
    in_maps = []
    for c in range(N_CORES):
        m = dict(shared)
        m["xT"] = np.ascontiguousarray(seqs[c % nseq].T)
        in_maps.append(m)
    res = run_bass_kernel_spmd(P.nc, in_maps, core_ids=list(range(N_CORES)))
    outs = [np.ascontiguousarray(np.asarray(res.results[c]["yT"]).T) for c in range(nseq)]
    y_prompt = np.stack(outs[:xp.shape[0]], axis=0).astype(np.float32)
    y_sample = np.stack(outs[xp.shape[0]:], axis=0).astype(np.float32)
    return (y_prompt, y_sample)
```

```python
import math
from contextlib import ExitStack

import numpy as np
import concourse.bass as bass
import concourse.mybir as mybir
from concourse.bass_utils import run_bass_kernel_spmd

F32 = mybir.dt.float32
BF16 = mybir.dt.bfloat16
AF = mybir.ActivationFunctionType
ALU = mybir.AluOpType

D = 2048
KC = D // 128
S5W = 1024
NG = 64
NQ = 8
NKV = 2
HD = 128
EVEN_IN = 2560
DFF = 5632
FC = DFF // 128
TT = 512
EPS = 1e-6
ROPE_THETA = 500000.0
NSLOT = 40


class Tl:
    __slots__ = ("name", "w", "r")

    def __init__(self, name):
        self.name = name
        self.w = None
        self.r = {}


class Sched:
    ENGS = ("pe", "act", "dve", "pool", "sp")

    def __init__(self, nc, es):
        self.nc = nc
        self.sem = {}
        self.cnt = {}
        for e in ("pe", "act", "dve", "pool"):
            self.sem[e] = es.enter_context(nc.semaphore("s_" + e))
            self.cnt[e] = 0
        self.slots = {}
        self.slot_i = {}
        for q in ("sp", "pool"):
            self.slots[q] = []
            for i in range(NSLOT):
                k = "d_%s%d" % (q, i)
                self.sem[k] = es.enter_context(nc.semaphore(k))
                self.cnt[k] = 0
                self.slots[q].append(k)
            self.slot_i[q] = 0
        self.known = {e: {} for e in self.ENGS}
        self.stream = {e: [] for e in self.ENGS}
        self.pending = {"sp": [], "pool": []}

    def _flush_pending(self, q):
        for ent in self.pending[q]:
            self.stream[q].append(ent[0])
        self.pending[q] = []

    def _emit(self, eng, fn, reads, writes, dma, defer=False):
        waits = {}

        def need(dep):
            if dep is None:
                return
            k, v = dep
            if waits.get(k, 0) < v:
                waits[k] = v

        for t in reads:
            need(t.w)
        for t in writes:
            need(t.w)
            for k, v in t.r.items():
                need((k, v))
        if dma:
            q = eng
            k = self.slots[q][self.slot_i[q] % NSLOT]
            self.slot_i[q] += 1
            if self.cnt[k] > 0:
                need((k, self.cnt[k]))
        for q2 in self.pending:
            if self.pending[q2]:
                for (_, tk, tv) in self.pending[q2]:
                    if waits.get(tk, 0) >= tv:
                        self._flush_pending(q2)
                        break
        kn = self.known[eng]
        wl = []
        for k_, v in waits.items():
            if kn.get(k_, 0) < v:
                if not defer:
                    kn[k_] = v
                wl.append((k_, v))
        if dma:
            self.cnt[k] += 16
            tok = (k, self.cnt[k])
            inc = 16
        else:
            self.cnt[eng] += 1
            tok = (eng, self.cnt[eng])
            inc = 1
        ent = (wl, fn, tok[0], inc)
        if defer:
            self.pending[eng].append((ent, tok[0], tok[1]))
            if len(self.pending[eng]) >= 32:
                self._flush_pending(eng)
        else:
            self.stream[eng].append(ent)
        for t in reads:
            if t.r.get(tok[0], 0) < tok[1]:
                t.r[tok[0]] = tok[1]
        for t in writes:
            t.w = tok
            t.r = {}
        return tok

    def op(self, eng, fn, reads=(), writes=()):
        return self._emit(eng, fn, reads, writes, False)

    def dma(self, q, fn, reads=(), writes=(), defer=False):
        return self._emit(q, fn, reads, writes, True, defer)

    def barrier(self):
        for q in self.pending:
            self._flush_pending(q)
        allv = dict(self.cnt)
        for eng in self.ENGS:
            kn = self.known[eng]
            wl = []
            for k, v in allv.items():
                if v > 0 and kn.get(k, 0) < v:
                    kn[k] = v
                    wl.append((k, v))
            if wl:
                self.stream[eng].append((wl, None, None, 0))

    def flush(self):
        nc = self.nc
        sem = self.sem
        streams = self.stream
        self.stream = {e: [] for e in self.ENGS}

        def replay(e, items):
            for wl, fn, tk, inc in items:
                for k, v in wl:
                    e.wait_ge(sem[k], v)
                if fn is not None:
                    ins = fn(e)
                    ins.then_inc(sem[tk], inc)

        with nc.Block() as block:
            @block.tensor
            def _(e):
                replay(e, streams["pe"])

            @block.scalar
            def _(e):
                replay(e, streams["act"])

            @block.vector
            def _(e):
                replay(e, streams["dve"])

            @block.gpsimd
            def _(e):
                replay(e, streams["pool"])

            @block.sync
            def _(e):
                replay(e, streams["sp"])


def kxm(W, m=128):
    K, M = W.shape
    return np.ascontiguousarray(
        W.reshape(K // 128, 128, M // m, m).transpose(2, 1, 0, 3))


def pvec(v):
    return np.ascontiguousarray(v.reshape(-1, 128).T)


class Prog:
    def __init__(self, L, debug=()):
        self.L = L
        self.NT = L // TT
        self.debug = set(debug)
        self.nc = bass.Bass("TRN2", target_bir_lowering=False)
        self.es = ExitStack()
        self.S = Sched(self.nc, self.es)
        self.dram = {}
        self.wcast = {}
        self._uid = 0

    def din(self, name, shape, dt=F32):
        t = self.nc.dram_tensor(name, list(shape), dt, kind="ExternalInput").ap()
        self.dram[name] = t
        return t

    def dout(self, name, shape, dt=F32):
        t = self.nc.dram_tensor(name, list(shape), dt, kind="ExternalOutput").ap()
        self.dram[name] = t
        return t

    def dscr(self, name, shape, dt):
        kind = "ExternalOutput" if name in self.debug else "Internal"
        t = self.nc.dram_tensor(name, list(shape), dt, kind=kind).ap()
        self.dram[name] = t
        return t

    def sb(self, es, name, shape, dt=F32):
        self._uid += 1
        return es.enter_context(self.nc.sbuf_tensor("%s_%d" % (name, self._uid), list(shape), dt))

    def ps(self, es, name, shape=(128, 512), dt=F32):
        self._uid += 1
        return es.enter_context(self.nc.psum_tensor("%s_%d" % (name, self._uid), list(shape), dt))


def _load(P, q, dst_ap, src_ap, tl, reads=()):
    return P.S.dma(q, lambda e: e.dma_start(out=dst_ap, in_=src_ap), reads=reads, writes=(tl,))


def _store(P, q, dst_ap, src_ap, tl, dtl):
    return P.S.dma(q, lambda e: e.dma_start(out=dst_ap, in_=src_ap), reads=(tl,), writes=(dtl,))


def emit_rmsnorm(P, C, xt, xt_tl, gain_ap, hT, hT_tl, sq, sq_tl, ps_stat, ps_stat_tl, rs, rs_tl):
    S = P.S
    S.op("act", lambda e: e.activation(out=sq[:, :, :], in_=xt[:, :, :], func=AF.Square),
         reads=(xt_tl,), writes=(sq_tl,))

    def mm(e):
        ins = None
        for kc in range(KC):
            ins = e.matmul(ps_stat[:, :], C["ones"][:, :], sq[:, kc, :], start=(kc == 0), stop=(kc == KC - 1))
        return ins
    S.op("pe", mm, reads=(sq_tl, C["ones_tl"]), writes=(ps_stat_tl,))
    S.op("act", lambda e: e.activation(out=rs[:, :], in_=ps_stat[:, :], func=AF.Ln, scale=1.0 / D,
                                       bias=C["eps"][:, 0:1]),
         reads=(ps_stat_tl, C["ones_tl"]), writes=(rs_tl,))
    S.op("act", lambda e: e.activation(out=rs[:, :], in_=rs[:, :], func=AF.Exp, scale=-0.5),
         reads=(rs_tl,), writes=(rs_tl,))
    for kc in range(KC):
        S.op("dve", lambda e, kc=kc: e.scalar_tensor_tensor(
            out=hT[:, kc, :], in0=xt[:, kc, :], scalar=gain_ap[:, kc:kc + 1], in1=rs[:, :],
            op0=ALU.mult, op1=ALU.mult),
            reads=(xt_tl, rs_tl, C["ones_tl"]), writes=(hT_tl[kc],))


def precast(P, specs):
    S = P.S
    for name, shape in specs:
        src = P.din(name, shape)
        dst = P.dscr(name + "_b", shape, BF16)
        tls = []
        if len(shape) == 4:
            for oc in range(shape[0]):
                tl = Tl(name + "_b")
                S.dma("pool", lambda e, oc=oc, src=src, dst=dst: e.dma_start(out=dst[oc], in_=src[oc]), writes=(tl,))
                tls.append(tl)
        else:
            h = shape[1] // 2
            for (k0, k1) in ((0, h), (h, shape[1])):
                tl = Tl(name + "_b")
                S.dma("pool", lambda e, k0=k0, k1=k1, src=src, dst=dst: e.dma_start(out=dst[:, k0:k1, :], in_=src[:, k0:k1, :]),
                      writes=(tl,))
                tls.append(tl)
        P.wcast[name] = (dst, tls)


def load_consts(P):
    es = P.es
    S = P.S
    C = {}
    tl = Tl("consts")
    C["ones_tl"] = tl

    def ld(name, shape, dt=F32):
        src = P.din(name, shape, dt)
        t = P.sb(es, "c_" + name, shape, dt)
        nd = len(shape)
        if nd == 2:
            S.dma("sp", lambda e: e.dma_start(out=t[:, :], in_=src), writes=(tl,))
        elif nd == 3:
            S.dma("sp", lambda e: e.dma_start(out=t[:, :, :], in_=src), writes=(tl,))
        else:
            S.dma("sp", lambda e: e.dma_start(out=t[:, :, :, :], in_=src), writes=(tl,))
        C[name] = t
        return t

    ld("ones", [128, 128])
    ld("eps", [128, 1])
    ld("gmix", [128, 2, KC])
    ld("gffn", [128, 2, KC])
    ld("ident", [128, 128])
    onesb = P.sb(es, "c_onesb", [128, 128], BF16)
    S.op("act", lambda e: e.activation(out=onesb[:, :], in_=C["ones"][:, :], func=AF.Copy),
         reads=(tl,), writes=(tl,))
    C["onesb"] = onesb
    identb = P.sb(es, "c_identb", [128, 128], BF16)
    S.op("act", lambda e: e.activation(out=identb[:, :], in_=C["ident"][:, :], func=AF.Copy),
         reads=(tl,), writes=(tl,))
    C["identb"] = identb
    return C


def phase_l0a(P, C, x_src):
    L, NT, S = P.L, P.NT, P.S
    NCH = L // 8
    precast(P, [("w_in0", [18, 128, KC, 128]), ("w_v0", [128, KC, 256])])
    w_in, w_in_tl = P.wcast["w_in0"]
    w_v, w_v_tl = P.wcast["w_v0"]
    ropeC = P.din("ropeC", [128, L])
    ropeS = P.din("ropeS", [128, L])
    ropeP = P.din("ropeP", [128, 128])
    gqk = P.din("gqk", [128, 2])
    U_d = P.dscr("U_d", [8, S5W, NCH], BF16)
    q_d = P.dscr("q_d", [NQ + NKV, 128, L], BF16)
    v_d = P.dscr("v_d", [L, 256], BF16)
    junk = Tl("dram")
    with ExitStack() as es:
        xt = P.sb(es, "xt", [128, KC, TT]); xt_tl = Tl("xt")
        sq = P.sb(es, "sq", [128, KC, TT]); sq_tl = Tl("sq")
        hT2 = [P.sb(es, "hT%d" % i, [128, KC, TT], BF16) for i in range(2)]
        hT2_tl = [[Tl("hT%d" % k) for k in range(KC)] for i in range(2)]
        rs = P.sb(es, "rs", [128, TT]); rs_tl = Tl("rs")
        tC = P.sb(es, "ropeC", [128, L]); tS = P.sb(es, "ropeS", [128, L]); tP = P.sb(es, "ropeP", [128, 128])
        tg = P.sb(es, "gqk", [128, 2]); cst_tl = Tl("l0a_consts")
        wv = P.sb(es, "wv", [128, KC, 256], BF16); wv_tl = Tl("wv")
        NWB = 4
        NPO = 4
        NQS = 3
        wb = [P.sb(es, "wb%d" % i, [128, KC, 128], BF16) for i in range(NWB)]
        wb_tl = [Tl("wb%d" % i) for i in range(NWB)]
        ude = [P.sb(es, "ude%d" % i, [128, 8, TT // 8], BF16) for i in range(2)]
        ude_tl = [Tl("ude%d" % i) for i in range(2)]
        sqq_ = [P.sb(es, "sqq%d" % i, [128, TT]) for i in range(NQS)]; sqq_tl_ = [Tl("sqq") for i in range(NQS)]
        rq_ = [P.sb(es, "rq%d" % i, [128, TT]) for i in range(NQS)]; rq_tl_ = [Tl("rq") for i in range(NQS)]
        qn_ = [P.sb(es, "qn%d" % i, [128, TT]) for i in range(NQS)]; qn_tl_ = [Tl("qn") for i in range(NQS)]
        t1_ = [P.sb(es, "t1%d" % i, [128, TT]) for i in range(NQS)]; t1_tl_ = [Tl("t1") for i in range(NQS)]
        t2_ = [P.sb(es, "t2%d" % i, [128, TT]) for i in range(NQS)]; t2_tl_ = [Tl("t2") for i in range(NQS)]
        qo = [P.sb(es, "qo%d" % i, [128, TT], BF16) for i in range(2)]
        qo_tl = [Tl("qo%d" % i) for i in range(2)]
        vo = [P.sb(es, "vo%d" % i, [128, 256], BF16) for i in range(2)]
        vo_tl = [Tl("vo%d" % i) for i in range(2)]
        ps_stat = P.ps(es, "ps_stat"); ps_stat_tl = Tl("ps_stat")
        ps_o = [P.ps(es, "ps_o%d" % i) for i in range(NPO)]; ps_o_tl = [Tl("ps_o%d" % i) for i in range(NPO)]
        ps_s = P.ps(es, "ps_s"); ps_s_tl = Tl("ps_s")
        ps_r = P.ps(es, "ps_r"); ps_r_tl = Tl("ps_r")
        ps_v = P.ps(es, "ps_v"); ps_v_tl = Tl("ps_v")

        for (dst, src) in ((tC, ropeC), (tS, ropeS), (tP, ropeP), (tg, gqk)):
            S.dma("sp", lambda e, dst=dst, src=src: e.dma_start(out=dst[:, :], in_=src), writes=(cst_tl,))
        S.dma("pool", lambda e: e.dma_start(out=wv[:, :, :], in_=w_v), reads=w_v_tl, writes=(wv_tl,))

        xview = x_src.rearrange("(kc p) t -> p kc t", p=128)
        wi = 0

        def norm(tt_):
            t0_ = tt_ * TT
            S.dma("sp", lambda e, t0_=t0_: e.dma_start(out=xt[:, :, :], in_=xview[:, :, t0_:t0_ + TT]),
                  writes=(xt_tl,))
            emit_rmsnorm(P, C, xt, xt_tl, C["gmix"][:, 0, :], hT2[tt_ % 2], hT2_tl[tt_ % 2], sq, sq_tl,
                         ps_stat, ps_stat_tl, rs, rs_tl)

        norm(0)
        pendB = []
        pendC = []
        for tt in range(NT):
            t0 = tt * TT
            hT, hT_tl = hT2[tt % 2], hT2_tl[tt % 2]
            for oc in range(18):
                if oc == 6 and tt + 1 < NT:
                    norm(tt + 1)
                wbi = wi % NWB
                b = wi % NPO
                wi += 1
                S.dma("pool", lambda e, wbi=wbi, oc=oc: e.dma_start(out=wb[wbi][:, :, :], in_=w_in[oc]),
                      reads=(w_in_tl[oc],), writes=(wb_tl[wbi],))

                def mm(e, b=b, wbi=wbi, hT=hT):
                    ins = None
                    for kc in range(KC):
                        ins = e.matmul(ps_o[b][:, :], wb[wbi][:, kc, :], hT[:, kc, :],
                                       start=(kc == 0), stop=(kc == KC - 1))
                    return ins
                S.op("pe", mm, reads=[wb_tl[wbi]] + hT_tl, writes=(ps_o_tl[b],))
                if oc < 8:
                    ub = oc % 2
                    S.op("act", lambda e, b=b, ub=ub: e.activation(
                        out=ude[ub][:, :, :], in_=ps_o[b][:, :].rearrange("p (c j) -> p j c", j=8),
                        func=AF.Copy), reads=(ps_o_tl[b],), writes=(ude_tl[ub],))
                    c0 = tt * (TT // 8)
                    for j0_ in (0, 4):
                        dst = U_d[j0_:j0_ + 4, oc * 128:(oc + 1) * 128, c0:c0 + TT // 8].rearrange("j p c -> p j c")
                        S.dma("sp", lambda e, dst=dst, ub=ub, j0_=j0_: e.dma_start(out=dst, in_=ude[ub][:, j0_:j0_ + 4, :]),
                              reads=(ude_tl[ub],), writes=(junk,), defer=True)
                else:
                    hq = oc - 8
                    gi = 0 if hq < 8 else 1
                    ob = hq % 2
                    qs = hq % NQS
                    sqq, sqq_tl = sqq_[qs], sqq_tl_[qs]
                    rq, rq_tl = rq_[qs], rq_tl_[qs]
                    qn, qn_tl = qn_[qs], qn_tl_[qs]
                    t1, t1_tl = t1_[qs], t1_tl_[qs]
                    t2, t2_tl = t2_[qs], t2_tl_[qs]
                    S.op("act", lambda e, b=b, sqq=sqq: e.activation(out=sqq[:, :], in_=ps_o[b][:, :], func=AF.Square),
                         reads=(ps_o_tl[b],), writes=(sqq_tl,))

                    def stageB(b=b, gi=gi, sqq=sqq, sqq_tl=sqq_tl, rq=rq, rq_tl=rq_tl, qn=qn, qn_tl=qn_tl):
                        S.op("pe", lambda e: e.matmul(ps_s[:, :], C["ones"][:, :], sqq[:, :], start=True, stop=True),
                             reads=(sqq_tl, C["ones_tl"]), writes=(ps_s_tl,))
                        S.op("act", lambda e: e.activation(out=rq[:, :], in_=ps_s[:, :], func=AF.Ln,
                                                           scale=1.0 / HD, bias=C["eps"][:, 0:1]),
                             reads=(ps_s_tl, C["ones_tl"]), writes=(rq_tl,))
                        S.op("act", lambda e: e.activation(out=rq[:, :], in_=rq[:, :], func=AF.Exp, scale=-0.5),
                             reads=(rq_tl,), writes=(rq_tl,))
                        S.op("dve", lambda e: e.scalar_tensor_tensor(
                            out=qn[:, :], in0=ps_o[b][:, :], scalar=tg[:, gi:gi + 1], in1=rq[:, :],
                            op0=ALU.mult, op1=ALU.mult),
                            reads=(ps_o_tl[b], rq_tl, cst_tl), writes=(qn_tl,))

                    def stageC(hq=hq, ob=ob, t0=t0, qn=qn, qn_tl=qn_tl, t1=t1, t1_tl=t1_tl, t2=t2, t2_tl=t2_tl):
                        S.op("pe", lambda e: e.matmul(ps_r[:, :], tP[:, :], qn[:, :], start=True, stop=True),
                             reads=(qn_tl, cst_tl), writes=(ps_r_tl,))
                        S.op("dve", lambda e: e.tensor_tensor(out=t1[:, :], in0=qn[:, :], in1=tC[:, t0:t0 + TT], op=ALU.mult),
                             reads=(qn_tl, cst_tl), writes=(t1_tl,))
                        S.op("dve", lambda e: e.tensor_tensor(out=t2[:, :], in0=ps_r[:, :], in1=tS[:, t0:t0 + TT], op=ALU.mult),
                             reads=(ps_r_tl, cst_tl), writes=(t2_tl,))
                        S.op("dve", lambda e: e.tensor_tensor(out=qo[ob][:, :], in0=t1[:, :], in1=t2[:, :], op=ALU.add),
                             reads=(t1_tl, t2_tl), writes=(qo_tl[ob],))
                        S.dma("sp", lambda e: e.dma_start(out=q_d[hq][:, t0:t0 + TT], in_=qo[ob][:, :]),
                              reads=(qo_tl[ob],), writes=(junk,), defer=True)
                    if len(pendC) > 0:
                        pendC.pop(0)()
                    if len(pendB) > 0:
                        fB, fC = pendB.pop(0)
                        fB()
                        pendC.append(fC)
                    pendB.append((stageB, stageC))
            while pendB or pendC:
                if pendC:
                    pendC.pop(0)()
                if pendB:
                    fB, fC = pendB.pop(0)
                    fB()
                    pendC.append(fC)
            for tb in range(TT // 128):
                vb = tb % 2

                def mmv(e, tb=tb, hT=hT):
                    ins = None
                    for kc in range(KC):
                        ins = e.matmul(ps_v[:, 0:256], hT[:, kc, tb * 128:(tb + 1) * 128], wv[:, kc, :],
                                       start=(kc == 0), stop=(kc == KC - 1))
                    return ins
                S.op("pe", mmv, reads=[wv_tl] + hT_tl, writes=(ps_v_tl,))
                S.op("act", lambda e, vb=vb: e.activation(out=vo[vb][:, :], in_=ps_v[:, 0:256], func=AF.Copy),
                     reads=(ps_v_tl,), writes=(vo_tl[vb],))
                r0 = t0 + tb * 128
                S.dma("sp", lambda e, vb=vb, r0=r0: e.dma_start(out=v_d[r0:r0 + 128, :], in_=vo[vb][:, :]),
                      reads=(vo_tl[vb],), writes=(junk,), defer=True)
        S.barrier()
        S.flush()
    return U_d, q_d, v_d


def host_consts(L):
    c = {}
    c["ones"] = np.ones((128, 128), np.float32)
    c["eps"] = np.full((128, 1), EPS, np.float32)
    c["ident"] = np.eye(128, dtype=np.float32)
    half = 16
    inv = (np.float32(ROPE_THETA) ** (-np.arange(half, dtype=np.float32) / np.float32(half))).astype(np.float32)
    ang = (np.arange(L, dtype=np.float32)[:, None] * inv[None, :]).astype(np.float32)
    cs = np.cos(ang).astype(np.float32).T
    sn = np.sin(ang).astype(np.float32).T
    rc = np.ones((128, L), np.float32)
    rsn = np.zeros((128, L), np.float32)
    rc[0:16] = cs; rc[16:32] = cs
    rsn[0:16] = sn; rsn[16:32] = sn
    c["ropeC"] = rc
    c["ropeS"] = rsn
    pt = np.zeros((128, 128), np.float32)
    for m in range(16):
        pt[m + 16, m] = -1.0
        pt[m, m + 16] = 1.0
    c["ropeP"] = pt
    return c


def host_weights(inp):
    w = {}
    w["gmix"] = np.ascontiguousarray(inp["norm_mix"].reshape(2, KC, 128).transpose(2, 0, 1))
    w["gffn"] = np.ascontiguousarray(inp["norm_ffn"].reshape(2, KC, 128).transpose(2, 0, 1))
    Win = inp["ev_w_in"][0]
    w["w_in0"] = kxm(Win[:, :2304])
    w["w_v0"] = np.ascontiguousarray(Win[:, 2304:2560].reshape(KC, 128, 256).transpose(1, 0, 2))
    w["gqk"] = np.ascontiguousarray(np.stack([inp["attn_q_norm"][0], inp["attn_k_norm"][0]], axis=1))
    return w


EXPS = [float(e) for e in range(-8, 9)] + [float(16 * 2 ** m) for m in range(8)]
TWO_PI = 2.0 * math.pi
CW1 = 6.28125
CW2 = TWO_PI - CW1
PI_F = 3.1415927410125732


def s5_host(inp):
    w = {}
    lre = inp["s5_lam_re"][0]; lim = inp["s5_lam_im"][0]; ldt = inp["s5_log_dt"][0]

    def st(a):
        return np.ascontiguousarray(a.reshape(2, 32, 2, 64).transpose(2, 3, 0, 1).reshape(128, 2, 32))
    w["s5_lre"] = st(lre)
    w["s5_lim"] = st(lim)
    w["s5_ldt"] = st(np.broadcast_to(ldt[:, :, None], (2, 64, 64)))

    def sb(a):
        return np.ascontiguousarray(a.reshape(2, 32, 2, 64, 16).transpose(2, 3, 0, 1, 4).reshape(128, 2, 32, 16))

    def sc(a):
        return np.ascontiguousarray(a.reshape(2, 32, 2, 16, 64).transpose(2, 4, 0, 1, 3).reshape(128, 2, 32, 16))
    w["s5_bre"] = sb(inp["s5_b_re"][0]); w["s5_bim"] = sb(inp["s5_b_im"][0])
    w["s5_cre"] = sc(inp["s5_c_re"][0]); w["s5_cim"] = sc(inp["s5_c_im"][0])
    d = inp["s5_d"][0].reshape(64, 16)
    w["s5_drep"] = np.ascontiguousarray(np.broadcast_to(d.T[None, :, :], (8, 16, 64)).reshape(128, 64))
    w["s5_etab"] = np.ascontiguousarray(np.broadcast_to(np.array(EXPS, np.float32)[None, :], (128, 25)))
    jj = np.repeat(np.arange(8), 16)
    m2 = np.zeros((128, 2, 128), np.float32)
    m2[:, 0, :] = (jj[None, :] >= jj[:, None])
    m2[:, 1, :] = (jj[:, None] >= jj[None, :])
    w["s5_m2"] = m2
    return w


def phase_s5(P, C, U_d):
    L, S = P.L, P.S
    NCH = L // 8
    NST = int(math.log2(NCH))
    junk = Tl("dram")
    Y_d = P.dscr("Y_d", [8, S5W, NCH], F32)
    src = {}
    for nm, shp in (("s5_lre", [128, 2, 32]), ("s5_lim", [128, 2, 32]), ("s5_ldt", [128, 2, 32]),
                    ("s5_bre", [128, 2, 32, 16]), ("s5_bim", [128, 2, 32, 16]),
                    ("s5_cre", [128, 2, 32, 16]), ("s5_cim", [128, 2, 32, 16]),
                    ("s5_drep", [128, 64]), ("s5_etab", [128, 25]), ("s5_m2", [128, 2, 128])):
        src[nm] = P.din(nm, shp)
    precast(P, [("w_glu", [8, 128, 8, 128]), ("w_out0", [KC, 128, KC, 128]),
                ("w_in1", [32, 128, KC, 128]), ("w_out1", [KC, 128, KC, 128])])
    with ExitStack() as es:
        MB = P.sb(es, "MB", [128, 2, 32, 2, 128], BF16); MB_tl = Tl("MB")
        MCb = P.sb(es, "MCb", [128, 2, 32, 2, 128], BF16); MCb_tl = Tl("MCb")
        Toep = P.sb(es, "Toep", [128, NG, 128], BF16); Toep_tl = Tl("Toep")
        kar = P.sb(es, "kar", [128, 9, 64]); kai = P.sb(es, "kai", [128, 9, 64]); kan = P.sb(es, "kan", [128, 9, 64])
        ks_tl = Tl("kstab")
        with ExitStack() as es2:
            pt = Tl("prep")
            t = {}
            for nm, shp in (("s5_lre", [128, 64]), ("s5_lim", [128, 64]), ("s5_ldt", [128, 64]),
                            ("s5_drep", [128, 64]), ("s5_etab", [128, 25])):
                t[nm] = P.sb(es2, nm, shp)
                sa = src[nm] if len(src[nm].shape) == 2 else src[nm].rearrange("p z q -> p (z q)")
                S.dma("sp", lambda e, d=t[nm], s=sa: e.dma_start(out=d[:, :], in_=s), writes=(pt,))
            for nm in ("s5_bre", "s5_bim", "s5_cre", "s5_cim"):
                t[nm] = P.sb(es2, nm, [128, 2, 32, 16])
                S.dma("sp", lambda e, d=t[nm], s=src[nm]: e.dma_start(out=d[:, :, :, :], in_=s), writes=(pt,))
            m2 = P.sb(es2, "m2", [128, 2, 128])
            S.dma("sp", lambda e: e.dma_start(out=m2[:, :, :], in_=src["s5_m2"]), writes=(pt,))
            NE = 25

            def T2(nm, n=64):
                return P.sb(es2, nm, [128, n])

            def T3(nm, dt=F32):
                return P.sb(es2, nm, [128, NE, 64], dt)
            dt_ = T2("dt"); lr = T2("lr"); xm = T2("xm"); an = T2("an")
            XE = T3("XE"); AE = T3("AE"); mag = XE; kf = T3("kf"); ki = T3("ki", mybir.dt.int32)
            rr = T3("rr"); rc = T3("rc"); msk = kf; wre = rc; wim = rr
            D1 = lambda f, r=(pt,), w=(pt,): S.op("dve", f, reads=r, writes=w)
            A1 = lambda f, r=(pt,), w=(pt,): S.op("act", f, reads=r, writes=w)
            A1(lambda e: e.activation(out=dt_[:, :], in_=t["s5_ldt"][:, :], func=AF.Exp))
            D1(lambda e: e.tensor_scalar(out=lr[:, :], in0=t["s5_lre"][:, :], scalar1=-1e-4, scalar2=None, op0=ALU.min))
            D1(lambda e: e.tensor_tensor(out=xm[:, :], in0=lr[:, :], in1=dt_[:, :], op=ALU.mult))
            D1(lambda e: e.tensor_tensor(out=an[:, :], in0=t["s5_lim"][:, :], in1=dt_[:, :], op=ALU.mult))
            eb = t["s5_etab"][:, :].unsqueeze(2).broadcast_to([128, NE, 64])
            D1(lambda e: e.tensor_tensor(out=XE[:, :, :], in0=xm[:, :].unsqueeze(1).broadcast_to([128, NE, 64]),
                                         in1=eb, op=ALU.mult))
            D1(lambda e: e.tensor_tensor(out=AE[:, :, :], in0=an[:, :].unsqueeze(1).broadcast_to([128, NE, 64]),
                                         in1=eb, op=ALU.mult))
            A1(lambda e: e.activation(out=mag[:, :, :], in_=XE[:, :, :], func=AF.Exp))
            D1(lambda e: e.tensor_scalar(out=kf[:, :, :], in0=AE[:, :, :], scalar1=1.0 / TWO_PI, scalar2=None, op0=ALU.mult))
            D1(lambda e: e.tensor_copy(out=ki[:, :, :], in_=kf[:, :, :]))
            D1(lambda e: e.tensor_copy(out=kf[:, :, :], in_=ki[:, :, :]))
            D1(lambda e: e.scalar_tensor_tensor(out=rr[:, :, :], in0=kf[:, :, :], scalar=-CW1, in1=AE[:, :, :],
                                                op0=ALU.mult, op1=ALU.add))
            D1(lambda e: e.scalar_tensor_tensor(out=rr[:, :, :], in0=kf[:, :, :], scalar=-CW2, in1=rr[:, :, :],
                                                op0=ALU.mult, op1=ALU.add))

            def wrap(x):
                D1(lambda e: e.tensor_scalar(out=msk[:, :, :], in0=x[:, :, :], scalar1=PI_F, scalar2=None, op0=ALU.is_gt))
                D1(lambda e: e.scalar_tensor_tensor(out=x[:, :, :], in0=msk[:, :, :], scalar=-TWO_PI, in1=x[:, :, :],
                                                    op0=ALU.mult, op1=ALU.add))
                D1(lambda e: e.tensor_scalar(out=msk[:, :, :], in0=x[:, :, :], scalar1=-PI_F, scalar2=None, op0=ALU.is_lt))
                D1(lambda e: e.scalar_tensor_tensor(out=x[:, :, :], in0=msk[:, :, :], scalar=TWO_PI, in1=x[:, :, :],
                                                    op0=ALU.mult, op1=ALU.add))
            wrap(rr)
            D1(lambda e: e.tensor_scalar(out=rc[:, :, :], in0=rr[:, :, :], scalar1=math.pi / 2, scalar2=None, op0=ALU.add))
            wrap(rc)
            A1(lambda e: e.activation(out=rr[:, :, :], in_=rr[:, :, :], func=AF.Sin))
            A1(lambda e: e.activation(out=rc[:, :, :], in_=rc[:, :, :], func=AF.Sin))
            D1(lambda e: e.tensor_tensor(out=wre[:, :, :], in0=mag[:, :, :], in1=rc[:, :, :], op=ALU.mult))
            D1(lambda e: e.tensor_tensor(out=wim[:, :, :], in0=mag[:, :, :], in1=rr[:, :, :], op=ALU.mult))
            D1(lambda e: e.tensor_copy(out=kar[:, :, :], in_=wre[:, 16:25, :]), w=(pt, ks_tl))
            D1(lambda e: e.tensor_copy(out=kai[:, :, :], in_=wim[:, 16:25, :]), w=(pt, ks_tl))
            D1(lambda e: e.tensor_scalar(out=kan[:, :, :], in0=wim[:, 16:25, :], scalar1=-1.0, scalar2=None, op0=ALU.mult),
               w=(pt, ks_tl))
            nr = T2("nr"); den = T2("den"); fa = T2("fa"); fb = T2("fb"); fre = T2("fre"); fim = T2("fim")
            li = t["s5_lim"]
            D1(lambda e: e.tensor_scalar(out=nr[:, :], in0=wre[:, 9, :], scalar1=-1.0, scalar2=None, op0=ALU.add))
            D1(lambda e: e.tensor_tensor(out=den[:, :], in0=lr[:, :], in1=lr[:, :], op=ALU.mult))
            D1(lambda e: e.tensor_tensor(out=fa[:, :], in0=li[:, :], in1=li[:, :], op=ALU.mult))
            D1(lambda e: e.tensor_tensor(out=den[:, :], in0=den[:, :], in1=fa[:, :], op=ALU.add))
            D1(lambda e: e.reciprocal(out=den[:, :], in_=den[:, :]))
            D1(lambda e: e.tensor_tensor(out=fa[:, :], in0=nr[:, :], in1=lr[:, :], op=ALU.mult))
            D1(lambda e: e.tensor_tensor(out=fb[:, :], in0=wim[:, 9, :], in1=li[:, :], op=ALU.mult))
            D1(lambda e: e.tensor_tensor(out=fa[:, :], in0=fa[:, :], in1=fb[:, :], op=ALU.add))
            D1(lambda e: e.tensor_tensor(out=fre[:, :], in0=fa[:, :], in1=den[:, :], op=ALU.mult))
            D1(lambda e: e.tensor_tensor(out=fa[:, :], in0=wim[:, 9, :], in1=lr[:, :], op=ALU.mult))
            D1(lambda e: e.tensor_tensor(out=fb[:, :], in0=nr[:, :], in1=li[:, :], op=ALU.mult))
            D1(lambda e: e.tensor_tensor(out=fa[:, :], in0=fa[:, :], in1=fb[:, :], op=ALU.subtract))
            D1(lambda e: e.tensor_tensor(out=fim[:, :], in0=fa[:, :], in1=den[:, :], op=ALU.mult))
            wfre = P.sb(es2, "wfre", [128, 16, 64]); wfim = P.sb(es2, "wfim", [128, 16, 64])
            tA = AE[:, 0:16, :]; tB = kf[:, 0:16, :]
            fre_b = fre[:, :].unsqueeze(1).broadcast_to([128, 16, 64])
            fim_b = fim[:, :].unsqueeze(1).broadcast_to([128, 16, 64])
            D1(lambda e: e.tensor_tensor(out=tA[:, :, :], in0=wre[:, 0:16, :], in1=fre_b, op=ALU.mult))
            D1(lambda e: e.tensor_tensor(out=tB[:, :, :], in0=wim[:, 0:16, :], in1=fim_b, op=ALU.mult))
            D1(lambda e: e.tensor_tensor(out=wfre[:, :, :], in0=tA[:, :, :], in1=tB[:, :, :], op=ALU.subtract))
            D1(lambda e: e.tensor_tensor(out=tA[:, :, :], in0=wre[:, 0:16, :], in1=fim_b, op=ALU.mult))
            D1(lambda e: e.tensor_tensor(out=tB[:, :, :], in0=wim[:, 0:16, :], in1=fre_b, op=ALU.mult))
            D1(lambda e: e.tensor_tensor(out=wfim[:, :, :], in0=tA[:, :, :], in1=tB[:, :, :], op=ALU.add))

            PCH = 4
            m1 = P.sb(es2, "m1", [128, PCH, 8, 16]); m2t = P.sb(es2, "m2t", [128, PCH, 8, 16])
            m_tl = [Tl("m1"), Tl("m2t")]
            MBt = P.sb(es2, "MBt", [128, 2, PCH, 2, 128], BF16); MBt_tl = Tl("MBt")
            MBn = P.sb(es2, "MBn", [128, 2, PCH, 2, 128]); MBn_tl = Tl("MBn")
            MC = P.sb(es2, "MC", [128, 2, PCH, 2, 128]); MC_tl = Tl("MC")
            tt2 = P.sb(es2, "tt2", [128, 2, 128]); tt2_tl = Tl("tt2")
            tt3 = P.sb(es2, "tt3", [128, 128]); tt3_tl = Tl("tt3")
            ps_tr = P.ps(es2, "ps_tr", (128, 1024), BF16); ps_tr_tl = Tl("ps_tr")
            ps_T = P.ps(es2, "ps_T"); ps_T_tl = Tl("ps_T")

            def wview(tab, z, pc, start, step):
                if step > 0:
                    v = tab[:, start:start + 8, z * 32 + pc * PCH: z * 32 + (pc + 1) * PCH]
                else:
                    stop = start - 8
                    v = tab[:, start:(stop if stop >= 0 else None):-1, z * 32 + pc * PCH: z * 32 + (pc + 1) * PCH]
                return v.rearrange("p j q -> p q j").unsqueeze(3).broadcast_to([128, PCH, 8, 16])

            def bview(tab, z, pc):
                return tab[:, z, pc * PCH:(pc + 1) * PCH, :].unsqueeze(2).broadcast_to([128, PCH, 8, 16])

            def oview(tab, z, comp):
                return tab[:, z, :, comp, :].rearrange("p q (j n) -> p q j n", n=16)

            def cprod(wr_v, wi_v, xr_v, xi_v, out_re, out_im, out_tl, neg_im):
                S.op("dve", lambda e: e.tensor_tensor(out=m1[:, :, :, :], in0=wr_v, in1=xr_v, op=ALU.mult),
                     reads=(pt,), writes=(m_tl[0],))
                S.op("pool", lambda e: e.tensor_tensor(out=m2t[:, :, :, :], in0=wi_v, in1=xi_v, op=ALU.mult),
                     reads=(pt,), writes=(m_tl[1],))
                S.op("dve", lambda e: e.tensor_tensor(out=out_re, in0=m1[:, :, :, :], in1=m2t[:, :, :, :], op=ALU.subtract),
                     reads=m_tl, writes=(out_tl,))
                S.op("dve", lambda e: e.tensor_tensor(out=m1[:, :, :, :], in0=wr_v, in1=xi_v, op=ALU.mult),
                     reads=(pt,), writes=(m_tl[0],))
                S.op("pool", lambda e: e.tensor_tensor(out=m2t[:, :, :, :], in0=wi_v, in1=xr_v, op=ALU.mult),
                     reads=(pt,), writes=(m_tl[1],))
                if neg_im:
                    S.op("dve", lambda e: e.scalar_tensor_tensor(out=out_im, in0=m1[:, :, :, :], scalar=-1.0,
                                                                 in1=m2t[:, :, :, :], op0=ALU.mult, op1=ALU.subtract),
                         reads=m_tl, writes=(out_tl,))
                else:
                    S.op("dve", lambda e: e.tensor_tensor(out=out_im, in0=m1[:, :, :, :], in1=m2t[:, :, :, :], op=ALU.add),
                         reads=m_tl, writes=(out_tl,))

            for pc in range(32 // PCH):
                for z in range(2):
                    if z == 0:
                        sB, stB = 15, -1
                        sN, stN = 7, -1
                        sC, stC = 9, 1
                    else:
                        sB, stB = 8, 1
                        sN, stN = 0, 1
                        sC, stC = 16, -1
                    bre_v = bview(t["s5_bre"], z, pc); bim_v = bview(t["s5_bim"], z, pc)
                    cre_v = bview(t["s5_cre"], z, pc); cim_v = bview(t["s5_cim"], z, pc)
                    cprod(wview(wfre, z, pc, sB, stB), wview(wfim, z, pc, sB, stB), bre_v, bim_v,
                          oview(MBt, z, 0), oview(MBt, z, 1), MBt_tl, False)
                    cprod(wview(wfre, z, pc, sN, stN), wview(wfim, z, pc, sN, stN), bre_v, bim_v,
                          oview(MBn, z, 0), oview(MBn, z, 1), MBn_tl, False)
                    cprod(wview(wre, z, pc, sC, stC), wview(wim, z, pc, sC, stC), cre_v, cim_v,
                          oview(MC, z, 0), oview(MC, z, 1), MC_tl, True)
                    S.op("act", lambda e, z=z, pc=pc: e.activation(
                        out=MCb[:, z, pc * PCH:(pc + 1) * PCH, :, :], in_=MC[:, z, :, :, :], func=AF.Copy),
                        reads=(MC_tl,), writes=(MCb_tl,))

                    def trs(e, z=z):
                        ins = None
                        for q in range(PCH):
                            for comp in range(2):
                                k = q * 2 + comp
                                ins = e.transpose(out=ps_tr[:, k * 128:(k + 1) * 128], in_=MBt[:, z, q, comp, :],
                                                  identity=C["identb"][:, :])
                        return ins
                    S.op("pe", trs, reads=(MBt_tl, C["ones_tl"]), writes=(ps_tr_tl,))
                    S.op("act", lambda e, z=z, pc=pc: e.activation(
                        out=MB[:, z, pc * PCH:(pc + 1) * PCH, :, :].rearrange("p q c m -> p (q c m)"),
                        in_=ps_tr[:, 0:PCH * 256], func=AF.Copy),
                        reads=(ps_tr_tl,), writes=(MB_tl,))
                for q in range(PCH):
                    for gpar in range(2):
                        g = 2 * (pc * PCH + q) + gpar
                        rows = slice(gpar * 64, (gpar + 1) * 64)

                        def mmT(e, q=q, rows=rows):
                            ins = None
                            for z in range(2):
                                for comp in range(2):
                                    ins = e.matmul(ps_T[:, z * 128:(z + 1) * 128], MBn[rows, z, q, comp, :],
                                                   MC[rows, z, q, comp, :], start=(comp == 0), stop=(comp == 1))
                            return ins
                        S.op("pe", mmT, reads=(MBn_tl, MC_tl), writes=(ps_T_tl,))
                        S.op("dve", lambda e: e.tensor_tensor(out=tt2[:, :, :].rearrange("p z m -> p (z m)"),
                                                              in0=ps_T[:, 0:256],
                                                              in1=m2[:, :, :].rearrange("p z m -> p (z m)"), op=ALU.mult),
                             reads=(ps_T_tl, pt), writes=(tt2_tl,))
                        S.op("pool", lambda e: e.tensor_tensor(out=tt3[:, :], in0=tt2[:, 0, :], in1=tt2[:, 1, :], op=ALU.add),
                             reads=(tt2_tl,), writes=(tt3_tl,))
                        S.op("dve", lambda e, g=g: e.scalar_tensor_tensor(
                            out=Toep[:, g, :], in0=C["ident"][:, :], scalar=t["s5_drep"][:, g:g + 1], in1=tt3[:, :],
                            op0=ALU.mult, op1=ALU.add),
                            reads=(tt3_tl, pt, C["ones_tl"]), writes=(Toep_tl,))
            S.barrier()
            S.flush()
        with ExitStack() as es3:
            ub = [[P.sb(es3, "ub%d%d" % (i, gp), [128, NCH], BF16) for gp in range(2)] for i in range(3)]
            ub_tl = [[[Tl("ub") for j in range(8)] for gp in range(2)] for i in range(3)]
            Hs = [[[P.sb(es3, "H%d%d%d" % (z, c, k), [128, NCH]) for k in range(4)] for c in range(2)] for z in range(2)]
            Hs_tl = [[[Tl("H") for k in range(4)] for c in range(2)] for z in range(2)]
            Hb = [[P.sb(es3, "Hb%d%d" % (z, c), [128, NCH], BF16) for c in range(2)] for z in range(2)]
            Hb_tl = [[Tl("Hb") for c in range(2)] for z in range(2)]
            yo = [P.sb(es3, "yo%d" % i, [128, NCH]) for i in range(2)]
            yo_tl = [Tl("yo") for i in range(2)]
            ps_S = [[P.ps(es3, "ps_S%d%d" % (z, c)) for c in range(2)] for z in range(2)]
            ps_S_tl = [[Tl("ps_S") for c in range(2)] for z in range(2)]
            ps_y = [P.ps(es3, "ps_y%d" % i) for i in range(2)]
            ps_y_tl = [Tl("ps_y") for i in range(2)]

            def front(pair):
                bi = pair % 3
                base = 2 * (pair % 2)
                for gp in range(2):
                    g_ = 2 * pair + gp
                    for j in range(8):
                        S.dma("pool", lambda e, bi=bi, gp=gp, g_=g_, j=j: e.dma_start(
                            out=ub[bi][gp][j * 16:(j + 1) * 16, :], in_=U_d[j, g_ * 16:(g_ + 1) * 16, :]),
                            writes=(ub_tl[bi][gp][j],))
                for z in range(2):
                    for comp in range(2):
                        def mmS(e, z=z, comp=comp, bi=bi, pair=pair):
                            ins = None
                            for gp in range(2):
                                ins = e.matmul(ps_S[z][comp][gp * 64:(gp + 1) * 64, 0:NCH],
                                               MB[:, z, pair, comp, gp * 64:(gp + 1) * 64], ub[bi][gp][:, :],
                                               start=True, stop=True)
                            return ins
                        S.op("pe", mmS, reads=[MB_tl] + ub_tl[bi][0] + ub_tl[bi][1], writes=(ps_S_tl[z][comp],))
                        S.op("act", lambda e, z=z, comp=comp, base=base: e.activation(out=Hs[z][comp][base][:, :],
                                                                                      in_=ps_S[z][comp][:, 0:NCH], func=AF.Copy),
                             reads=(ps_S_tl[z][comp],), writes=(Hs_tl[z][comp][base],))
                        keep0 = slice(0, 1) if z == 0 else slice(NCH - 1, NCH)
                        S.op("act", lambda e, z=z, comp=comp, base=base, keep0=keep0: e.activation(
                            out=Hs[z][comp][base + 1][:, keep0], in_=Hs[z][comp][base][:, keep0], func=AF.Copy),
                            reads=(Hs_tl[z][comp][base],), writes=(Hs_tl[z][comp][base + 1],))

            def ks(pair):
                base = 2 * (pair % 2)
                cur = 0
                for m in range(NST):
                    s = 2 ** m
                    nxt = 1 - cur
                    plan = []
                    for z in range(2):
                        col = z * 32 + pair
                        a_r = kar[:, m, col:col + 1]; a_i = kai[:, m, col:col + 1]; a_n = kan[:, m, col:col + 1]
                        o_re, o_im = Hs[z][0][base + cur], Hs[z][1][base + cur]
                        n_re, n_im = Hs[z][0][base + nxt], Hs[z][1][base + nxt]
                        o_tl = (Hs_tl[z][0][base + cur], Hs_tl[z][1][base + cur])
                        n_tl = (Hs_tl[z][0][base + nxt], Hs_tl[z][1][base + nxt])
                        if z == 0:
                            dst = slice(s, NCH); sh = slice(0, NCH - s); keep = slice(0, s)
                        else:
                            dst = slice(0, NCH - s); sh = slice(s, NCH); keep = slice(NCH - s, NCH)
                        if m > 0:
                            S.op("act", lambda e, a=n_re, b=o_re, keep=keep: e.activation(out=a[:, keep], in_=b[:, keep], func=AF.Copy),
                                 reads=(o_tl[0],), writes=(n_tl[0],))
                            S.op("act", lambda e, a=n_im, b=o_im, keep=keep: e.activation(out=a[:, keep], in_=b[:, keep], func=AF.Copy),
                                 reads=(o_tl[1],), writes=(n_tl[1],))
                        plan.append((a_r, a_i, a_n, o_re, o_im, n_re, n_im, o_tl, n_tl, dst, sh))
                    for (a_r, a_i, a_n, o_re, o_im, n_re, n_im, o_tl, n_tl, dst, sh) in plan:
                        S.op("dve", lambda e, a=n_re, b=o_re, dst=dst, sh=sh, sc=a_r: e.scalar_tensor_tensor(
                            out=a[:, dst], in0=b[:, sh], scalar=sc, in1=b[:, dst], op0=ALU.mult, op1=ALU.add),
                            reads=(o_tl[0], ks_tl), writes=(n_tl[0],))
                    for (a_r, a_i, a_n, o_re, o_im, n_re, n_im, o_tl, n_tl, dst, sh) in plan:
                        S.op("dve", lambda e, a=n_im, b=o_im, dst=dst, sh=sh, sc=a_r: e.scalar_tensor_tensor(
                            out=a[:, dst], in0=b[:, sh], scalar=sc, in1=b[:, dst], op0=ALU.mult, op1=ALU.add),
                            reads=(o_tl[1], ks_tl), writes=(n_tl[1],))
                    for (a_r, a_i, a_n, o_re, o_im, n_re, n_im, o_tl, n_tl, dst, sh) in plan:
                        S.op("dve", lambda e, a=n_re, b=o_im, dst=dst, sh=sh, sc=a_n: e.scalar_tensor_tensor(
                            out=a[:, dst], in0=b[:, sh], scalar=sc, in1=a[:, dst], op0=ALU.mult, op1=ALU.add),
                            reads=(o_tl[1], n_tl[0], ks_tl), writes=(n_tl[0],))
                    for (a_r, a_i, a_n, o_re, o_im, n_re, n_im, o_tl, n_tl, dst, sh) in plan:
                        S.op("dve", lambda e, a=n_im, b=o_re, dst=dst, sh=sh, sc=a_i: e.scalar_tensor_tensor(
                            out=a[:, dst], in0=b[:, sh], scalar=sc, in1=a[:, dst], op0=ALU.mult, op1=ALU.add),
                            reads=(o_tl[0], n_tl[1], ks_tl), writes=(n_tl[1],))
                    cur = nxt
                return base + cur

            def back(pair, fin):
                bi = pair % 3
                for z in range(2):
                    for comp in range(2):
                        S.op("act", lambda e, z=z, comp=comp, fin=fin: e.activation(out=Hb[z][comp][:, :], in_=Hs[z][comp][fin][:, :],
                                                                                    func=AF.Copy),
                             reads=(Hs_tl[z][comp][fin],), writes=(Hb_tl[z][comp],))
                for gp in range(2):
                    g = 2 * pair + gp
                    rows = slice(gp * 64, (gp + 1) * 64)

                    def mmY(e, gp=gp, g=g, rows=rows, bi=bi, pair=pair):
                        e.matmul(ps_y[gp][:, 0:NCH], Toep[:, g, :], ub[bi][gp][:, :], start=True, stop=False)
                        e.matmul(ps_y[gp][:, 1:NCH], MCb[rows, 0, pair, 0, :], Hb[0][0][rows, 0:NCH - 1], start=False, stop=False)
                        e.matmul(ps_y[gp][:, 1:NCH], MCb[rows, 0, pair, 1, :], Hb[0][1][rows, 0:NCH - 1], start=False, stop=False)
                        e.matmul(ps_y[gp][:, 0:NCH - 1], MCb[rows, 1, pair, 0, :], Hb[1][0][rows, 1:NCH], start=False, stop=False)
                        return e.matmul(ps_y[gp][:, 0:NCH - 1], MCb[rows, 1, pair, 1, :], Hb[1][1][rows, 1:NCH],
                                        start=False, stop=True)
                    S.op("pe", mmY, reads=[Toep_tl, MCb_tl, Hb_tl[0][0], Hb_tl[0][1], Hb_tl[1][0], Hb_tl[1][1]] + ub_tl[bi][gp],
                         writes=(ps_y_tl[gp],))
                    S.op("act", lambda e, gp=gp: e.activation(out=yo[gp][:, :], in_=ps_y[gp][:, 0:NCH], func=AF.Copy),
                         reads=(ps_y_tl[gp],), writes=(yo_tl[gp],))
                    for i in range(8):
                        S.dma("sp", lambda e, gp=gp, g=g, i=i: e.dma_start(out=Y_d[i, g * 16:(g + 1) * 16, :],
                                                                         in_=yo[gp][i * 16:(i + 1) * 16, :]),
                              reads=(yo_tl[gp],), writes=(junk,), defer=True)

            front(0)
            for pair in range(32):
                if pair + 1 < 32:
                    front(pair + 1)
                fin = ks(pair)
                back(pair, fin)
            S.barrier()
            S.flush()
    return Y_d


GELU_K = 2.0 * math.sqrt(2.0 / math.pi)


def emit_gelu(S, src, src_tl, tmp, tmp_tl, out, out_tl):
    S.op("act", lambda e: e.activation(out=tmp, in_=src, func=AF.Square), reads=(src_tl,), writes=(tmp_tl,))
    S.op("dve", lambda e: e.tensor_scalar(out=tmp, in0=tmp, scalar1=0.044715, scalar2=1.0, op0=ALU.mult, op1=ALU.add),
         reads=(tmp_tl,), writes=(tmp_tl,))
    S.op("dve", lambda e: e.tensor_tensor(out=tmp, in0=tmp, in1=src, op=ALU.mult), reads=(tmp_tl, src_tl), writes=(tmp_tl,))
    S.op("act", lambda e: e.activation(out=tmp, in_=tmp, func=AF.Sigmoid, scale=GELU_K), reads=(tmp_tl,), writes=(tmp_tl,))
    S.op("dve", lambda e: e.tensor_tensor(out=out, in0=tmp, in1=src, op=ALU.mult), reads=(tmp_tl, src_tl), writes=(out_tl,))


def attn_host(inp):
    w = {}
    w["a_sink"] = np.ascontiguousarray(np.broadcast_to(inp["attn_sink"][0][None, :], (128, NQ)))
    jj = np.arange(128)
    m = np.zeros((128, 2, 128), np.float32)
    m[:, 0, :] = (jj[:, None] >= jj[None, :])
    m[:, 1, :] = (jj[:, None] <= jj[None, :])
    w["a_mask"] = m
    return w


def phase_attn(P, C, q_d, v_d):
    L, S = P.L, P.S
    NB = L // 128
    junk = Tl("dram")
    a_sink = P.din("a_sink", [128, NQ])
    a_mask = P.din("a_mask", [128, 2, 128])
    at_d = P.dscr("at_d", [NQ * HD, L], BF16)
    scale = HD ** -0.5
    with ExitStack() as es:
        esk = P.sb(es, "esk", [128, NQ]); msk = P.sb(es, "amask", [128, 2, 128]); cst = Tl("acst")
        S.dma("sp", lambda e: e.dma_start(out=esk[:, :], in_=a_sink), writes=(cst,))
        S.dma("sp", lambda e: e.dma_start(out=msk[:, :, :], in_=a_mask), writes=(cst,))
        S.op("act", lambda e: e.activation(out=esk[:, :], in_=esk[:, :], func=AF.Exp), reads=(cst,), writes=(cst,))
        kt = P.sb(es, "kt", [128, L], BF16); kt_tl = Tl("kt")
        q4 = P.sb(es, "q4", [128, 4, L], BF16); q4_tl = [Tl("q4") for i in range(4)]
        vt = P.sb(es, "vt", [128, NB, 128], BF16); vt_tls = [Tl("vt") for i in range(NB // 4)]
        ao = P.sb(es, "ao", [128, 4, L], BF16); ao_tl = Tl("ao")
        pb = [P.sb(es, "pb%d" % i, [128, 4, 128], BF16) for i in range(6)]
        pb_tl = [Tl("pb") for i in range(6)]
        den = P.sb(es, "den", [128, 4, 128]); den_tl = Tl("den")
        ps_s = [P.ps(es, "ps_s%d" % i) for i in range(3)]; ps_s_tl = [Tl("ps_s") for i in range(3)]
        ps_o = [P.ps(es, "ps_o%d" % i) for i in range(2)]; ps_o_tl = [Tl("ps_o") for i in range(2)]
        ps_d = [P.ps(es, "ps_d%d" % i) for i in range(2)]; ps_d_tl = [Tl("ps_d") for i in range(2)]
        vview = v_d.rearrange("(b p) c -> p b c", p=128)
        si = 0
        for h in range(NKV):
            S.dma("sp", lambda e, h=h: e.dma_start(out=kt[:, :], in_=q_d[NQ + h]), writes=(kt_tl,))
            for hh in range(4):
                S.dma("sp", lambda e, h=h, hh=hh: e.dma_start(out=q4[:, hh, :], in_=q_d[4 * h + hh]), writes=(q4_tl[hh],))
            for b0 in range(0, NB, 4):
                S.dma("sp", lambda e, h=h, b0=b0: e.dma_start(out=vt[:, b0:b0 + 4, :], in_=vview[:, b0:b0 + 4, h * 128:(h + 1) * 128]),
                      writes=(vt_tls[b0 // 4],))
            for n in range(NB):
                kbs = [kb for kb in (n - 1, n, n + 1) if 0 <= kb < NB]
                pbs = []
                for kb in kbs:
                    s_i = si % 3
                    p_i = si % 6
                    si += 1
                    S.op("pe", lambda e, s_i=s_i, kb=kb, n=n: e.matmul(
                        ps_s[s_i][:, :], kt[:, kb * 128:(kb + 1) * 128], q4[:, :, n * 128:(n + 1) * 128],
                        start=True, stop=True), reads=[kt_tl] + q4_tl, writes=(ps_s_tl[s_i],))
                    S.op("act", lambda e, s_i=s_i, p_i=p_i: e.activation(
                        out=pb[p_i][:, :, :].rearrange("p a b -> p (a b)"), in_=ps_s[s_i][:, :], func=AF.Exp, scale=scale),
                        reads=(ps_s_tl[s_i],), writes=(pb_tl[p_i],))
                    if kb != n:
                        mi = 0 if kb < n else 1
                        S.op("pool", lambda e, p_i=p_i, mi=mi: e.tensor_tensor(
                            out=pb[p_i][:, :, :], in0=pb[p_i][:, :, :],
                            in1=msk[:, mi, :].unsqueeze(1).broadcast_to([128, 4, 128]), op=ALU.mult),
                            reads=(pb_tl[p_i], cst), writes=(pb_tl[p_i],))
                    pbs.append(p_i)
                ob = n % 2

                def mmo(e, pbs=pbs, kbs=kbs, ob=ob):
                    ins = None
                    for i, (p_i, kb) in enumerate(zip(pbs, kbs)):
                        ins = e.matmul(ps_o[ob][:, :], vt[:, kb, :], pb[p_i][:, :, :].rearrange("p a b -> p (a b)"),
                                       start=(i == 0), stop=(i == len(kbs) - 1))
                    for i, (p_i, kb) in enumerate(zip(pbs, kbs)):
                        ins = e.matmul(ps_d[ob][:, :], C["onesb"][:, :], pb[p_i][:, :, :].rearrange("p a b -> p (a b)"),
                                       start=(i == 0), stop=(i == len(kbs) - 1))
                    return ins
                S.op("pe", mmo, reads=[C["ones_tl"]] + [vt_tls[kb // 4] for kb in kbs] + [pb_tl[p] for p in pbs],
                     writes=(ps_o_tl[ob], ps_d_tl[ob]))
                S.op("dve", lambda e, ob=ob, h=h: e.tensor_tensor(
                    out=den[:, :, :], in0=ps_d[ob][:, :].rearrange("p (a b) -> p a b", a=4),
                    in1=esk[:, 4 * h:4 * h + 4].unsqueeze(2).broadcast_to([128, 4, 128]), op=ALU.add),
                    reads=(ps_d_tl[ob], cst), writes=(den_tl,))
                S.op("dve", lambda e: e.reciprocal(out=den[:, :, :], in_=den[:, :, :]), reads=(den_tl,), writes=(den_tl,))
                S.op("dve", lambda e, ob=ob, n=n: e.tensor_tensor(
                    out=ao[:, :, n * 128:(n + 1) * 128], in0=ps_o[ob][:, :].rearrange("p (a b) -> p a b", a=4),
                    in1=den[:, :, :], op=ALU.mult),
                    reads=(ps_o_tl[ob], den_tl), writes=(ao_tl,))
            dst = at_d[h * 512:(h + 1) * 512, :].rearrange("(a d) t -> d a t", d=128)
            S.dma("sp", lambda e, dst=dst: e.dma_start(out=dst, in_=ao[:, :, :]), reads=(ao_tl,), writes=(junk,), defer=True)
        S.barrier()
        S.flush()
    return at_d


def l0e_host(inp):
    w = {}
    w["w_glu"] = kxm(inp["s5_w_glu"][0])
    w["b_glu"] = pvec(inp["s5_b_glu"][0])
    w["w_out0"] = kxm(inp["ev_w_out"][0])
    return w


def phase_l0e(P, C, Y_d, at_d, x_src, xres):
    L, NT, S = P.L, P.NT, P.S
    junk = Tl("dram")
    w_glu, w_glu_tl = P.wcast["w_glu"]
    b_glu = P.din("b_glu", [128, 8])
    w_out, w_out_tl = P.wcast["w_out0"]
    with ExitStack() as es:
        bg = P.sb(es, "bg", [128, 8]); cst = Tl("cst")
        S.dma("sp", lambda e: e.dma_start(out=bg[:, :], in_=b_glu), writes=(cst,))
        xt = P.sb(es, "xt", [128, KC, TT]); xt_tl = [Tl("xt") for k in range(KC)]
        yt = [P.sb(es, "yt%d" % i, [128, 8, TT // 8]) for i in range(2)]; yt_tl2 = [[Tl("yt") for j in range(2)] for i in range(2)]
        yn_ = [P.sb(es, "yn%d" % i, [128, TT]) for i in range(2)]; yn_tl_ = [Tl("yn") for i in range(2)]
        tmp_ = [P.sb(es, "tmp%d" % i, [128, TT]) for i in range(2)]; tmp_tl_ = [Tl("tmp") for i in range(2)]
        g32_ = [P.sb(es, "g32%d" % i, [128, 8, TT]) for i in range(2)]; g32_tl_ = [[Tl("g32") for k in range(8)] for i in range(2)]
        gb_ = [P.sb(es, "gb%d" % i, [128, 8, TT], BF16) for i in range(2)]; gb_tl_ = [[Tl("gb") for k in range(8)] for i in range(2)]
        mix_ = [P.sb(es, "mix%d" % i, [128, KC, TT], BF16) for i in range(2)]
        mix_tl_ = [[Tl("mix") for k in range(KC)] for i in range(2)]
        sg_ = [P.sb(es, "sg%d" % i, [128, TT]) for i in range(2)]; sg_tl_ = [Tl("sg") for i in range(2)]
        NWG = 3
        NWO = 6
        NPS = 6
        wg = [P.sb(es, "wg%d" % i, [128, 8, 128], BF16) for i in range(NWG)]; wg_tl = [Tl("wg") for i in range(NWG)]
        wo = [P.sb(es, "wo%d" % i, [128, KC, 128], BF16) for i in range(NWO)]; wo_tl = [Tl("wo") for i in range(NWO)]
        xo = [P.sb(es, "xo%d" % i, [128, TT]) for i in range(2)]; xo_tl = [Tl("xo") for i in range(2)]
        ps = [P.ps(es, "ps%d" % i) for i in range(NPS)]; ps_tl = [Tl("ps") for i in range(NPS)]
        xview = x_src.rearrange("(kc p) t -> p kc t", p=128)
        cnt = {"pi": 0}

        def gelu_stage(tt):
            t0 = tt * TT
            c0 = tt * (TT // 8)
            k = tt % 2
            g32, g32_tl, gb, gb_tl, mix, mix_tl = g32_[k], g32_tl_[k], gb_[k], gb_tl_[k], mix_[k], mix_tl_[k]
            S.dma("sp", lambda e: e.dma_start(
                out=mix[:, 8:16, :], in_=at_d.rearrange("(c p) t -> p c t", p=128)[:, :, t0:t0 + TT]),
                writes=mix_tl[8:16])
            for ct in range(8):
                yb = ct % 2
                for i0_ in (0, 4):
                    S.dma("sp", lambda e, yb=yb, ct=ct, i0_=i0_: e.dma_start(
                        out=yt[yb][:, i0_:i0_ + 4, :],
                        in_=Y_d[i0_:i0_ + 4, ct * 128:(ct + 1) * 128, c0:c0 + TT // 8].rearrange("i p c -> p i c")),
                        writes=(yt_tl2[yb][i0_ // 4],))
                yn, yn_tl, tmp, tmp_tl = yn_[yb], yn_tl_[yb], tmp_[yb], tmp_tl_[yb]
                S.op("act", lambda e, yb=yb, yn=yn: e.activation(out=yn[:, :].rearrange("p (c i) -> p i c", i=8), in_=yt[yb][:, :, :],
                                                                 func=AF.Copy),
                     reads=yt_tl2[yb], writes=(yn_tl,))
                emit_gelu(S, yn[:, :], yn_tl, tmp[:, :], tmp_tl, g32[:, ct, :], g32_tl[ct])
                S.op("act", lambda e, ct=ct: e.activation(out=gb[:, ct, :], in_=g32[:, ct, :], func=AF.Copy),
                     reads=(g32_tl[ct],), writes=(gb_tl[ct],))

        gelu_stage(0)
        for tt in range(NT):
            t0 = tt * TT
            k = tt % 2
            g32, g32_tl, gb, gb_tl, mix, mix_tl = g32_[k], g32_tl_[k], gb_[k], gb_tl_[k], mix_[k], mix_tl_[k]
            for kc in range(KC):
                S.dma("sp", lambda e, kc=kc, t0=t0: e.dma_start(out=xt[:, kc, :], in_=xview[:, kc, t0:t0 + TT]),
                      writes=(xt_tl[kc],))
            for oc in range(8):
                b = oc % NWG
                sg, sg_tl = sg_[oc % 2], sg_tl_[oc % 2]
                S.dma("pool", lambda e, b=b, oc=oc: e.dma_start(out=wg[b][:, :, :], in_=w_glu[oc]), reads=(w_glu_tl[oc],), writes=(wg_tl[b],))
                p_i = cnt["pi"] % NPS
                cnt["pi"] += 1

                def mm(e, b=b, p_i=p_i, gb=gb):
                    ins = None
                    for kc in range(8):
                        ins = e.matmul(ps[p_i][:, :], wg[b][:, kc, :], gb[:, kc, :], start=(kc == 0), stop=(kc == 7))
                    return ins
                S.op("pe", mm, reads=[wg_tl[b]] + gb_tl, writes=(ps_tl[p_i],))
                S.op("act", lambda e, p_i=p_i, oc=oc, sg=sg: e.activation(out=sg[:, :], in_=ps[p_i][:, :], func=AF.Sigmoid,
                                                                          bias=bg[:, oc:oc + 1]),
                     reads=(ps_tl[p_i], cst), writes=(sg_tl,))
                S.op("dve", lambda e, oc=oc, sg=sg, mix=mix, g32=g32: e.tensor_tensor(out=mix[:, oc, :], in0=g32[:, oc, :], in1=sg[:, :],
                                                                                      op=ALU.mult),
                     reads=(g32_tl[oc], sg_tl), writes=(mix_tl[oc],))
            for oc in range(KC):
                if oc == 2 and tt + 1 < NT:
                    gelu_stage(tt + 1)
                b = oc % NWO
                xb_ = oc % 2
                S.dma("pool", lambda e, b=b, oc=oc: e.dma_start(out=wo[b][:, :, :], in_=w_out[oc]), reads=(w_out_tl[oc],), writes=(wo_tl[b],))
                p_i = cnt["pi"] % NPS
                cnt["pi"] += 1

                def mm2(e, b=b, p_i=p_i, mix=mix):
                    ins = None
                    for kc in range(KC):
                        ins = e.matmul(ps[p_i][:, :], wo[b][:, kc, :], mix[:, kc, :], start=(kc == 0), stop=(kc == KC - 1))
                    return ins
                S.op("pe", mm2, reads=[wo_tl[b]] + mix_tl, writes=(ps_tl[p_i],))
                S.op("dve", lambda e, xb_=xb_, p_i=p_i, oc=oc: e.tensor_tensor(out=xo[xb_][:, :], in0=ps[p_i][:, :], in1=xt[:, oc, :],
                                                                               op=ALU.add),
                     reads=(ps_tl[p_i], xt_tl[oc]), writes=(xo_tl[xb_],))
                S.dma("sp", lambda e, xb_=xb_, oc=oc, t0=t0: e.dma_start(out=xres[oc * 128:(oc + 1) * 128, t0:t0 + TT], in_=xo[xb_][:, :]),
                      reads=(xo_tl[xb_],), writes=(junk,), defer=True)
        S.barrier()
        S.flush()


def ffn_host(inp):
    w = {}
    for l in range(2):
        w["w1_%d" % l] = kxm(inp["ffn_w1"][l])
        w["w3_%d" % l] = kxm(inp["ffn_w3"][l])
        w["w2_%d" % l] = kxm(inp["ffn_w2"][l])
    return w


def phase_ffn(P, C, layer, xres, TF):
    L, S = P.L, P.S
    NTF = L // TF
    NH = TF // TT
    junk = Tl("dram")
    w1 = P.din("w1_%d" % layer, [FC, 128, KC, 128])
    w3 = P.din("w3_%d" % layer, [FC, 128, KC, 128])
    w2 = P.din("w2_%d" % layer, [KC, 128, FC, 128])
    gain = C["gffn"][:, layer, :]
    with ExitStack() as es:
        hT = P.sb(es, "hT", [128, KC, TF], BF16); hT_tl = [Tl("hT") for k in range(KC)]
        act = P.sb(es, "act", [128, FC, TF], BF16); act_tl = [Tl("act") for k in range(FC)]
        xc = [P.sb(es, "xc%d" % i, [128, TF]) for i in range(3)]; xc_tl = [Tl("xc") for i in range(3)]
        sq = [P.sb(es, "sq%d" % i, [128, TF]) for i in range(2)]; sq_tl = [Tl("sq") for i in range(2)]
        rs = P.sb(es, "rs", [128, TF]); rs_tl = Tl("rs")
        s1 = [P.sb(es, "s1%d" % i, [128, TT]) for i in range(2)]; s1_tl = [Tl("s1") for i in range(2)]
        wa = [P.sb(es, "wa%d" % i, [128, KC, 128], BF16) for i in range(2)]; wa_tl = [Tl("wa") for i in range(2)]
        wb3 = [P.sb(es, "wb%d" % i, [128, KC, 128], BF16) for i in range(2)]; wb3_tl = [Tl("wb") for i in range(2)]
        wc = [P.sb(es, "wc%d" % i, [128, FC, 128], BF16) for i in range(2)]; wc_tl = [[Tl("wc") for k in range(3)] for i in range(2)]
        xo = [P.sb(es, "xo%d" % i, [128, TT]) for i in range(2)]; xo_tl = [Tl("xo") for i in range(2)]
        ps_st = [P.ps(es, "ps_st%d" % i) for i in range(NH)]; ps_st_tl = [Tl("ps_st") for i in range(NH)]
        ps_a = [P.ps(es, "ps_a%d" % i) for i in range(2)]; ps_a_tl = [Tl("ps_a") for i in range(2)]
        ps_b = [P.ps(es, "ps_b%d" % i) for i in range(2)]; ps_b_tl = [Tl("ps_b") for i in range(2)]
        ps_o = [P.ps(es, "ps_o%d" % i) for i in range(2)]; ps_o_tl = [Tl("ps_o") for i in range(2)]
        xview = xres.rearrange("(kc p) t -> p kc t", p=128)
        rs2 = [rs, P.sb(es, "rsb", [128, TF])]; rs2_tl = [rs_tl, Tl("rsb")]
        cnt = {"xi": 0, "wi": 0}

        def stats(tf):
            t0 = tf * TF
            rsx, rsx_tl = rs2[tf % 2], rs2_tl[tf % 2]
            for kc in range(KC):
                b = cnt["xi"] % 3; cnt["xi"] += 1
                sb_ = kc % 2
                S.dma("sp", lambda e, b=b, kc=kc, t0=t0: e.dma_start(out=xc[b][:, :], in_=xview[:, kc, t0:t0 + TF]),
                      writes=(xc_tl[b],))
                S.op("act", lambda e, b=b, sb_=sb_: e.activation(out=sq[sb_][:, :], in_=xc[b][:, :], func=AF.Square),
                     reads=(xc_tl[b],), writes=(sq_tl[sb_],))

                def mm(e, kc=kc, sb_=sb_):
                    ins = None
                    for hh in range(NH):
                        ins = e.matmul(ps_st[hh][:, :], C["ones"][:, :], sq[sb_][:, hh * TT:(hh + 1) * TT],
                                       start=(kc == 0), stop=(kc == KC - 1))
                    return ins
                S.op("pe", mm, reads=(sq_tl[sb_], C["ones_tl"]), writes=ps_st_tl)
            for hh in range(NH):
                S.op("act", lambda e, hh=hh, rsx=rsx: e.activation(out=rsx[:, hh * TT:(hh + 1) * TT], in_=ps_st[hh][:, :], func=AF.Ln,
                                                                   scale=1.0 / D, bias=C["eps"][:, 0:1]),
                     reads=(ps_st_tl[hh], C["ones_tl"]), writes=(rsx_tl,))
            S.op("act", lambda e, rsx=rsx: e.activation(out=rsx[:, :], in_=rsx[:, :], func=AF.Exp, scale=-0.5),
                 reads=(rsx_tl,), writes=(rsx_tl,))

        stats(0)
        for tf in range(NTF):
            t0 = tf * TF
            rsx, rsx_tl = rs2[tf % 2], rs2_tl[tf % 2]
            for kc in range(KC):
                b = cnt["xi"] % 3; cnt["xi"] += 1
                S.dma("sp", lambda e, b=b, kc=kc, t0=t0: e.dma_start(out=xc[b][:, :], in_=xview[:, kc, t0:t0 + TF]),
                      writes=(xc_tl[b],))
                S.op("dve", lambda e, b=b, kc=kc, rsx=rsx: e.scalar_tensor_tensor(
                    out=hT[:, kc, :], in0=xc[b][:, :], scalar=gain[:, kc:kc + 1], in1=rsx[:, :], op0=ALU.mult, op1=ALU.mult),
                    reads=(xc_tl[b], rsx_tl, C["ones_tl"]), writes=(hT_tl[kc],))
            for fc in range(FC):
                b = cnt["wi"] % 2; cnt["wi"] += 1
                S.dma("pool", lambda e, b=b, fc=fc: e.dma_start(out=wa[b][:, :, :], in_=w1[fc]), writes=(wa_tl[b],))
                S.dma("pool", lambda e, b=b, fc=fc: e.dma_start(out=wb3[b][:, :, :], in_=w3[fc]), writes=(wb3_tl[b],))
                for hh in range(NH):
                    pb_ = hh % 2
                    cs = slice(hh * TT, (hh + 1) * TT)

                    def mm1(e, b=b, pb_=pb_, cs=cs):
                        ins = None
                        for kc in range(KC):
                            ins = e.matmul(ps_a[pb_][:, :], wa[b][:, kc, :], hT[:, kc, cs], start=(kc == 0), stop=(kc == KC - 1))
                        return ins

                    def mm3(e, b=b, pb_=pb_, cs=cs):
                        ins = None
                        for kc in range(KC):
                            ins = e.matmul(ps_b[pb_][:, :], wb3[b][:, kc, :], hT[:, kc, cs], start=(kc == 0), stop=(kc == KC - 1))
                        return ins
                    S.op("pe", mm1, reads=[wa_tl[b]] + hT_tl, writes=(ps_a_tl[pb_],))
                    S.op("pe", mm3, reads=[wb3_tl[b]] + hT_tl, writes=(ps_b_tl[pb_],))
                    S.op("act", lambda e, pb_=pb_: e.activation(out=s1[pb_][:, :], in_=ps_a[pb_][:, :], func=AF.Silu),
                         reads=(ps_a_tl[pb_],), writes=(s1_tl[pb_],))
                    S.op("dve", lambda e, pb_=pb_, fc=fc, cs=cs: e.tensor_tensor(out=act[:, fc, cs], in0=ps_b[pb_][:, :],
                                                                                in1=s1[pb_][:, :], op=ALU.mult),
                         reads=(ps_b_tl[pb_], s1_tl[pb_]), writes=(act_tl[fc],))
            for oc in range(KC):
                b = oc % 2
                for pi_, (k0, k1) in enumerate(((0, 16), (16, 32), (32, FC))):
                    S.dma("pool", lambda e, b=b, oc=oc, k0=k0, k1=k1: e.dma_start(out=wc[b][:, k0:k1, :], in_=w2[oc][:, k0:k1, :]),
                          writes=(wc_tl[b][pi_],))
                xb_ = cnt["xi"] % 3; cnt["xi"] += 1
                S.dma("sp", lambda e, xb_=xb_, oc=oc, t0=t0: e.dma_start(out=xc[xb_][:, :], in_=xview[:, oc, t0:t0 + TF]),
                      writes=(xc_tl[xb_],))
                for hh in range(NH):
                    pb_ = hh % 2
                    cs = slice(hh * TT, (hh + 1) * TT)

                    def mm2(e, b=b, pb_=pb_, cs=cs):
                        ins = None
                        for fc in range(FC):
                            ins = e.matmul(ps_o[pb_][:, :], wc[b][:, fc, :], act[:, fc, cs], start=(fc == 0), stop=(fc == FC - 1))
                        return ins
                    S.op("pe", mm2, reads=wc_tl[b] + act_tl, writes=(ps_o_tl[pb_],))
                    S.op("dve", lambda e, pb_=pb_, xb_=xb_, cs=cs: e.tensor_tensor(out=xo[pb_][:, :], in0=ps_o[pb_][:, :],
                                                                                  in1=xc[xb_][:, cs], op=ALU.add),
                         reads=(ps_o_tl[pb_], xc_tl[xb_]), writes=(xo_tl[pb_],))
                    S.dma("sp", lambda e, pb_=pb_, oc=oc, t0=t0, hh=hh: e.dma_start(
                        out=xres[oc * 128:(oc + 1) * 128, t0 + hh * TT:t0 + (hh + 1) * TT], in_=xo[pb_][:, :]),
                        reads=(xo_tl[pb_],), writes=(junk,), defer=True)
                if oc == 3 and tf + 1 < NTF:
                    stats(tf + 1)
        S.barrier()
        S.flush()


def l1_host(inp):
    w = {}
    w["w_in1"] = kxm(inp["od_w_in"][0])
    w["w_out1"] = kxm(inp["od_w_out"][0])
    w["l_cw"] = np.ascontiguousarray(inp["lru_conv_w"][0].reshape(4, KC, 128).transpose(2, 1, 0))
    w["l_cb"] = pvec(inp["lru_conv_b"][0])

    def v2(a):
        return np.ascontiguousarray(a.reshape(2, KC, 128).transpose(2, 0, 1))
    w["l_ba"] = v2(inp["lru_ba"][0]); w["l_bx"] = v2(inp["lru_bx"][0]); w["l_lam"] = v2(inp["lru_lam"][0])
    w["l_wa"] = np.ascontiguousarray(inp["lru_wa"][0])
    w["l_wx"] = np.ascontiguousarray(inp["lru_wx"][0])
    return w


def phase_l1a(P, C, xres):
    L, NT, S = P.L, P.NT, P.S
    junk = Tl("dram")
    w_in, w_in_tl = P.wcast["w_in1"]
    gate_d = P.dscr("gate_d", [D, L], BF16)
    xb_d = P.dscr("xb_d", [D, L], F32)
    with ExitStack() as es:
        xt = P.sb(es, "xt", [128, KC, TT]); xt_tl = Tl("xt")
        sq = P.sb(es, "sq", [128, KC, TT]); sq_tl = Tl("sq")
        hT2 = [P.sb(es, "hT%d" % i, [128, KC, TT], BF16) for i in range(2)]
        hT2_tl = [[Tl("hT") for k in range(KC)] for i in range(2)]
        rs = P.sb(es, "rs", [128, TT]); rs_tl = Tl("rs")
        NWB = 6
        NPO = 5
        wb = [P.sb(es, "wb%d" % i, [128, KC, 128], BF16) for i in range(NWB)]; wb_tl = [Tl("wb") for i in range(NWB)]
        tmp_ = [P.sb(es, "tmp%d" % i, [128, TT]) for i in range(2)]; tmp_tl_ = [Tl("tmp") for i in range(2)]
        go = [P.sb(es, "go%d" % i, [128, TT], BF16) for i in range(2)]; go_tl = [Tl("go") for i in range(2)]
        xo = [P.sb(es, "xo%d" % i, [128, TT]) for i in range(2)]; xo_tl = [Tl("xo") for i in range(2)]
        ps_stat = P.ps(es, "ps_stat"); ps_stat_tl = Tl("ps_stat")
        ps_o = [P.ps(es, "ps_o%d" % i) for i in range(NPO)]; ps_o_tl = [Tl("ps_o") for i in range(NPO)]
        xview = xres.rearrange("(kc p) t -> p kc t", p=128)
        wi = 0
        def norm(tt_):
            t0_ = tt_ * TT
            S.dma("sp", lambda e, t0_=t0_: e.dma_start(out=xt[:, :, :], in_=xview[:, :, t0_:t0_ + TT]), writes=(xt_tl,))
            emit_rmsnorm(P, C, xt, xt_tl, C["gmix"][:, 1, :], hT2[tt_ % 2], hT2_tl[tt_ % 2], sq, sq_tl,
                         ps_stat, ps_stat_tl, rs, rs_tl)

        norm(0)
        for tt in range(NT):
            t0 = tt * TT
            hT, hT_tl = hT2[tt % 2], hT2_tl[tt % 2]
            for oc in range(32):
                if oc == 10 and tt + 1 < NT:
                    norm(tt + 1)
                wbi = wi % NWB
                b = wi % 2
                pb_ = wi % NPO
                wi += 1
                S.dma("pool", lambda e, wbi=wbi, oc=oc: e.dma_start(out=wb[wbi][:, :, :], in_=w_in[oc]), reads=(w_in_tl[oc],), writes=(wb_tl[wbi],))

                def mm(e, wbi=wbi, pb_=pb_, hT=hT):
                    ins = None
                    for kc in range(KC):
                        ins = e.matmul(ps_o[pb_][:, :], wb[wbi][:, kc, :], hT[:, kc, :], start=(kc == 0), stop=(kc == KC - 1))
                    return ins
                S.op("pe", mm, reads=[wb_tl[wbi]] + hT_tl, writes=(ps_o_tl[pb_],))
                if oc < 16:
                    emit_gelu(S, ps_o[pb_][:, :], ps_o_tl[pb_], tmp_[b][:, :], tmp_tl_[b], go[b][:, :], go_tl[b])
                    S.dma("sp", lambda e, b=b, oc=oc, t0=t0: e.dma_start(out=gate_d[oc * 128:(oc + 1) * 128, t0:t0 + TT], in_=go[b][:, :]),
                          reads=(go_tl[b],), writes=(junk,), defer=True)
                else:
                    S.op("act", lambda e, b=b, pb_=pb_: e.activation(out=xo[b][:, :], in_=ps_o[pb_][:, :], func=AF.Copy),
                         reads=(ps_o_tl[pb_],), writes=(xo_tl[b],))
                    S.dma("sp", lambda e, b=b, oc=oc, t0=t0: e.dma_start(
                        out=xb_d[(oc - 16) * 128:(oc - 15) * 128, t0:t0 + TT], in_=xo[b][:, :]),
                        reads=(xo_tl[b],), writes=(junk,), defer=True)
        S.barrier()
        S.flush()
    return gate_d, xb_d


def phase_l1b(P, C, gate_d, xb_d):
    L, S = P.L, P.S
    NT = L // TT
    junk = Tl("dram")
    srcs = {nm: P.din(nm, shp) for nm, shp in (("l_cw", [128, KC, 4]), ("l_cb", [128, KC]), ("l_ba", [128, 2, KC]),
                                               ("l_bx", [128, 2, KC]), ("l_lam", [128, 2, KC]))}
    l_wa = P.din("l_wa", [2, 8, 256, 256])
    l_wx = P.din("l_wx", [2, 8, 256, 256])
    yrec_d = P.dscr("yrec_d", [D, L], BF16)
    with ExitStack() as es:
        cst = Tl("cst")
        cw = P.sb(es, "cw", [128, KC, 4]); cb = P.sb(es, "cb", [128, KC])
        ba = P.sb(es, "ba", [128, 2, KC]); bx = P.sb(es, "bx", [128, 2, KC]); lam = P.sb(es, "lam", [128, 2 * KC])
        S.dma("sp", lambda e: e.dma_start(out=cw[:, :, :], in_=srcs["l_cw"]), writes=(cst,))
        S.dma("sp", lambda e: e.dma_start(out=cb[:, :], in_=srcs["l_cb"]), writes=(cst,))
        S.dma("sp", lambda e: e.dma_start(out=ba[:, :, :], in_=srcs["l_ba"]), writes=(cst,))
        S.dma("sp", lambda e: e.dma_start(out=bx[:, :, :], in_=srcs["l_bx"]), writes=(cst,))
        S.dma("sp", lambda e: e.dma_start(out=lam[:, :], in_=srcs["l_lam"].rearrange("p z c -> p (z c)")), writes=(cst,))
        tt_ = P.sb(es, "sp_t", [128, 2 * KC]); ser = P.sb(es, "sp_ser", [128, 2 * KC]); lnv = P.sb(es, "sp_ln", [128, 2 * KC])
        mk = P.sb(es, "sp_mk", [128, 2 * KC]); nsp = P.sb(es, "nsp", [128, 2 * KC])
        O = lambda eng, f: S.op(eng, f, reads=(cst,), writes=(cst,))
        O("act", lambda e: e.activation(out=tt_[:, :], in_=lam[:, :], func=AF.Exp, scale=-1.0))
        O("act", lambda e: e.activation(out=lnv[:, :], in_=tt_[:, :], func=AF.Ln, bias=C["ones"][:, 0:1]))
        O("dve", lambda e: e.tensor_scalar(out=ser[:, :], in0=tt_[:, :], scalar1=-0.25, scalar2=1.0 / 3.0, op0=ALU.mult, op1=ALU.add))
        O("dve", lambda e: e.tensor_tensor(out=ser[:, :], in0=ser[:, :], in1=tt_[:, :], op=ALU.mult))
        O("dve", lambda e: e.tensor_scalar(out=ser[:, :], in0=ser[:, :], scalar1=-0.5, scalar2=None, op0=ALU.add))
        O("dve", lambda e: e.tensor_tensor(out=ser[:, :], in0=ser[:, :], in1=tt_[:, :], op=ALU.mult))
        O("dve", lambda e: e.tensor_scalar(out=ser[:, :], in0=ser[:, :], scalar1=1.0, scalar2=None, op0=ALU.add))
        O("dve", lambda e: e.tensor_tensor(out=ser[:, :], in0=ser[:, :], in1=tt_[:, :], op=ALU.mult))
        O("dve", lambda e: e.tensor_scalar(out=mk[:, :], in0=tt_[:, :], scalar1=0.05, scalar2=None, op0=ALU.is_lt))
        O("dve", lambda e: e.tensor_tensor(out=ser[:, :], in0=ser[:, :], in1=lnv[:, :], op=ALU.subtract))
        O("dve", lambda e: e.tensor_tensor(out=ser[:, :], in0=ser[:, :], in1=mk[:, :], op=ALU.mult))
        O("dve", lambda e: e.tensor_tensor(out=ser[:, :], in0=ser[:, :], in1=lnv[:, :], op=ALU.add))
        O("dve", lambda e: e.tensor_scalar(out=nsp[:, :], in0=ser[:, :], scalar1=-8.0, scalar2=None, op0=ALU.mult))

        xb = P.sb(es, "xb", [128, 2, L]); xb_tl = [Tl("xb") for i in range(2)]
        cv = P.sb(es, "cv", [128, 2, L]); cv_tl = [Tl("cv") for i in range(2)]
        cvb = P.sb(es, "cvb", [128, 2, L], BF16); cvb_tl = [Tl("cvb") for i in range(2)]
        hs = xb; hs_tl = xb_tl
        rr_ = [P.sb(es, "rr%d" % i, [128, L]) for i in range(2)]; rr_tl_ = [Tl("rr") for i in range(2)]
        ii_ = [P.sb(es, "ii%d" % i, [128, L]) for i in range(2)]; ii_tl_ = [Tl("ii") for i in range(2)]
        tmp_ = [P.sb(es, "tmp%d" % i, [128, L]) for i in range(2)]; tmp_tl_ = [Tl("tmp") for i in range(2)]
        gt = P.sb(es, "gt", [128, 2, L], BF16); gt_tl = [Tl("gt") for i in range(2)]
        wsb = [[P.sb(es, "w%d%d" % (k, z), [128, 2, 256], BF16) for z in range(2)] for k in range(2)]
        wsb_tl = [[Tl("w") for z in range(2)] for k in range(2)]
        ps_r = [P.ps(es, "ps_r%d" % i) for i in range(3)]; ps_r_tl = [Tl("ps_r") for i in range(3)]
        ps_i = [P.ps(es, "ps_i%d" % i) for i in range(3)]; ps_i_tl = [Tl("ps_i") for i in range(3)]
        pi_box = [0]
        ui = 0

        def stage1(nb, z, co, k):
            ct = nb * 2 + co
            rr, rr_tl, ii, ii_tl = rr_[k], rr_tl_[k], ii_[k], ii_tl_[k]
            for tk in range(NT):
                cs = slice(tk * TT, (tk + 1) * TT)
                pb_ = pi_box[0] % 3
                pi_box[0] += 1

                def mmr(e, z=z, co=co, cs=cs, pb_=pb_):
                    ins = None
                    for ci in range(2):
                        ins = e.matmul(ps_r[pb_][:, :], wsb[0][z][:, ci, co * 128:(co + 1) * 128], cvb[:, ci, cs],
                                       start=(ci == 0), stop=(ci == 1))
                    return ins

                def mmi(e, z=z, co=co, cs=cs, pb_=pb_):
                    ins = None
                    for ci in range(2):
                        ins = e.matmul(ps_i[pb_][:, :], wsb[1][z][:, ci, co * 128:(co + 1) * 128], cvb[:, ci, cs],
                                       start=(ci == 0), stop=(ci == 1))
                    return ins
                S.op("pe", mmr, reads=(wsb_tl[0][z], cvb_tl[0], cvb_tl[1]), writes=(ps_r_tl[pb_],))
                S.op("pe", mmi, reads=(wsb_tl[1][z], cvb_tl[0], cvb_tl[1]), writes=(ps_i_tl[pb_],))
                S.op("act", lambda e, z=z, ct=ct, cs=cs, pb_=pb_, rr=rr: e.activation(
                    out=rr[:, cs], in_=ps_r[pb_][:, :], func=AF.Sigmoid, bias=ba[:, z, ct:ct + 1]),
                    reads=(ps_r_tl[pb_], cst), writes=(rr_tl,))
                S.op("act", lambda e, z=z, ct=ct, cs=cs, pb_=pb_, ii=ii: e.activation(
                    out=ii[:, cs], in_=ps_i[pb_][:, :], func=AF.Sigmoid, bias=bx[:, z, ct:ct + 1]),
                    reads=(ps_i_tl[pb_], cst), writes=(ii_tl,))

        def stage2(nb, z, co, k):
            ct = nb * 2 + co
            rr, rr_tl, ii, ii_tl, tmp, tmp_tl = rr_[k], rr_tl_[k], ii_[k], ii_tl_[k], tmp_[k], tmp_tl_[k]
            col = z * KC + ct
            S.op("act", lambda e: e.activation(out=rr[:, :], in_=rr[:, :], func=AF.Exp, scale=nsp[:, col:col + 1]),
                 reads=(rr_tl, cst), writes=(rr_tl,))
            S.op("dve", lambda e: e.tensor_tensor(out=ii[:, :], in0=ii[:, :], in1=cv[:, co, :], op=ALU.mult),
                 reads=(ii_tl, cv_tl[co]), writes=(ii_tl,))
            S.op("act", lambda e: e.activation(out=tmp[:, :], in_=rr[:, :], func=AF.Square),
                 reads=(rr_tl,), writes=(tmp_tl,))
            S.op("act", lambda e: e.activation(out=tmp[:, :], in_=tmp[:, :], func=AF.Sqrt, scale=-1.0, bias=C["ones"][:, 0:1]),
                 reads=(tmp_tl, C["ones_tl"]), writes=(tmp_tl,))
            S.op("dve", lambda e: e.tensor_tensor(out=ii[:, :], in0=ii[:, :], in1=tmp[:, :], op=ALU.mult),
                 reads=(ii_tl, tmp_tl), writes=(ii_tl,))
            if z == 0:
                S.op("dve", lambda e: e.tensor_tensor_scan(out=hs[:, co, :], data0=rr[:, :], data1=ii[:, :], initial=0.0,
                                                           op0=ALU.mult, op1=ALU.add),
                     reads=(rr_tl, ii_tl), writes=(hs_tl[co],))
            else:
                S.op("dve", lambda e: e.tensor_tensor_scan(out=tmp[:, ::-1], data0=rr[:, ::-1], data1=ii[:, ::-1], initial=0.0,
                                                           op0=ALU.mult, op1=ALU.add),
                     reads=(rr_tl, ii_tl, tmp_tl), writes=(tmp_tl,))
                S.op("dve", lambda e: e.tensor_tensor(out=hs[:, co, :], in0=hs[:, co, :], in1=tmp[:, :], op=ALU.add),
                     reads=(hs_tl[co], tmp_tl), writes=(hs_tl[co],))

        for nb in range(8):
            for ci in range(2):
                S.dma("sp", lambda e, ci=ci, nb=nb: e.dma_start(out=xb[:, ci, :], in_=xb_d[nb * 256 + ci * 128: nb * 256 + (ci + 1) * 128, :]),
                      writes=(xb_tl[ci],))
                S.dma("sp", lambda e, ci=ci, nb=nb: e.dma_start(out=gt[:, ci, :], in_=gate_d[nb * 256 + ci * 128: nb * 256 + (ci + 1) * 128, :]),
                      writes=(gt_tl[ci],))
            for z in range(2):
                S.dma("pool", lambda e, z=z, nb=nb: e.dma_start(out=wsb[0][z][:, :, :], in_=l_wa[z, nb].rearrange("(c p) j -> p c j", p=128)),
                      writes=(wsb_tl[0][z],))
                S.dma("pool", lambda e, z=z, nb=nb: e.dma_start(out=wsb[1][z][:, :, :], in_=l_wx[z, nb].rearrange("(c p) j -> p c j", p=128)),
                      writes=(wsb_tl[1][z],))
            for ci in range(2):
                ct = nb * 2 + ci
                S.op("act", lambda e, ci=ci, ct=ct: e.activation(out=cv[:, ci, :], in_=xb[:, ci, :], func=AF.Identity,
                                                                 scale=cw[:, ct, 2:3], bias=cb[:, ct:ct + 1]),
                     reads=(xb_tl[ci], cst), writes=(cv_tl[ci],))
                for (tap, dsl, ssl) in ((0, slice(2, L), slice(0, L - 2)), (1, slice(1, L), slice(0, L - 1)),
                                        (3, slice(0, L - 1), slice(1, L))):
                    S.op("dve", lambda e, ci=ci, ct=ct, tap=tap, dsl=dsl, ssl=ssl: e.scalar_tensor_tensor(
                        out=cv[:, ci, dsl], in0=xb[:, ci, ssl], scalar=cw[:, ct, tap:tap + 1], in1=cv[:, ci, dsl],
                        op0=ALU.mult, op1=ALU.add),
                        reads=(xb_tl[ci], cv_tl[ci], cst), writes=(cv_tl[ci],))
                S.op("act", lambda e, ci=ci: e.activation(out=cvb[:, ci, :], in_=cv[:, ci, :], func=AF.Copy),
                     reads=(cv_tl[ci],), writes=(cvb_tl[ci],))
            units = [(0, 0), (0, 1), (1, 0), (1, 1)]
            ks = [(ui + j) % 2 for j in range(4)]
            ui += 4
            for j in (0, 2):
                stage1(nb, units[j][0], units[j][1], ks[j])
                stage1(nb, units[j + 1][0], units[j + 1][1], ks[j + 1])
                stage2(nb, units[j][0], units[j][1], ks[j])
                stage2(nb, units[j + 1][0], units[j + 1][1], ks[j + 1])
            for co in range(2):
                S.op("dve", lambda e, co=co: e.tensor_tensor(out=gt[:, co, :], in0=hs[:, co, :], in1=gt[:, co, :], op=ALU.mult),
                     reads=(hs_tl[co], gt_tl[co]), writes=(gt_tl[co],))
                S.dma("sp", lambda e, co=co, nb=nb: e.dma_start(out=yrec_d[nb * 256 + co * 128: nb * 256 + (co + 1) * 128, :], in_=gt[:, co, :]),
                      reads=(gt_tl[co],), writes=(junk,), defer=True)
        S.barrier()
        S.flush()
    return yrec_d


def phase_l1c(P, C, yrec_d, xres):
    L, NT, S = P.L, P.NT, P.S
    junk = Tl("dram")
    w_out, w_out_tl = P.wcast["w_out1"]
    with ExitStack() as es:
        xt = P.sb(es, "xt", [128, KC, TT]); xt_tl = [Tl("xt") for k in range(KC)]
        mix2 = [P.sb(es, "mix%d" % i, [128, KC, TT], BF16) for i in range(2)]; mix2_tl = [Tl("mix") for i in range(2)]
        NWO = 6
        wo = [P.sb(es, "wo%d" % i, [128, KC, 128], BF16) for i in range(NWO)]; wo_tl = [Tl("wo") for i in range(NWO)]
        xo = [P.sb(es, "xo%d" % i, [128, TT]) for i in range(2)]; xo_tl = [Tl("xo") for i in range(2)]
        ps = [P.ps(es, "ps%d" % i) for i in range(6)]; ps_tl = [Tl("ps") for i in range(6)]
        xview = xres.rearrange("(kc p) t -> p kc t", p=128)
        yview = yrec_d.rearrange("(kc p) t -> p kc t", p=128)
        pi = 0
        for tt in range(NT):
            t0 = tt * TT
            for kc in range(KC):
                S.dma("sp", lambda e, kc=kc, t0=t0: e.dma_start(out=xt[:, kc, :], in_=xview[:, kc, t0:t0 + TT]), writes=(xt_tl[kc],))
            mix, mix_tl = mix2[tt % 2], mix2_tl[tt % 2]
            if tt == 0:
                S.dma("sp", lambda e, t0=t0, mix=mix: e.dma_start(out=mix[:, :, :], in_=yview[:, :, t0:t0 + TT]), writes=(mix_tl,))
            if tt + 1 < NT:
                S.dma("sp", lambda e, t1=t0 + TT, m2=mix2[(tt + 1) % 2]: e.dma_start(out=m2[:, :, :], in_=yview[:, :, t1:t1 + TT]),
                      writes=(mix2_tl[(tt + 1) % 2],))
            for oc in range(KC):
                b = oc % NWO
                xb_ = oc % 2
                S.dma("pool", lambda e, b=b, oc=oc: e.dma_start(out=wo[b][:, :, :], in_=w_out[oc]), reads=(w_out_tl[oc],), writes=(wo_tl[b],))
                p_i = pi % 6
                pi += 1

                def mm2(e, b=b, p_i=p_i, mix=mix):
                    ins = None
                    for kc in range(KC):
                        ins = e.matmul(ps[p_i][:, :], wo[b][:, kc, :], mix[:, kc, :], start=(kc == 0), stop=(kc == KC - 1))
                    return ins
                S.op("pe", mm2, reads=(wo_tl[b], mix_tl), writes=(ps_tl[p_i],))
                S.op("dve", lambda e, xb_=xb_, p_i=p_i, oc=oc: e.tensor_tensor(out=xo[xb_][:, :], in0=ps[p_i][:, :], in1=xt[:, oc, :], op=ALU.add),
                     reads=(ps_tl[p_i], xt_tl[oc]), writes=(xo_tl[xb_],))
                S.dma("sp", lambda e, xb_=xb_, oc=oc, t0=t0: e.dma_start(out=xres[oc * 128:(oc + 1) * 128, t0:t0 + TT], in_=xo[xb_][:, :]),
                      reads=(xo_tl[xb_],), writes=(junk,), defer=True)
        S.barrier()
        S.flush()


def all_host_weights(inp):
    w = host_weights(inp)
    w.update(s5_host(inp))
    w.update(attn_host(inp))
    w.update(l0e_host(inp))
    w.update(ffn_host(inp))
    w.update(l1_host(inp))
    return w


def build(L, upto="all", debug=(), TF=512):
    P = Prog(L, debug)
    x_in = P.din("xT", [D, L])
    xres = P.dout("yT", [D, L])
    C = load_consts(P)
    stages = ["l0a", "s5", "attn", "l0e", "ffn0", "all"]
    U_d, q_d, v_d = phase_l0a(P, C, x_in)
    if upto != "l0a":
        Y_d = phase_s5(P, C, U_d)
    if upto not in ("l0a", "s5"):
        at_d = phase_attn(P, C, q_d, v_d)
    if upto not in ("l0a", "s5", "attn"):
        phase_l0e(P, C, Y_d, at_d, x_in, xres)
    if upto not in ("l0a", "s5", "attn", "l0e"):
        phase_ffn(P, C, 0, xres, TF)
    if upto not in ("l0a", "s5", "attn", "l0e", "ffn0"):
        gate_d, xb_d = phase_l1a(P, C, xres)
    if upto not in ("l0a", "s5", "attn", "l0e", "ffn0", "l1a"):
        yrec_d = phase_l1b(P, C, gate_d, xb_d)
    if upto not in ("l0a", "s5", "attn", "l0e", "ffn0", "l1a", "l1b"):
        phase_l1c(P, C, yrec_d, xres)
    if upto not in ("l0a", "s5", "attn", "l0e", "ffn0", "l1a", "l1b", "l1c"):
        phase_ffn(P, C, 1, xres, TF)
    P.es.close()
    return P


SEQ_LEN = 4096
N_CORES = 8
TF_FFN = 1024


def kernel(**inputs):
    inp = {k: np.asarray(v) for k, v in inputs.items()}
    xp = inp["x_prompt"]
    xs = inp["x_sample"]
    seqs = [xp[i] for i in range(xp.shape[0])] + [xs[i] for i in range(xs.shape[0])]
    nseq = len(seqs)
    L = seqs[0].shape[0]
    P = build(L, "all", TF=TF_FFN)
    shared = {}
    shared.update(host_consts(L))
    shared.update(all_host_weights(inp))
    names = set(P.dram.keys())
    shared = {k: v for k, v in shared.items() if k in names}
    in_maps = []
    for c in range(N_CORES):
        m = dict(shared)
        m["xT"] = np.ascontiguousarray(seqs[c % nseq].T)
        in_maps.append(m)
    res = run_bass_kernel_spmd(P.nc, in_maps, core_ids=list(range(N_CORES)))
    outs = [np.ascontiguousarray(np.asarray(res.results[c]["yT"]).T) for c in range(nseq)]
    y_prompt = np.stack(outs[:xp.shape[0]], axis=0).astype(np.float32)
    y_sample = np.stack(outs[xp.shape[0]:], axis=0).astype(np.float32)
    return (y_prompt, y_sample)
```
